# Optimizing a Trainium2 kernel written in Bass

```python
import math
import jax, jax.numpy as jnp
from jax import lax
import numpy as np

D_MODEL = 1024
BATCH = 32
SEQ = 2048
DEPTH = 2

HEAD_DIM = 128
HEADS_PER_GROUP = 4
ATTN_PATTERNS = ((128, 1), (512, 4), (2048, 16))
N_ATTN_GROUPS = len(ATTN_PATTERNS)
ATTN_WIDTH = N_ATTN_GROUPS * HEADS_PER_GROUP * HEAD_DIM
ATTN_OUT_WIDTH = HEADS_PER_GROUP * HEAD_DIM
ROPE_DIM = HEAD_DIM // 4
ROPE_THETA = 500000.0
BLOCK = 128
NEG_INF = -1e30
POOL_WINDOWS = (2, 4, 8, 16)
POOL_GROUP_WIDTH = D_MODEL // 4
POOL_WIDTH = len(POOL_WINDOWS) * POOL_GROUP_WIDTH
IN_WIDTH = 3 * ATTN_WIDTH + POOL_WIDTH + 2 * D_MODEL
D_FF = 2816
CONV_WIDTH = 3
PLE_DIM = 256
RMS_EPS = 1e-6

kernel_name = 'hybrid_dilated_attn_pool_gated_merge'


def rmsnorm(x, g):
    x32 = x.astype(jnp.float32)
    y = x32 * lax.rsqrt(jnp.mean(x32 * x32, axis=-1, keepdims=True) + RMS_EPS)
    return (y * g.astype(jnp.float32)).astype(x.dtype)


def partial_rotary(t, cos, sin):
    half = ROPE_DIM // 2
    t1 = t[..., :half].astype(jnp.float32)
    t2 = t[..., half:ROPE_DIM].astype(jnp.float32)
    c = cos[:, None, None, :]
    s = sin[:, None, None, :]
    rot = jnp.concatenate([t1 * c - t2 * s, t2 * c + t1 * s], axis=-1).astype(t.dtype)
    return jnp.concatenate([rot, t[..., ROPE_DIM:]], axis=-1)


def dilated_window_attention(q, k, v, window, dilation):
    B, S, H, hd = q.shape
    span = BLOCK * dilation
    s_pad = -(-S // span) * span
    L = s_pad // dilation
    nb = L // BLOCK
    w_sub = window // dilation

    def to_blocks(t):
        t = jnp.pad(t, ((0, 0), (0, s_pad - S), (0, 0), (0, 0)))
        t = t.reshape(B, L, dilation, H, hd).transpose(0, 2, 1, 3, 4)
        return t.reshape(B, dilation, nb, BLOCK, H, hd)

    def with_prev(t):
        prev = jnp.pad(t, ((0, 0), (0, 0), (1, 0), (0, 0), (0, 0), (0, 0)))[:, :, :-1]
        return jnp.concatenate([prev, t], axis=3)

    qb = to_blocks(q)
    kk = with_prev(to_blocks(k))
    vv = with_prev(to_blocks(v))
    scores = jnp.einsum('brnqhd,brnkhd->brnhqk', qb, kk,
                        preferred_element_type=jnp.float32) * (hd ** -0.5)
    qi = jnp.arange(BLOCK)[:, None]
    ki = jnp.arange(2 * BLOCK)[None, :]
    diff = BLOCK + qi - ki
    band = (diff >= 0) & (diff <= w_sub)
    blk = jnp.arange(nb)[:, None, None]
    mask = band[None] & ((blk > 0) | (ki[None] >= BLOCK))
    scores = jnp.where(mask[None, None, :, None], scores, NEG_INF)
    lse = jax.nn.logsumexp(scores, axis=-1)
    probs = jnp.exp(scores - lse[..., None])
    out = jnp.einsum('brnhqk,brnkhd->brnqhd', probs.astype(v.dtype), vv,
                     preferred_element_type=jnp.float32)
    out = out.reshape(B, dilation, L, H, hd).transpose(0, 2, 1, 3, 4)
    out = out.reshape(B, s_pad, H, hd)[:, :S]
    lse = lse.transpose(0, 1, 2, 4, 3).reshape(B, dilation, L, H).transpose(0, 2, 1, 3)
    lse = lse.reshape(B, s_pad, H)[:, :S]
    return out, lse


def multiscale_pool_mixer(u, pool_w, pool_scale):
    B, S, _ = u.shape
    u32 = u.astype(jnp.float32)
    csum = jnp.cumsum(u32, axis=1)
    t = jnp.arange(S)
    groups = []
    for g, w in enumerate(POOL_WINDOWS):
        sl = slice(g * POOL_GROUP_WIDTH, (g + 1) * POOL_GROUP_WIDTH)
        cg = csum[..., sl]
        shifted = jnp.pad(cg, ((0, 0), (w, 0), (0, 0)))[:, :S]
        count = jnp.minimum(t + 1, w).astype(jnp.float32)
        groups.append((cg - shifted) / count[None, :, None] - u32[..., sl])
    pooled = jnp.stack(groups, axis=2).astype(u.dtype)
    mixed = jnp.einsum('bsgc,gcd->bsgd', pooled, pool_w).reshape(B, S, POOL_WIDTH)
    return mixed * pool_scale


def conv_gated_mlp(h, w_up, conv_w, conv_b, w_down):
    S = h.shape[1]
    u = h @ w_up
    y = conv_b
    for tap in range(CONV_WIDTH):
        shift = CONV_WIDTH - 1 - tap
        y = y + conv_w[tap] * jnp.pad(u, ((0, 0), (shift, 0), (0, 0)))[:, :S]
    gate, val = jnp.split(y, 2, axis=-1)
    return (jax.nn.silu(gate) * val) @ w_down


def setup_inputs(seed: int = 0) -> dict:
    key = jax.random.key(seed)
    ks = jax.random.split(key, 20)
    f32 = jnp.float32

    def nrm(k, shape, fan_in):
        return jax.random.normal(k, shape, f32) * (fan_in ** -0.5)

    def gain(k, shape):
        return 1.0 + 0.02 * jax.random.normal(k, shape, f32)

    return {
        'x': jax.random.normal(ks[0], (BATCH, SEQ, D_MODEL), f32),
        'p': jax.random.normal(ks[1], (DEPTH, BATCH, SEQ, PLE_DIM), f32),
        'g_mix': gain(ks[2], (DEPTH, D_MODEL)),
        'w_in': nrm(ks[3], (DEPTH, D_MODEL, IN_WIDTH), D_MODEL),
        'w_ya': nrm(ks[4], (DEPTH, ATTN_OUT_WIDTH, D_MODEL), ATTN_OUT_WIDTH),
        'w_yb': nrm(ks[5], (DEPTH, POOL_WIDTH, D_MODEL), POOL_WIDTH),
        'pool_w': nrm(ks[6], (DEPTH, len(POOL_WINDOWS), POOL_GROUP_WIDTH, POOL_GROUP_WIDTH), POOL_GROUP_WIDTH),
        'pool_scale': gain(ks[7], (DEPTH, POOL_WIDTH)),
        'w_o': nrm(ks[8], (DEPTH, D_MODEL, D_MODEL), D_MODEL),
        'g_ffn': gain(ks[9], (DEPTH, D_MODEL)),
        'w_up': nrm(ks[10], (DEPTH, D_MODEL, 2 * D_FF), D_MODEL),
        'conv_w': nrm(ks[11], (DEPTH, CONV_WIDTH, 2 * D_FF), CONV_WIDTH),
        'conv_b': 0.01 * jax.random.normal(ks[12], (DEPTH, 2 * D_FF), f32),
        'w_down': nrm(ks[13], (DEPTH, D_FF, D_MODEL), D_FF),
        'g_ple': gain(ks[14], (DEPTH, D_MODEL)),
        'w_ple': nrm(ks[15], (DEPTH, PLE_DIM, D_MODEL), PLE_DIM),
        'w_ple_gate': nrm(ks[16], (DEPTH, D_MODEL, D_MODEL), D_MODEL),
        'g_final': gain(ks[17], (D_MODEL,)),
    }


def reference(x, p, g_mix, w_in, w_ya, w_yb, pool_w, pool_scale, w_o, g_ffn,
              w_up, conv_w, conv_b, w_down, g_ple, w_ple, w_ple_gate, g_final):
    B, S, _ = x.shape
    pos = jnp.arange(S, dtype=jnp.float32)
    inv_freq = jnp.exp(jnp.arange(0, ROPE_DIM, 2, dtype=jnp.float32)
                       * (-math.log(ROPE_THETA) / ROPE_DIM))
    ang = pos[:, None] * inv_freq[None, :]
    cos, sin = jnp.cos(ang), jnp.sin(ang)
    split_at = [ATTN_WIDTH, 2 * ATTN_WIDTH, 3 * ATTN_WIDTH,
                3 * ATTN_WIDTH + POOL_WIDTH, 3 * ATTN_WIDTH + POOL_WIDTH + D_MODEL]
    head_shape = (B, S, N_ATTN_GROUPS, HEADS_PER_GROUP, HEAD_DIM)

    for i in range(DEPTH):
        h = rmsnorm(x, g_mix[i])
        z = h @ w_in[i]
        q, k, v, u_pool, gate_a, gate_b = jnp.split(z, split_at, axis=-1)
        q = partial_rotary(q.reshape(head_shape), cos, sin)
        k = partial_rotary(k.reshape(head_shape), cos, sin)
        v = v.reshape(head_shape)

        outs, lses = [], []
        for g, (window, dilation) in enumerate(ATTN_PATTERNS):
            o_g, lse_g = dilated_window_attention(q[:, :, g], k[:, :, g], v[:, :, g],
                                                  window, dilation)
            outs.append(o_g)
            lses.append(lse_g)
        weights = jax.nn.softmax(jnp.stack(lses, axis=0), axis=0)
        attn = jnp.sum(weights[..., None] * jnp.stack(outs, axis=0), axis=0)
        y_a = attn.reshape(B, S, ATTN_OUT_WIDTH).astype(x.dtype) @ w_ya[i]

        y_b = multiscale_pool_mixer(u_pool, pool_w[i], pool_scale[i]) @ w_yb[i]

        merged = jax.nn.sigmoid(gate_a) * y_a + jax.nn.sigmoid(gate_b) * y_b
        x = x + merged @ w_o[i]

        x = x + conv_gated_mlp(rmsnorm(x, g_ffn[i]), w_up[i], conv_w[i], conv_b[i], w_down[i])

        ple_gate = jax.nn.sigmoid(rmsnorm(x, g_ple[i]) @ w_ple_gate[i])
        x = x + (p[i] @ w_ple[i]) * ple_gate

    return rmsnorm(x, g_final)
```

```python
import contextlib
import math
import numpy as np
import concourse.bass as bass
import concourse.mybir as mybir
from concourse.bass_utils import run_bass_kernel_spmd

F32 = mybir.dt.float32
BF16 = mybir.dt.bfloat16
AF = mybir.ActivationFunctionType
ALU = mybir.AluOpType
AX = mybir.AxisListType

NCORES = 8
S = 2048
D = 1024
DEPTH = 2
NSLOT = 4
GROUPS = ((128, 1), (512, 4), (2048, 16))
POOLW = (2, 4, 8, 16)
GORDER = (1, 2, 0)
NV = 8 + 8 + 8 + 8 + 44 * 3 + 44
V_GMIX, V_GFFN, V_GPLE, V_PSC, V_CW, V_CB = 0, 8, 16, 24, 32, 32 + 132
C_ID, C_SEL, C_INVC, C_MASK, C_ODIV, C_ONE = 0, 128, 640, 704, 960, 1088
NCST = 1216


class _Op:
    __slots__ = ("eng", "fn", "stream", "sidx", "waits", "clock_after", "flagged", "is_dma")


class Prog:
    ENGS = ("pe", "act", "dve", "pool", "sp")

    def __init__(self, nc):
        self.nc = nc
        self.ops = {e: [] for e in self.ENGS}
        self.eng_clock = {e: {} for e in self.ENGS}
        self.stream_ops = {e: [] for e in self.ENGS}
        self.res = {}
        self.epoch = None

    def op(self, eng, fn, reads=(), writes=(), dma=None, loose=()):
        o = _Op()
        o.eng = eng
        o.fn = fn
        o.is_dma = dma is not None
        o.stream = dma if dma is not None else eng
        sl = self.stream_ops.setdefault(o.stream, [])
        o.sidx = len(sl) + 1
        o.flagged = o.is_dma
        deps = {}

        def add(tok, kind):
            if tok is None:
                return
            s, i = tok
            if s == eng and not o.is_dma:
                if eng == "pe" or kind.endswith("_L"):
                    return
            if deps.get(s, 0) < i:
                deps[s] = i

        for k in reads:
            r = self.res.get(k)
            if r is not None:
                add(r["w"], "RAW")
        for k in writes:
            r = self.res.get(k)
            if r is not None:
                sfx = "_L" if k in loose else ""
                add(r["w"], "WAW" + sfx)
                for s, i in r["r"].items():
                    add((s, i), "WAR" + sfx)
        if o.is_dma and o.sidx > 1:
            add((o.stream, o.sidx - 1), "RAW")
        if self.epoch is not None:
            add(self.epoch, "RAW")
        clk = self.eng_clock[eng]
        waits = []
        for s, i in deps.items():
            if clk.get(s, 0) >= i:
                continue
            waits.append((s, i))
            dop = self.stream_ops[s][i - 1]
            dop.flagged = True
            for s2, i2 in dop.clock_after.items():
                if clk.get(s2, 0) < i2:
                    clk[s2] = i2
        o.waits = waits
        ca = dict(clk)
        ca[o.stream] = o.sidx
        o.clock_after = ca
        sl.append(o)
        self.ops[eng].append(o)
        tok = (o.stream, o.sidx)
        for k in reads:
            r = self.res.setdefault(k, {"w": None, "r": {}})
            if r["r"].get(o.stream, 0) < o.sidx:
                r["r"][o.stream] = o.sidx
        for k in writes:
            self.res[k] = {"w": tok, "r": {}}
        return o

    def barrier(self, fn):
        keys = list(self.res.keys())
        o = self.op("dve", fn, reads=keys, writes=keys)
        self.res = {}
        self.epoch = (o.stream, o.sidx)

    def emit(self):
        nc = self.nc
        streams = [s for s in self.stream_ops if self.stream_ops[s]]
        val = {}
        for s in streams:
            c = 0
            vs = []
            for o in self.stream_ops[s]:
                if o.is_dma:
                    c += 16
                elif o.flagged:
                    c += 1
                vs.append(c)
            val[s] = vs
        with contextlib.ExitStack() as st:
            sems = {s: st.enter_context(nc.semaphore("s_" + s)) for s in streams}
            block = st.enter_context(nc.Block())
            engobj = {"pe": block.tensor, "act": block.scalar, "dve": block.vector,
                      "pool": block.gpsimd, "sp": block.sync}

            def make(e):
                def body(eng):
                    for o in self.ops[e]:
                        for (s, i) in o.waits:
                            eng.wait_ge(sems[s], val[s][i - 1])
                        if o.fn is None:
                            continue
                        ins = o.fn(eng)
                        if o.is_dma:
                            ins.then_inc(sems[o.stream], 16)
                        elif o.flagged:
                            ins.then_inc(sems[o.stream], 1)
                return body

            for e in self.ENGS:
                if self.ops[e]:
                    engobj[e](make(e))


def _head_perm(hh):
    rest = list(range(32, 128))
    out = []
    for qd in range(4):
        if qd == hh:
            out += list(range(32))
        else:
            out += rest[:32]
            rest = rest[32:]
    return out


def layer_tiles():
    t = []
    for g in range(3):
        t += [(("R", g), 2048)]
        for hh in range(4):
            t += [(("HQK", g, hh), 2048), (("HV", g, hh), 1024)]
    t.append((("PW",), 2048))
    t += [(("PU", j), 2048) for j in range(4)]
    for j in range(4):
        t += [(("YB", j), 2048), (("GB", j), 2048)]
    t += [(("YA", j), 2048) for j in range(2)]
    t += [(("GA", j), 2048) for j in range(4)]
    t += [(("WO", j), 2048) for j in range(4)]
    t += [(("UP", j), 2048) for j in range(22)]
    t += [(("DN", j), 2048) for j in range(11)]
    t.append((("PLE",), 2048))
    t += [(("PG", j), 2048) for j in range(4)]
    return t


def tile_offsets():
    off = {}
    o = 0
    for name, n in layer_tiles():
        off[name] = (o, n)
        o += n
    return off, o


FFN_PARTS = (range(0, 8), range(8, 16), range(16, 22))
POOL_ORDER = (3, 2, 1, 0)


def consume_order():
    seq = []
    for g in GORDER:
        seq += [("R", g)]
        for hh in range(4):
            seq += [("HQK", g, hh), ("HV", g, hh)]
    for hf in range(2):
        seq += [("PW",)] + [("PU", j) for j in POOL_ORDER]
        for j in range(4):
            seq += [("YB", j), ("GB", j)]
        seq += [("YA", 0), ("GA", 0), ("GA", 1), ("YA", 1), ("GA", 2), ("GA", 3)]
        seq += [("WO", j) for j in range(4)]
    for pairs in FFN_PARTS:
        seq += [("UP", pr) for pr in pairs]
        seq += [("DN", j) for j in range(pairs[0] // 2, (pairs[-1] + 1) // 2)]
    seq += [("PLE",)] + [("PG", j) for j in range(4)]
    return seq


def _k1024(W, cols):
    sub = W[:, cols]
    n = sub.shape[1]
    return sub.reshape(8, 128, n).transpose(1, 0, 2).reshape(128, 8 * n)


def build_weight_stream(w_in, w_ya, w_yb, pool_w, w_o, w_up, w_down, w_ple, w_ple_gate):
    off, tot = tile_offsets()
    out = np.zeros((DEPTH, 128, tot), np.float32)
    ar = np.arange
    for l in range(DEPTH):
        def put(name, arr):
            o, n = off[name]
            assert arr.shape == (128, n), (name, arr.shape, n)
            out[l, :, o:o + n] = arr
        for g in range(3):
            r = []
            for base0 in (0, 1536):
                for hh in range(4):
                    b = base0 + g * 512 + hh * 128
                    r += [b + i for i in range(32)]
            put(("R", g), _k1024(w_in[l], r))
            for hh in range(4):
                pm = _head_perm(hh)
                bq = g * 512 + hh * 128
                put(("HQK", g, hh), _k1024(w_in[l], [bq + m for m in pm] + [1536 + bq + m for m in pm]))
                put(("HV", g, hh), _k1024(w_in[l], [3072 + bq + i for i in range(128)]))
        for j in range(4):
            put(("PU", j), _k1024(w_in[l], list(4608 + j * 256 + ar(256))))
            put(("GA", j), _k1024(w_in[l], list(5632 + j * 256 + ar(256))))
            put(("GB", j), _k1024(w_in[l], list(6656 + j * 256 + ar(256))))
            put(("YB", j), _k1024(w_yb[l], list(j * 256 + ar(256))))
            put(("WO", j), _k1024(w_o[l], list(j * 256 + ar(256))))
            put(("PG", j), _k1024(w_ple_gate[l], list(j * 256 + ar(256))))
        put(("PW",), pool_w[l].reshape(4, 2, 128, 256).transpose(2, 0, 1, 3).reshape(128, 2048))
        for j in range(2):
            put(("YA", j), w_ya[l][:, j * 512:(j + 1) * 512].reshape(4, 128, 512).transpose(1, 0, 2).reshape(128, 2048))
        put(("PLE",), w_ple[l].reshape(2, 128, 1024).transpose(1, 0, 2).reshape(128, 2048))
        for pr in range(22):
            put(("UP", pr), _k1024(w_up[l], list(pr * 128 + ar(128)) + list(2816 + pr * 128 + ar(128))))
        for j in range(11):
            blk = w_down[l][j * 256:(j + 1) * 256]
            put(("DN", j), blk.reshape(2, 128, 1024).transpose(1, 0, 2).reshape(128, 2048))
    return out


def build_consts():
    cst = np.zeros((128, NCST), np.float32)
    cst[:, C_ID:C_ID + 128] = np.eye(128, dtype=np.float32)
    for hh in range(4):
        cst[32 * hh, C_SEL + hh * 128:C_SEL + (hh + 1) * 128] = 1.0
    for g, w in enumerate(POOLW):
        t = np.arange(16)
        cst[:, C_INVC + g * 16:C_INVC + (g + 1) * 16] = (1.0 / np.minimum(t + 1, w)).astype(np.float32)[None, :]
    p = np.arange(128)[:, None]
    j = np.arange(128)[None, :]
    cst[:, C_MASK:C_MASK + 128] = (p <= j)
    cst[:, C_MASK + 128:C_MASK + 256] = (p >= j)
    cst[:, C_ODIV:C_ODIV + 128] = 1.0 / 1024.0
    cst[:, C_ONE:C_ONE + 128] = 1.0
    pos = np.arange(S, dtype=np.float32)
    inv_freq = np.exp(np.arange(0, 32, 2, dtype=np.float32) * np.float32(-math.log(500000.0) / 32)).astype(np.float32)
    ang = (pos[:, None] * inv_freq[None, :]).astype(np.float32)
    cos, sin = np.cos(ang).T.astype(np.float32), np.sin(ang).T.astype(np.float32)
    c32 = np.concatenate([cos, cos], 0)
    s32 = np.concatenate([sin, -sin], 0)
    rope = np.zeros((128, 2 * S), np.float32)
    rope[:, :S] = np.tile(c32, (4, 1))
    rope[:, S:] = np.tile(s32, (4, 1))
    return cst, rope


def build_vecs(g_mix, g_ffn, g_ple, pool_scale, conv_w, conv_b):
    v = np.zeros((128, DEPTH * NV), np.float32)
    fm = lambda a: a.reshape(-1, 128).T
    for l in range(DEPTH):
        b = l * NV
        v[:, b + V_GMIX:b + V_GMIX + 8] = fm(g_mix[l])
        v[:, b + V_GFFN:b + V_GFFN + 8] = fm(g_ffn[l])
        v[:, b + V_GPLE:b + V_GPLE + 8] = fm(g_ple[l])
        v[:, b + V_PSC:b + V_PSC + 8] = fm(pool_scale[l])
        cw = conv_w[l].reshape(3, 44, 128).transpose(2, 1, 0).reshape(128, 132)
        v[:, b + V_CW:b + V_CW + 132] = cw
        v[:, b + V_CB:b + V_CB + 44] = fm(conv_b[l])
    return v


A_RSTD, A_ROPE, A_ACCD, A_ACCN, A_X, A_Y = 0, 8192, 16384, 24576, 57344, 77824
ARENA = 88064


def build(nseq=4, depth=DEPTH):
    nc = bass.Bass("TRN2", target_bir_lowering=False)
    toff, TOT = tile_offsets()
    xin = nc.dram_tensor("xin", [nseq, 16, 128, 1024], F32, kind="ExternalInput").ap()
    pin = nc.dram_tensor("pin", [DEPTH, nseq, 16, 128, 256], F32, kind="ExternalInput").ap()
    wt = nc.dram_tensor("wt", [DEPTH, 128, TOT], F32, kind="ExternalInput").ap()
    vecd = nc.dram_tensor("vec", [128, DEPTH * NV], F32, kind="ExternalInput").ap()
    cstd = nc.dram_tensor("cst", [128, NCST], F32, kind="ExternalInput").ap()
    roped = nc.dram_tensor("rope", [128, 2 * S], F32, kind="ExternalInput").ap()
    gfind = nc.dram_tensor("gfin", [128, 1024], F32, kind="ExternalInput").ap()
    outd = nc.dram_tensor("out", [nseq, 16, 128, 1024], F32, kind="ExternalOutput").ap()

    with contextlib.ExitStack() as st:
        sb = lambda name, shape, dt: st.enter_context(nc.sbuf_tensor(name, shape, dt))
        xT = sb("xT", [128, 8, S], F32)
        hT = sb("hT", [128, 8, S], BF16)
        ring = [sb(f"ring{i}", [128, 2048], BF16) for i in range(NSLOT)]
        vecs = sb("vecs", [128, DEPTH * NV], F32)
        cstf = sb("cstf", [128, NCST], F32)
        cstb = sb("cstb", [128, 640], BF16)
        epsT = sb("epsT", [128, 2], F32)
        arena = sb("arena", [128, ARENA // 2], BF16)
        psT = [st.enter_context(nc.psum_tensor(f"psT{i}", [128, 2048], F32)) for i in range(2)]

        def ab(off, n):
            return arena[:, off // 2: off // 2 + n]

        def af(off, n):
            return arena[:, off // 2: off // 2 + 2 * n].bitcast(F32)

        ident32 = cstf[:, C_ID:C_ID + 128]
        sel = cstf[:, C_SEL:C_SEL + 512].rearrange("p (h m) -> p h m", h=4)
        invc = cstf[:, C_INVC:C_INVC + 64].rearrange("p (g t) -> p g t", g=4)
        identb = cstb[:, 0:128]
        mask2 = cstb[:, 128:384]
        odivb = cstb[:, 384:512]
        onesb = cstb[:, 512:640]

        P = Prog(nc)
        bank_ctr = [0]

        def nb():
            b = bank_ctr[0] % 8
            bank_ctr[0] += 1
            return b

        def bank(b):
            return psT[b // 4][:, (b % 4) * 512:(b % 4 + 1) * 512]

        def bk(b):
            return ("bk", b)

        def barrier():
            P.barrier(lambda e: e.memset(epsT[:, 1:2], 0.0))

        P.op("sp", lambda e: e.dma_start(out=vecs[:], in_=vecd), writes=["vecs"], dma="d_c0")
        P.op("sp", lambda e: e.dma_start(out=cstf[:], in_=cstd), writes=["cstf"], dma="d_c1")
        P.op("dve", lambda e: e.memset(epsT[:, 0:1], 1e-6), writes=["eps"])
        P.op("dve", lambda e: e.tensor_copy(cstb[:, 0:128], cstf[:, C_ID:C_ID + 128]), reads=["cstf"], writes=["cstb"])
        P.op("dve", lambda e: e.tensor_copy(cstb[:, 128:640], cstf[:, C_MASK:C_MASK + 512]), reads=["cstf"], writes=["cstb"])

        corder = consume_order()
        gseq = [(l, nm) for _ in range(nseq) for l in range(depth) for nm in corder]
        wst = {"next": 0, "free": list(range(NSLOT)), "slot": {}, "pos": 0}

        def w_issue():
            while wst["free"] and wst["next"] < len(gseq):
                idx = wst["next"]
                l, nm = gseq[idx]
                slot = wst["free"].pop(0)
                o, n = toff[nm]
                P.op("pool", (lambda e, slot=slot, l=l, o=o, n=n: e.dma_start(out=ring[slot][:, 0:n], in_=wt[l, :, o:o + n])),
                     writes=[("w", slot)], dma=f"d_w{slot}")
                wst["slot"][idx] = slot
                wst["next"] += 1

        def w_get(l, nm):
            idx = wst["pos"]
            assert gseq[idx] == (l, nm), (gseq[idx], l, nm)
            w_issue()
            assert idx in wst["slot"], ("weight ring too small at", nm)
            wst["pos"] += 1
            slot = wst["slot"][idx]
            return slot

        def w_free(slot):
            wst["free"].append(slot)
            w_issue()

        def wv(slot, k, c):
            return ring[slot][:, 0:k * c].rearrange("p (k c) -> p k c", k=k)

        def wk(slot):
            return ("w", slot)

        def proj(slot, view, chunk, tt, extra_reads=()):
            b = nb()
            for k in range(8):
                P.op("pe", (lambda e, b=b, k=k: e.matmul(bank(b), view[:, k, chunk * 128:(chunk + 1) * 128],
                                                         hT[:, k, tt * 512:(tt + 1) * 512], start=(k == 0), stop=(k == 7))),
                     reads=[wk(slot), ("hT", tt)], writes=[bk(b)])
            return b

        rstd = af(A_RSTD, 2048)

        sqh = [ab(A_Y, 2048).rearrange("p (k t) -> p k t", k=8), ab(A_Y + 4096, 2048).rearrange("p (k t) -> p k t", k=8)]
        RSTD_KEYS = [("rstd", j) for j in range(8)]

        def norm_stats(js=range(8), xkey=None):
            for j in js:
                sqb = sqh[j % 2]
                xs = xT[:, :, j * 256:(j + 1) * 256]
                xk = ["xT"] if xkey is None else [xkey]
                if j % 2 == 0:
                    P.op("act", (lambda e, sqb=sqb, xs=xs: e.activation(sqb, xs, AF.Square)), reads=xk, writes=[("sq", j % 2)])
                else:
                    P.op("dve", (lambda e, sqb=sqb, xs=xs: e.tensor_tensor(sqb, xs, xs, ALU.mult)), reads=xk, writes=[("sq", j % 2)])
                b = nb()
                for k in range(8):
                    P.op("pe", (lambda e, b=b, k=k, sqb=sqb: e.matmul(bank(b)[:, 0:256], odivb, sqb[:, k, :], start=(k == 0), stop=(k == 7))),
                         reads=[("sq", j % 2), "cstb"], writes=[bk(b)])
                rs_ = rstd[:, j * 256:(j + 1) * 256]
                P.op("act", (lambda e, b=b, rs_=rs_: e.activation(rs_, bank(b)[:, 0:256], AF.Ln, bias=epsT[:, 0:1], scale=1.0)),
                     reads=[bk(b), "eps"], writes=[("rstd", j)])
                P.op("act", (lambda e, rs_=rs_: e.activation(rs_, rs_, AF.Exp, scale=-0.5)), reads=[("rstd", j)], writes=[("rstd", j)])

        def pview(ap2, d):
            return ap2 if d == 1 else ap2.rearrange("p (j r) -> p r j", r=d)

        def cview(ap2, d):
            return ap2 if d == 1 else ap2.rearrange("p (r j) -> p r j", r=d)

        ALL_UNITS = [(tt, k) for tt in range(4) for k in range(8)]

        def normalize(gv, d, units=None):
            for tt, k in (ALL_UNITS if units is None else units):
                rk = RSTD_KEYS if d != 1 else [("rstd", 2 * tt), ("rstd", 2 * tt + 1)]
                P.op("dve", (lambda e, k=k, tt=tt: e.scalar_tensor_tensor(span_c(hT[:, k, tt * 512:(tt + 1) * 512], d), span_p(xT[:, k, :], d, tt),
                                                                         gv[:, k:k + 1], span_p(rstd, d, tt), ALU.mult, ALU.mult)),
                     reads=["xT", "vecs"] + rk, writes=[("hT", tt)], loose=[("hT", tt)])

        def span_p(ap2, d, m):
            if d == 1:
                return ap2[:, m * 512:(m + 1) * 512]
            if d == 4:
                return ap2.rearrange("p (j r) -> p r j", r=4)[:, m, :]
            return ap2.rearrange("p (j r) -> p r j", r=16)[:, 4 * m:4 * m + 4, :]

        def span_c(ap2, d):
            return ap2.rearrange("p (a b) -> p a b", a=4) if d == 16 else ap2

        def phase_attn(l):
            gv = vecs[:, l * NV + V_GMIX: l * NV + V_GMIX + 8]
            ropeC = ab(A_ROPE, 2048)
            ropeS = ab(A_ROPE + 4096, 2048)
            accD = af(A_ACCD, 2048)
            accN = af(A_ACCN, 8192).rearrange("p (h t) -> p h t", h=4)
            qk = [ab(A_X, 2048), ab(A_X + 4096, 2048)]
            Vh = ab(A_X + 8192, 2048).rearrange("p (b d) -> p b d", b=16)
            qkrot = [ab(A_X + 12288, 2048), ab(A_X + 16384, 2048)]
            attnT = ab(A_X, 8192).rearrange("p (h t) -> p h t", h=4)
            NPT = 12
            PT = [ab(A_Y + i * 512, 256) for i in range(NPT)]
            t1 = af(A_Y + 6144, 512)
            t2 = af(A_Y + 8192, 512)
            scale = 1.0 / math.sqrt(128.0)
            t3 = af(A_X + 8192, 512)
            SWAP16 = list(range(16, 32)) + list(range(0, 16))

            barrier()
            P.op("pool", lambda e: e.dma_start(out=ab(A_ROPE, 4096), in_=roped), writes=["rope"], dma="d_rope")
            norm_stats()
            barrier()
            pre_norm = [False]
            for gi, g in enumerate(GORDER):
                d = GROUPS[g][1]
                nbk = (S // d) // 128
                if not pre_norm[0]:
                    normalize(gv, d)
                pre_norm[0] = False
                rs = w_get(l, ("R", g))
                Rv = wv(rs, 8, 256)
                for which in range(2):
                    for tt in range(4):
                        b1 = proj(rs, Rv, which, tt)
                        P.op("dve", (lambda e, b1=b1, tt=tt, d=d: e.tensor_tensor(span_c(t1, d), span_c(bank(b1), d), span_p(ropeC, d, tt), ALU.mult)),
                             reads=[bk(b1), "rope"], writes=["t1"])
                        P.op("dve", (lambda e, b1=b1, tt=tt, d=d: e.tensor_tensor(span_c(t2, d), span_c(bank(b1), d), span_p(ropeS, d, tt), ALU.mult)),
                             reads=[bk(b1), "rope"], writes=["t2"])
                        P.op("dve", (lambda e: e.stream_shuffle(t3, t2, SWAP16)), reads=["t2"], writes=["Vh"])
                        P.op("dve", (lambda e, which=which, tt=tt: e.tensor_tensor(qkrot[which][:, tt * 512:(tt + 1) * 512], t1, t3, ALU.add)),
                             reads=["t1", "Vh"], writes=[("qkrot", which)], loose=[("qkrot", which)])
                w_free(rs)
                for hh in range(4):
                    hs = w_get(l, ("HQK", g, hh))
                    Hv = wv(hs, 8, 256)
                    for which in range(2):
                        for tt in range(4):
                            b = proj(hs, Hv, which, tt)
                            P.op("act", (lambda e, b=b, which=which, tt=tt: e.copy(qk[which][:, tt * 512:(tt + 1) * 512], bank(b))),
                                 reads=[bk(b)], writes=[("qk", which)], loose=[("qk", which)])
                        P.op("dve", (lambda e, which=which, hh=hh: e.tensor_copy(qk[which][32 * hh:32 * hh + 32, :],
                                                                                 qkrot[which][32 * hh:32 * hh + 32, :])),
                             reads=[("qkrot", which)], writes=[("qk", which)])
                    w_free(hs)
                    vs = w_get(l, ("HV", g, hh))
                    Vv = wv(vs, 8, 128)
                    for blk in range(16):
                        b = nb()
                        for k in range(8):
                            P.op("pe", (lambda e, b=b, k=k, blk=blk, Vv=Vv: e.matmul(bank(b)[:, 0:128], hT[:, k, blk * 128:(blk + 1) * 128],
                                                                              Vv[:, k, :], start=(k == 0), stop=(k == 7))),
                                 reads=[wk(vs), ("hT", blk // 4)], writes=[bk(b)])
                        P.op("act", (lambda e, b=b, blk=blk: e.copy(Vh[:, blk, :], bank(b)[:, 0:128])),
                             reads=[bk(b)], writes=["Vh"], loose=["Vh"])
                    w_free(vs)
                    nxt_d = GROUPS[GORDER[gi + 1]][1] if (hh == 3 and gi < 2) else None
                    def S_(m, nbk=nbk):
                        for B in range(4 * m, 4 * m + 4):
                            ncol = 256 if (B % nbk) < nbk - 1 else 128
                            sbk = nb()
                            P.op("pe", (lambda e, sbk=sbk, B=B, ncol=ncol: e.matmul(bank(sbk)[:, 0:ncol], qk[1][:, B * 128:(B + 1) * 128],
                                                                                    qk[0][:, B * 128:B * 128 + ncol], start=True, stop=True)),
                                 reads=[("qk", 0), ("qk", 1)], writes=[bk(sbk)])
                            P.op("act", (lambda e, sbk=sbk, B=B, ncol=ncol: e.activation(PT[B % NPT][:, 0:ncol], bank(sbk)[:, 0:ncol], AF.Exp, scale=scale)),
                                 reads=[bk(sbk)], writes=[("PT", B % NPT)])
                            P.op("dve", (lambda e, B=B, ncol=ncol: e.tensor_tensor(PT[B % NPT][:, 0:ncol], PT[B % NPT][:, 0:ncol], mask2[:, 0:ncol], ALU.mult)),
                                 reads=[("PT", B % NPT), "cstb"], writes=[("PT", B % NPT)])

                    def V_(m, nbk=nbk, gi=gi, d=d, hh=hh):
                        nbn, nbd = nb(), nb()
                        for Bq in range(4 * m, 4 * m + 4):
                            srcs = []
                            if Bq % nbk > 0:
                                srcs.append((Bq - 1, 128))
                            srcs.append((Bq, 0))
                            for tgt, isN in ((nbn, True), (nbd, False)):
                                for i, (Bk, co) in enumerate(srcs):
                                    lhs = Vh[:, Bk, :] if isN else onesb
                                    P.op("pe", (lambda e, tgt=tgt, lhs=lhs, Bk=Bk, co=co, Bq=Bq, i=i, n=len(srcs): e.matmul(
                                        bank(tgt)[:, (Bq % 4) * 128:(Bq % 4 + 1) * 128], lhs, PT[Bk % NPT][:, co:co + 128],
                                        start=(i == 0), stop=(i == n - 1))),
                                         reads=[("PT", Bk % NPT), "Vh", "cstb"], writes=[bk(tgt)])
                        dN = span_p(accN[:, hh, :], d, m)
                        dD = span_p(accD[32 * hh:32 * hh + 32, :], d, m)
                        sN = span_c(bank(nbn), d)
                        sD = span_c(bank(nbd)[32 * hh:32 * hh + 32, :], d)
                        if gi == 0:
                            P.op("act", (lambda e, dN=dN, sN=sN: e.copy(dN, sN)), reads=[bk(nbn)], writes=["accN"], loose=["accN"])
                            P.op("act", (lambda e, dD=dD, sD=sD: e.copy(dD, sD)), reads=[bk(nbd)], writes=["accD"], loose=["accD"])
                        else:
                            P.op("dve", (lambda e, dN=dN, sN=sN: e.tensor_tensor(dN, sN, dN, ALU.add)), reads=[bk(nbn), "accN"], writes=["accN"], loose=["accN"])
                            P.op("dve", (lambda e, dD=dD, sD=sD: e.tensor_tensor(dD, sD, dD, ALU.add)), reads=[bk(nbd), "accD"], writes=["accD"], loose=["accD"])

                    kq = list(ALL_UNITS)

                    def nrm(n):
                        if nxt_d is not None:
                            for _ in range(4 * n):
                                if kq:
                                    normalize(gv, nxt_d, units=[kq.pop(0)])

                    S_(0)
                    for m in range(4):
                        if m + 1 < 4:
                            S_(m + 1)
                        nrm(1)
                        V_(m)
                        nrm(1)
                    if nxt_d is not None:
                        pre_norm[0] = True
            barrier()
            for tt in range(4):
                P.op("act", (lambda e, tt=tt: e.activation(accD[:, tt * 512:(tt + 1) * 512], accD[:, tt * 512:(tt + 1) * 512], AF.Ln)),
                     reads=["accD"], writes=[("racc", tt)])
                P.op("act", (lambda e, tt=tt: e.activation(accD[:, tt * 512:(tt + 1) * 512], accD[:, tt * 512:(tt + 1) * 512], AF.Exp, scale=-1.0)),
                     reads=[("racc", tt)], writes=[("racc", tt)])
                for hh in range(4):
                    b = nb()
                    P.op("pe", (lambda e, b=b, hh=hh, tt=tt: e.matmul(bank(b), sel[:, hh, :], accD[:, tt * 512:(tt + 1) * 512], start=True, stop=True)),
                         reads=[("racc", tt), "cstf"], writes=[bk(b)])
                    P.op("dve", (lambda e, b=b, hh=hh, tt=tt: e.tensor_tensor(attnT[:, hh, tt * 512:(tt + 1) * 512], bank(b), accN[:, hh, tt * 512:(tt + 1) * 512], ALU.mult)),
                         reads=[bk(b), "accN"], writes=["attnT"], loose=["attnT"])
            return attnT

        def phase_merge(l, attnT):
            gv = vecs[:, l * NV + V_GMIX: l * NV + V_GMIX + 8]
            psc = vecs[:, l * NV + V_PSC: l * NV + V_PSC + 8]
            mixed = ab(8192, 8192).rearrange("p (k t) -> p k t", k=8)
            merged = ab(24576, 8192).rearrange("p (k t) -> p k t", k=8)
            UW = 528
            ub = [[af(40960 + (st_ * 3 + i) * 2112, UW) for i in range(3)] for st_ in range(2)]
            usetc = [0]
            pooled2 = [ab(73728, 2048).rearrange("p (k t) -> p k t", k=2), ab(A_Y + 6144, 2048).rearrange("p (k t) -> p k t", k=2)]
            sg = [af(A_Y, 512), af(A_Y + 2048, 512)]
            m1 = af(A_Y + 4096, 512)
            barrier()
            sgc = [0]
            for hf in range(2):
                T0 = hf * 1024
                pws = w_get(l, ("PW",))
                PWv = ring[pws][:, 0:2048].rearrange("p (g k c) -> p g k c", g=4, k=2)
                pending = []
                for pi_, pj in enumerate(POOL_ORDER):
                    pus = w_get(l, ("PU", pj))
                    PUv = wv(pus, 8, 256)
                    pooled = pooled2[pi_ % 2]
                    pkey = ("pooled", pi_ % 2)
                    for c4 in range(2):
                        c = pj * 2 + c4
                        grp, c2 = c // 2, c % 2
                        w = POOLW[grp]
                        prevU = None
                        for t2 in range(2):
                            tt = 2 * hf + t2
                            st_ = usetc[0] % 2
                            usetc[0] += 1
                            ubs = ub[st_]
                            uk = lambda i, st_=st_: ("ub", st_, i)
                            U = ubs[0]
                            b = proj(pus, PUv, c4, tt)
                            P.op("act", (lambda e, b=b, U=U: e.copy(U[:, 16:UW], bank(b))), reads=[bk(b)], writes=[uk(0)])
                            if tt == 0:
                                P.op("dve", (lambda e, U=U: e.memset(U[:, 0:16], 0.0)), writes=[uk(0)])
                            elif t2 == 1:
                                P.op("act", (lambda e, U=U, prevU=prevU: e.copy(U[:, 0:16], prevU[:, UW - 16:UW])),
                                     reads=[("ub", 1 - st_, 0)], writes=[uk(0)])
                            else:
                                b = nb()
                                T0 = tt * 512
                                for k in range(8):
                                    P.op("pe", (lambda e, b=b, k=k, c4=c4, PUv=PUv, T0=T0: e.matmul(bank(b)[:, 0:16], PUv[:, k, c4 * 128:(c4 + 1) * 128],
                                                                                                 hT[:, k, T0 - 16:T0], start=(k == 0), stop=(k == 7))),
                                         reads=[wk(pus), ("hT", (T0 - 16) // 512)], writes=[bk(b)])
                                P.op("act", (lambda e, b=b, U=U: e.copy(U[:, 0:16], bank(b)[:, 0:16])), reads=[bk(b)], writes=[uk(0)])
                            cur, ci = U, 0
                            step = 1
                            while step < w:
                                ni = 1 if ci != 1 else 2
                                nxt = ubs[ni]
                                P.op("dve", (lambda e, cur=cur, nxt=nxt, step=step: e.tensor_tensor(nxt[:, step:UW], cur[:, step:UW], cur[:, 0:UW - step], ALU.add)),
                                     reads=[uk(ci)], writes=[uk(ni)])
                                cur, ci = nxt, ni
                                step *= 2
                            P.op("dve", (lambda e, cur=cur, U=U, c2=c2, w=w, t2=t2, pooled=pooled: e.scalar_tensor_tensor(pooled[:, c2, t2 * 512:(t2 + 1) * 512], cur[:, 16:UW], 1.0 / w,
                                                                                                            U[:, 16:UW], ALU.mult, ALU.subtract)),
                                 reads=[uk(ci), uk(0)], writes=[pkey])
                            if tt == 0:
                                P.op("dve", (lambda e, cur=cur, grp=grp: e.tensor_tensor(m1[:, 0:16], cur[:, 16:32], invc[:, grp, :], ALU.mult)),
                                     reads=[uk(ci), "cstf"], writes=["m1"])
                                P.op("dve", (lambda e, U=U, c2=c2, pooled=pooled: e.tensor_tensor(pooled[:, c2, 0:16], m1[:, 0:16], U[:, 16:32], ALU.subtract)),
                                     reads=["m1", uk(0)], writes=[pkey])
                            prevU = U
                        if c2 == 1:
                            def pw_stage(grp=grp, pooled=pooled, pkey=pkey):
                                for oc in range(2):
                                    for t2 in range(2):
                                        b = nb()
                                        for k2 in range(2):
                                            P.op("pe", (lambda e, b=b, k2=k2, oc=oc, t2=t2, grp=grp, PWv=PWv, pooled=pooled: e.matmul(
                                                bank(b), PWv[:, grp, k2, oc * 128:(oc + 1) * 128], pooled[:, k2, t2 * 512:(t2 + 1) * 512],
                                                start=(k2 == 0), stop=(k2 == 1))),
                                                 reads=[wk(pws), pkey], writes=[bk(b)])
                                        cc = grp * 2 + oc
                                        P.op("act", (lambda e, b=b, cc=cc, t2=t2: e.activation(mixed[:, cc, t2 * 512:(t2 + 1) * 512], bank(b), AF.Identity,
                                                                                              scale=psc[:, cc:cc + 1])),
                                             reads=[bk(b), "vecs"], writes=["mixed"], loose=["mixed"])
                            for f_ in pending:
                                f_()
                            pending = [pw_stage]
                    w_free(pus)
                for f_ in pending:
                    f_()
                w_free(pws)
                for j in range(4):
                    ys = w_get(l, ("YB", j))
                    gs = w_get(l, ("GB", j))
                    Yv_, Gv_ = wv(ys, 8, 256), wv(gs, 8, 256)
                    for c4 in range(2):
                        c = 2 * j + c4
                        for t2 in range(2):
                            b1 = nb()
                            for k in range(8):
                                P.op("pe", (lambda e, b1=b1, k=k, c4=c4, t2=t2, Yv_=Yv_: e.matmul(bank(b1), Yv_[:, k, c4 * 128:(c4 + 1) * 128],
                                                                                        mixed[:, k, t2 * 512:(t2 + 1) * 512], start=(k == 0), stop=(k == 7))),
                                     reads=[wk(ys), "mixed"], writes=[bk(b1)])
                            b2 = proj(gs, Gv_, c4, 2 * hf + t2)
                            sgi = sgc[0] % 2
                            sgc[0] += 1
                            P.op("act", (lambda e, b2=b2, sgi=sgi: e.activation(sg[sgi], bank(b2), AF.Sigmoid)), reads=[bk(b2)], writes=[("sg", sgi)])
                            P.op("dve", (lambda e, b1=b1, sgi=sgi, c=c, t2=t2: e.tensor_tensor(merged[:, c, t2 * 512:(t2 + 1) * 512], bank(b1), sg[sgi], ALU.mult)),
                                 reads=[bk(b1), ("sg", sgi)], writes=["merged"], loose=["merged"])
                    w_free(ys)
                    w_free(gs)
                for jj in range(2):
                    yas = w_get(l, ("YA", jj))
                    YAv = wv(yas, 4, 512)
                    for j2 in range(2):
                        j = 2 * jj + j2
                        gs = w_get(l, ("GA", j))
                        Gv_ = wv(gs, 8, 256)
                        for c4 in range(2):
                            c = 2 * j + c4
                            cl = c - 4 * jj
                            for t2 in range(2):
                                tt = 2 * hf + t2
                                b1 = nb()
                                for k in range(4):
                                    P.op("pe", (lambda e, b1=b1, k=k, cl=cl, tt=tt, YAv=YAv: e.matmul(bank(b1), YAv[:, k, cl * 128:(cl + 1) * 128],
                                                                                                    attnT[:, k, tt * 512:(tt + 1) * 512], start=(k == 0), stop=(k == 3))),
                                         reads=[wk(yas), "attnT"], writes=[bk(b1)])
                                b2 = proj(gs, Gv_, c4, tt)
                                sgi = sgc[0] % 2
                                sgc[0] += 1
                                P.op("act", (lambda e, b2=b2, sgi=sgi: e.activation(sg[sgi], bank(b2), AF.Sigmoid)), reads=[bk(b2)], writes=[("sg", sgi)])
                                P.op("dve", (lambda e, b1=b1, sgi=sgi: e.tensor_tensor(m1, bank(b1), sg[sgi], ALU.mult)),
                                     reads=[bk(b1), ("sg", sgi)], writes=["m1"])
                                P.op("dve", (lambda e, c=c, t2=t2: e.tensor_tensor(merged[:, c, t2 * 512:(t2 + 1) * 512], m1, merged[:, c, t2 * 512:(t2 + 1) * 512], ALU.add)),
                                     reads=["m1", "merged"], writes=["merged"], loose=["merged"])
                        w_free(gs)
                    w_free(yas)
                for j in range(4):
                    ws_ = w_get(l, ("WO", j))
                    Wv_ = wv(ws_, 8, 256)
                    for c4 in range(2):
                        c = 2 * j + c4
                        for t2 in range(2):
                            tt = 2 * hf + t2
                            b = nb()
                            for k in range(8):
                                P.op("pe", (lambda e, b=b, k=k, c4=c4, t2=t2, Wv_=Wv_: e.matmul(bank(b), Wv_[:, k, c4 * 128:(c4 + 1) * 128],
                                                                                      merged[:, k, t2 * 512:(t2 + 1) * 512], start=(k == 0), stop=(k == 7))),
                                     reads=[wk(ws_), "merged"], writes=[bk(b)])
                            P.op("dve", (lambda e, b=b, c=c, tt=tt: e.tensor_tensor(xT[:, c, tt * 512:(tt + 1) * 512], bank(b), xT[:, c, tt * 512:(tt + 1) * 512], ALU.add)),
                                 reads=[bk(b), "xT"], writes=["xT"], loose=["xT"])
                    w_free(ws_)

        def phase_ffn(l, s):
            gv = vecs[:, l * NV + V_GFFN: l * NV + V_GFFN + 8]
            cw = vecs[:, l * NV + V_CW: l * NV + V_CW + 132].rearrange("p (c t) -> p c t", t=3)
            cb = vecs[:, l * NV + V_CB: l * NV + V_CB + 44]
            actT = ab(8192, 16384).rearrange("p (k t) -> p k t", k=8)
            Yb = [[af(40960, 2048), af(49152, 2048)], [af(57344, 2048), af(65536, 2048)]]
            barrier()
            norm_stats()
            normalize(gv, 1)
            gctr = [0]
            for pi, pairs in enumerate(FFN_PARTS):
                for pr in pairs:
                    ups = w_get(l, ("UP", pr))
                    UPv = wv(ups, 8, 256)
                    Ys = Yb[pr % 2]
                    for role in range(2):
                        chunk = role
                        cidx = pr + 22 * role
                        G = gctr[0] % 2
                        gctr[0] += 1
                        for tt in range(4):
                            for k in range(8):
                                P.op("pe", (lambda e, G=G, k=k, tt=tt, chunk=chunk, UPv=UPv: e.matmul(psT[G][:, tt * 512:(tt + 1) * 512],
                                                                                                    UPv[:, k, chunk * 128:(chunk + 1) * 128],
                                                                                                    hT[:, k, tt * 512:(tt + 1) * 512], start=(k == 0), stop=(k == 7))),
                                     reads=[wk(ups), ("hT", tt)], writes=[bk(4 * G + tt)])
                        Y = Ys[role]
                        gb = [bk(4 * G + i) for i in range(4)]
                        yk = ("Y", pr % 2, role)
                        P.op("act", (lambda e, Y=Y, G=G, cidx=cidx: e.activation(Y, psT[G][:, :], AF.Identity, bias=cb[:, cidx:cidx + 1],
                                                                                  scale=cw[:, cidx, 2:3])),
                             reads=gb + ["vecs"], writes=[yk])
                        P.op("dve", (lambda e, Y=Y, G=G, cidx=cidx: e.scalar_tensor_tensor(Y[:, 1:S], psT[G][:, 0:S - 1], cw[:, cidx, 1:2], Y[:, 1:S],
                                                                                            ALU.mult, ALU.add)),
                             reads=gb + ["vecs", yk], writes=[yk])
                        P.op("dve", (lambda e, Y=Y, G=G, cidx=cidx: e.scalar_tensor_tensor(Y[:, 2:S], psT[G][:, 0:S - 2], cw[:, cidx, 0:1], Y[:, 2:S],
                                                                                            ALU.mult, ALU.add)),
                             reads=gb + ["vecs", yk], writes=[yk])
                    P.op("act", (lambda e, Ys=Ys: e.activation(Ys[0], Ys[0], AF.Silu)), reads=[("Y", pr % 2, 0)], writes=[("Y", pr % 2, 0)])
                    P.op("dve", (lambda e, Ys=Ys, pr=pr, p0=pairs[0]: e.tensor_tensor(actT[:, pr - p0, :], Ys[0], Ys[1], ALU.mult)),
                         reads=[("Y", pr % 2, 0), ("Y", pr % 2, 1)], writes=["actT"], loose=["actT"])
                    w_free(ups)
                dns = [w_get(l, ("DN", j)) for j in range(pairs[0] // 2, (pairs[-1] + 1) // 2)]
                dv = [wv(dn_, 2, 1024) for dn_ in dns]
                dkeys = [wk(dn_) for dn_ in dns]
                nk = len(pairs)
                last = (pi == len(FFN_PARTS) - 1)
                if last:
                    P.op("pool", lambda e: e.dma_start(out=ab(57344, 4096).rearrange("p (t f) -> p t f", t=16), in_=pin[l, s].rearrange("t p f -> p t f")),
                         writes=["ptok", ("Y", 1, 0), ("Y", 1, 1)], dma="d_p")
                gpl = vecs[:, l * NV + V_GPLE: l * NV + V_GPLE + 8]

                def pre_ple(tt):
                    norm_stats(js=(2 * tt, 2 * tt + 1), xkey=("xTt", tt))
                    normalize(gpl, 1, units=[(tt, k) for k in range(8)])

                order = [(c, tt) for tt in range(4) for c in range(8)] if last else [(c, tt) for c in range(8) for tt in range(4)]
                for c, tt in order:
                    if last and c == 0 and tt >= 2:
                        pre_ple(tt - 2)
                    b = nb()
                    for kk in range(nk):
                        P.op("pe", (lambda e, b=b, kk=kk, c=c, tt=tt, nk=nk, dv=dv: e.matmul(bank(b), dv[kk // 2][:, kk % 2, c * 128:(c + 1) * 128],
                                                                                      actT[:, kk, tt * 512:(tt + 1) * 512], start=(kk == 0), stop=(kk == nk - 1))),
                             reads=dkeys + ["actT"], writes=[bk(b)])
                    P.op("dve", (lambda e, b=b, c=c, tt=tt: e.tensor_tensor(xT[:, c, tt * 512:(tt + 1) * 512], bank(b), xT[:, c, tt * 512:(tt + 1) * 512], ALU.add)),
                         reads=[bk(b), "xT"], writes=["xT", ("xTt", tt)], loose=["xT", ("xTt", tt)])
                if last:
                    pre_ple(2)
                    pre_ple(3)
                for dn_ in dns:
                    w_free(dn_)

        def phase_ple(l, s):
            gv = vecs[:, l * NV + V_GPLE: l * NV + V_GPLE + 8]
            ptok = ab(57344, 4096).rearrange("p (t f) -> p t f", t=16)
            pT = ab(16384, 4096).rearrange("p (k t) -> p k t", k=2)
            sg = [af(24576, 512), af(26624, 512)]
            m1 = [af(28672, 512), af(30720, 512)]
            barrier()
            for k2 in range(2):
                for g4 in range(4):
                    b = nb()
                    bb = bank(b).bitcast(BF16)
                    for i in range(4):
                        tile = g4 * 4 + i
                        P.op("pe", (lambda e, bb=bb, i=i, tile=tile, k2=k2: e.transpose(bb[:, i * 128:(i + 1) * 128], ptok[:, tile, k2 * 128:(k2 + 1) * 128], identb)),
                             reads=["ptok", "cstb"], writes=[bk(b)])
                    P.op("act", (lambda e, bb=bb, k2=k2, g4=g4: e.copy(pT[:, k2, g4 * 512:(g4 + 1) * 512], bb[:, 0:512])), reads=[bk(b)], writes=["pT"], loose=["pT"])
            ps_ = w_get(l, ("PLE",))
            PLv = wv(ps_, 2, 1024)
            ctr = 0
            for j in range(4):
                gs = w_get(l, ("PG", j))
                Gv_ = wv(gs, 8, 256)
                for c4 in range(2):
                    c = 2 * j + c4
                    for tt in range(4):
                        b1 = nb()
                        for k2 in range(2):
                            P.op("pe", (lambda e, b1=b1, k2=k2, c=c, tt=tt: e.matmul(bank(b1), PLv[:, k2, c * 128:(c + 1) * 128],
                                                                                    pT[:, k2, tt * 512:(tt + 1) * 512], start=(k2 == 0), stop=(k2 == 1))),
                                 reads=[wk(ps_), "pT"], writes=[bk(b1)])
                        b2 = proj(gs, Gv_, c4, tt)
                        i = ctr % 2
                        ctr += 1
                        P.op("act", (lambda e, b2=b2, i=i: e.activation(sg[i], bank(b2), AF.Sigmoid)), reads=[bk(b2)], writes=[("sg", i)])
                        P.op("dve", (lambda e, b1=b1, i=i: e.tensor_tensor(m1[i], bank(b1), sg[i], ALU.mult)), reads=[bk(b1), ("sg", i)], writes=[("m1", i)])
                        P.op("dve", (lambda e, i=i, c=c, tt=tt: e.tensor_tensor(xT[:, c, tt * 512:(tt + 1) * 512], m1[i], xT[:, c, tt * 512:(tt + 1) * 512], ALU.add)),
                             reads=[("m1", i), "xT"], writes=["xT"], loose=["xT"])
                w_free(gs)
            w_free(ps_)

        def load_x(s):
            stage = [af(8192 + 4096 * i, 1024) for i in range(4)]
            barrier()
            for tile in range(16):
                sgt = stage[tile % 4]
                P.op("sp", (lambda e, sgt=sgt, tile=tile: e.dma_start(out=sgt, in_=xin[s, tile])), writes=[("stg", tile % 4)], dma=f"d_x{tile % 4}")
                for hb in range(2):
                    b = nb()
                    for i in range(4):
                        k = hb * 4 + i
                        P.op("pe", (lambda e, b=b, i=i, k=k, sgt=sgt: e.transpose(bank(b)[:, i * 128:(i + 1) * 128], sgt[:, k * 128:(k + 1) * 128], ident32)),
                             reads=[("stg", tile % 4), "cstf"], writes=[bk(b)])
                    eng = "act" if hb == 0 else "dve"
                    dst = xT[:, hb * 4:hb * 4 + 4, tile * 128:(tile + 1) * 128]
                    src = bank(b).rearrange("p (a t) -> p a t", a=4)
                    if eng == "act":
                        P.op("act", (lambda e, dst=dst, src=src: e.copy(dst, src)), reads=[bk(b)], writes=["xT"], loose=["xT"])
                    else:
                        P.op("dve", (lambda e, dst=dst, src=src: e.tensor_copy(dst, src)), reads=[bk(b)], writes=["xT"], loose=["xT"])

        def final(s):
            gfin = af(8192, 1024)
            ost = [af(12288, 1024), af(16384, 1024)]
            junk = af(20480, 1024)
            ss = af(24576, 8)
            barrier()
            P.op("sp", lambda e: e.dma_start(out=gfin, in_=gfind), writes=["gfin"], dma="d_c0")
            for tile in range(16):
                G = tile % 2
                half = (tile // 2) % 2
                pz = psT[G][:, half * 1024:(half + 1) * 1024]
                keys = [bk(4 * G + 2 * half), bk(4 * G + 2 * half + 1)]
                for k in range(8):
                    P.op("pe", (lambda e, pz=pz, k=k, tile=tile: e.transpose(pz[:, k * 128:(k + 1) * 128], xT[:, k, tile * 128:(tile + 1) * 128], ident32)),
                         reads=["xT", "cstf"], writes=keys)
                si = tile % 4
                P.op("dve", (lambda e, si=si: e.memset(ss[:, si:si + 1], 0.0)), writes=[("ss", si)])
                P.op("act", (lambda e, pz=pz, si=si: e.activation(junk, pz, AF.Square, accum_out=ss[:, si:si + 1])), reads=keys + [("ss", si)],
                     writes=["junk", ("ss", si)])
                P.op("act", (lambda e, si=si: e.activation(ss[:, si:si + 1], ss[:, si:si + 1], AF.Sqrt, bias=epsT[:, 0:1], scale=1.0 / 1024.0)),
                     reads=[("ss", si), "eps"], writes=[("ss", si)])
                P.op("dve", (lambda e, si=si: e.reciprocal(ss[:, si:si + 1], ss[:, si:si + 1])), reads=[("ss", si)], writes=[("ss", si)])
                o = ost[tile % 2]
                P.op("dve", (lambda e, o=o, pz=pz, si=si: e.scalar_tensor_tensor(o, pz, ss[:, si:si + 1], gfin, ALU.mult, ALU.mult)),
                     reads=keys + [("ss", si), "gfin"], writes=[("ost", tile % 2)])
                P.op("sp", (lambda e, o=o, tile=tile: e.dma_start(out=outd[s, tile], in_=o)), reads=[("ost", tile % 2)], dma=f"d_o{tile % 2}")

        for s in range(nseq):
            load_x(s)
            for l in range(depth):
                attnT = phase_attn(l)
                phase_merge(l, attnT)
                phase_ffn(l, s)
                phase_ple(l, s)
            final(s)
        keys = list(P.res.keys())
        P.op("sp", None, reads=keys, writes=keys)
        P.emit()
    return nc


_NC_CACHE = {}


def kernel(x, p, g_mix, w_in, w_ya, w_yb, pool_w, pool_scale, w_o, g_ffn, w_up, conv_w, conv_b,
           w_down, g_ple, w_ple, w_ple_gate, g_final):
    f = lambda a: np.ascontiguousarray(np.asarray(a, dtype=np.float32))
    x, p = f(x), f(p)
    B = x.shape[0]
    nseq = B // NCORES
    wts = build_weight_stream(f(w_in), f(w_ya), f(w_yb), f(pool_w), f(w_o), f(w_up), f(w_down), f(w_ple), f(w_ple_gate))
    cst, rope = build_consts()
    vec = build_vecs(f(g_mix), f(g_ffn), f(g_ple), f(pool_scale), f(conv_w), f(conv_b))
    gfin = np.ascontiguousarray(np.broadcast_to(f(g_final)[None, :], (128, 1024)))
    if nseq not in _NC_CACHE:
        _NC_CACHE[nseq] = build(nseq, DEPTH)
    nc = _NC_CACHE[nseq]
    in_maps = []
    for c in range(NCORES):
        xs = x[c * nseq:(c + 1) * nseq].reshape(nseq, 16, 128, 1024)
        ps = p[:, c * nseq:(c + 1) * nseq].reshape(DEPTH, nseq, 16, 128, 256)
        in_maps.append({"xin": np.ascontiguousarray(xs), "pin": np.ascontiguousarray(ps), "wt": wts, "vec": vec,
                        "cst": cst, "rope": rope, "gfin": gfin})
    res = run_bass_kernel_spmd(nc, in_maps, core_ids=list(range(NCORES)))
    out = np.concatenate([r["out"].reshape(nseq, S, D) for r in res.results], axis=0)
    return out.astype(np.float32)
```

```python
import contextlib
import math
import numpy as np
import concourse.bass as bass
import concourse.mybir as mybir
from concourse.bass_utils import run_bass_kernel_spmd

F32 = mybir.dt.float32
BF16 = mybir.dt.bfloat16
AF = mybir.ActivationFunctionType
ALU = mybir.AluOpType
AX = mybir.AxisListType

NCORES = 8
S = 2048
D = 1024
DEPTH = 2
NSLOT = 4
GROUPS = ((128, 1), (512, 4), (2048, 16))
POOLW = (2, 4, 8, 16)
GORDER = (1, 2, 0)
NV = 8 + 8 + 8 + 8 + 44 * 3 + 44
V_GMIX, V_GFFN, V_GPLE, V_PSC, V_CW, V_CB = 0, 8, 16, 24, 32, 32 + 132
C_ID, C_SEL, C_INVC, C_MASK, C_ODIV, C_ONE = 0, 128, 640, 704, 960, 1088
NCST = 1216


class _Op:
    __slots__ = ("eng", "fn", "stream", "sidx", "waits", "clock_after", "flagged", "is_dma")


class Prog:
    ENGS = ("pe", "act", "dve", "pool", "sp")

    def __init__(self, nc):
        self.nc = nc
        self.ops = {e: [] for e in self.ENGS}
        self.eng_clock = {e: {} for e in self.ENGS}
        self.stream_ops = {e: [] for e in self.ENGS}
        self.res = {}
        self.epoch = None

    def op(self, eng, fn, reads=(), writes=(), dma=None, loose=()):
        o = _Op()
        o.eng = eng
        o.fn = fn
        o.is_dma = dma is not None
        o.stream = dma if dma is not None else eng
        sl = self.stream_ops.setdefault(o.stream, [])
        o.sidx = len(sl) + 1
        o.flagged = o.is_dma
        deps = {}

        def add(tok, kind):
            if tok is None:
                return
            s, i = tok
            if s == eng and not o.is_dma:
                if eng == "pe" or kind.endswith("_L"):
                    return
            if deps.get(s, 0) < i:
                deps[s] = i

        for k in reads:
            r = self.res.get(k)
            if r is not None:
                add(r["w"], "RAW")
        for k in writes:
            r = self.res.get(k)
            if r is not None:
                sfx = "_L" if k in loose else ""
                add(r["w"], "WAW" + sfx)
                for s, i in r["r"].items():
                    add((s, i), "WAR" + sfx)
        if o.is_dma and o.sidx > 1:
            add((o.stream, o.sidx - 1), "RAW")
        if self.epoch is not None:
            add(self.epoch, "RAW")
        clk = self.eng_clock[eng]
        waits = []
        for s, i in deps.items():
            if clk.get(s, 0) >= i:
                continue
            waits.append((s, i))
            dop = self.stream_ops[s][i - 1]
            dop.flagged = True
            for s2, i2 in dop.clock_after.items():
                if clk.get(s2, 0) < i2:
                    clk[s2] = i2
        o.waits = waits
        ca = dict(clk)
        ca[o.stream] = o.sidx
        o.clock_after = ca
        sl.append(o)
        self.ops[eng].append(o)
        tok = (o.stream, o.sidx)
        for k in reads:
            r = self.res.setdefault(k, {"w": None, "r": {}})
            if r["r"].get(o.stream, 0) < o.sidx:
                r["r"][o.stream] = o.sidx
        for k in writes:
            self.res[k] = {"w": tok, "r": {}}
        return o

    def barrier(self, fn):
        keys = list(self.res.keys())
        o = self.op("dve", fn, reads=keys, writes=keys)
        self.res = {}
        self.epoch = (o.stream, o.sidx)

    def emit(self):
        nc = self.nc
        streams = [s for s in self.stream_ops if self.stream_ops[s]]
        val = {}
        for s in streams:
            c = 0
            vs = []
            for o in self.stream_ops[s]:
                if o.is_dma:
                    c += 16
                elif o.flagged:
                    c += 1
                vs.append(c)
            val[s] = vs
        with contextlib.ExitStack() as st:
            sems = {s: st.enter_context(nc.semaphore("s_" + s)) for s in streams}
            block = st.enter_context(nc.Block())
            engobj = {"pe": block.tensor, "act": block.scalar, "dve": block.vector,
                      "pool": block.gpsimd, "sp": block.sync}

            def make(e):
                def body(eng):
                    for o in self.ops[e]:
                        for (s, i) in o.waits:
                            eng.wait_ge(sems[s], val[s][i - 1])
                        if o.fn is None:
                            continue
                        ins = o.fn(eng)
                        if o.is_dma:
                            ins.then_inc(sems[o.stream], 16)
                        elif o.flagged:
                            ins.then_inc(sems[o.stream], 1)
                return body

            for e in self.ENGS:
                if self.ops[e]:
                    engobj[e](make(e))


def _head_perm(hh):
    rest = list(range(32, 128))
    out = []
    for qd in range(4):
        if qd == hh:
            out += list(range(32))
        else:
            out += rest[:32]
            rest = rest[32:]
    return out


def layer_tiles():
    t = []
    for g in range(3):
        t += [(("R", g), 2048)]
        for hh in range(4):
            t += [(("HQK", g, hh), 2048), (("HV", g, hh), 1024)]
    t.append((("PW",), 2048))
    t += [(("PU", j), 2048) for j in range(4)]
    for j in range(4):
        t += [(("YB", j), 2048), (("GB", j), 2048)]
    t += [(("YA", j), 2048) for j in range(2)]
    t += [(("GA", j), 2048) for j in range(4)]
    t += [(("WO", j), 2048) for j in range(4)]
    t += [(("UP", j), 2048) for j in range(22)]
    t += [(("DN", j), 2048) for j in range(11)]
    t.append((("PLE",), 2048))
    t += [(("PG", j), 2048) for j in range(4)]
    return t


def tile_offsets():
    off = {}
    o = 0
    for name, n in layer_tiles():
        off[name] = (o, n)
        o += n
    return off, o


FFN_PARTS = (range(0, 8), range(8, 16), range(16, 22))


def consume_order():
    seq = []
    for g in GORDER:
        seq += [("R", g)]
        for hh in range(4):
            seq += [("HQK", g, hh), ("HV", g, hh)]
    for hf in range(2):
        seq += [("PW",)] + [("PU", j) for j in range(4)]
        for j in range(4):
            seq += [("YB", j), ("GB", j)]
        seq += [("YA", 0), ("GA", 0), ("GA", 1), ("YA", 1), ("GA", 2), ("GA", 3)]
        seq += [("WO", j) for j in range(4)]
    for pairs in FFN_PARTS:
        seq += [("UP", pr) for pr in pairs]
        seq += [("DN", j) for j in range(pairs[0] // 2, (pairs[-1] + 1) // 2)]
    seq += [("PLE",)] + [("PG", j) for j in range(4)]
    return seq


def _k1024(W, cols):
    sub = W[:, cols]
    n = sub.shape[1]
    return sub.reshape(8, 128, n).transpose(1, 0, 2).reshape(128, 8 * n)


def build_weight_stream(w_in, w_ya, w_yb, pool_w, w_o, w_up, w_down, w_ple, w_ple_gate):
    off, tot = tile_offsets()
    out = np.zeros((DEPTH, 128, tot), np.float32)
    ar = np.arange
    for l in range(DEPTH):
        def put(name, arr):
            o, n = off[name]
            assert arr.shape == (128, n), (name, arr.shape, n)
            out[l, :, o:o + n] = arr
        for g in range(3):
            r = []
            for base0 in (0, 1536):
                for hh in range(4):
                    b = base0 + g * 512 + hh * 128
                    r += [b + i for i in range(32)]
            put(("R", g), _k1024(w_in[l], r))
            for hh in range(4):
                pm = _head_perm(hh)
                bq = g * 512 + hh * 128
                put(("HQK", g, hh), _k1024(w_in[l], [bq + m for m in pm] + [1536 + bq + m for m in pm]))
                put(("HV", g, hh), _k1024(w_in[l], [3072 + bq + i for i in range(128)]))
        for j in range(4):
            put(("PU", j), _k1024(w_in[l], list(4608 + j * 256 + ar(256))))
            put(("GA", j), _k1024(w_in[l], list(5632 + j * 256 + ar(256))))
            put(("GB", j), _k1024(w_in[l], list(6656 + j * 256 + ar(256))))
            put(("YB", j), _k1024(w_yb[l], list(j * 256 + ar(256))))
            put(("WO", j), _k1024(w_o[l], list(j * 256 + ar(256))))
            put(("PG", j), _k1024(w_ple_gate[l], list(j * 256 + ar(256))))
        put(("PW",), pool_w[l].reshape(4, 2, 128, 256).transpose(2, 0, 1, 3).reshape(128, 2048))
        for j in range(2):
            put(("YA", j), w_ya[l][:, j * 512:(j + 1) * 512].reshape(4, 128, 512).transpose(1, 0, 2).reshape(128, 2048))
        put(("PLE",), w_ple[l].reshape(2, 128, 1024).transpose(1, 0, 2).reshape(128, 2048))
        for pr in range(22):
            put(("UP", pr), _k1024(w_up[l], list(pr * 128 + ar(128)) + list(2816 + pr * 128 + ar(128))))
        for j in range(11):
            blk = w_down[l][j * 256:(j + 1) * 256]
            put(("DN", j), blk.reshape(2, 128, 1024).transpose(1, 0, 2).reshape(128, 2048))
    return out


def build_consts():
    cst = np.zeros((128, NCST), np.float32)
    cst[:, C_ID:C_ID + 128] = np.eye(128, dtype=np.float32)
    for hh in range(4):
        cst[32 * hh, C_SEL + hh * 128:C_SEL + (hh + 1) * 128] = 1.0
    for g, w in enumerate(POOLW):
        t = np.arange(16)
        cst[:, C_INVC + g * 16:C_INVC + (g + 1) * 16] = (1.0 / np.minimum(t + 1, w)).astype(np.float32)[None, :]
    p = np.arange(128)[:, None]
    j = np.arange(128)[None, :]
    cst[:, C_MASK:C_MASK + 128] = (p <= j)
    cst[:, C_MASK + 128:C_MASK + 256] = (p >= j)
    cst[:, C_ODIV:C_ODIV + 128] = 1.0 / 1024.0
    cst[:, C_ONE:C_ONE + 128] = 1.0
    pos = np.arange(S, dtype=np.float32)
    inv_freq = np.exp(np.arange(0, 32, 2, dtype=np.float32) * np.float32(-math.log(500000.0) / 32)).astype(np.float32)
    ang = (pos[:, None] * inv_freq[None, :]).astype(np.float32)
    cos, sin = np.cos(ang).T.astype(np.float32), np.sin(ang).T.astype(np.float32)
    c32 = np.concatenate([cos, cos], 0)
    s32 = np.concatenate([sin, -sin], 0)
    rope = np.zeros((128, 2 * S), np.float32)
    rope[:, :S] = np.tile(c32, (4, 1))
    rope[:, S:] = np.tile(s32, (4, 1))
    return cst, rope


def build_vecs(g_mix, g_ffn, g_ple, pool_scale, conv_w, conv_b):
    v = np.zeros((128, DEPTH * NV), np.float32)
    fm = lambda a: a.reshape(-1, 128).T
    for l in range(DEPTH):
        b = l * NV
        v[:, b + V_GMIX:b + V_GMIX + 8] = fm(g_mix[l])
        v[:, b + V_GFFN:b + V_GFFN + 8] = fm(g_ffn[l])
        v[:, b + V_GPLE:b + V_GPLE + 8] = fm(g_ple[l])
        v[:, b + V_PSC:b + V_PSC + 8] = fm(pool_scale[l])
        cw = conv_w[l].reshape(3, 44, 128).transpose(2, 1, 0).reshape(128, 132)
        v[:, b + V_CW:b + V_CW + 132] = cw
        v[:, b + V_CB:b + V_CB + 44] = fm(conv_b[l])
    return v


A_RSTD, A_ROPE, A_ACCD, A_ACCN, A_X, A_Y = 0, 8192, 16384, 24576, 57344, 77824
ARENA = 88064


def build(nseq=4, depth=DEPTH):
    nc = bass.Bass("TRN2", target_bir_lowering=False)
    toff, TOT = tile_offsets()
    xin = nc.dram_tensor("xin", [nseq, 16, 128, 1024], F32, kind="ExternalInput").ap()
    pin = nc.dram_tensor("pin", [DEPTH, nseq, 16, 128, 256], F32, kind="ExternalInput").ap()
    wt = nc.dram_tensor("wt", [DEPTH, 128, TOT], F32, kind="ExternalInput").ap()
    vecd = nc.dram_tensor("vec", [128, DEPTH * NV], F32, kind="ExternalInput").ap()
    cstd = nc.dram_tensor("cst", [128, NCST], F32, kind="ExternalInput").ap()
    roped = nc.dram_tensor("rope", [128, 2 * S], F32, kind="ExternalInput").ap()
    gfind = nc.dram_tensor("gfin", [128, 1024], F32, kind="ExternalInput").ap()
    outd = nc.dram_tensor("out", [nseq, 16, 128, 1024], F32, kind="ExternalOutput").ap()
    wsc = nc.dram_tensor("wsc", [DEPTH, 128, TOT], BF16).ap()

    with contextlib.ExitStack() as st:
        sb = lambda name, shape, dt: st.enter_context(nc.sbuf_tensor(name, shape, dt))
        xT = sb("xT", [128, 8, S], F32)
        hT = sb("hT", [128, 8, S], BF16)
        ring = [sb(f"ring{i}", [128, 2048], BF16) for i in range(NSLOT)]
        vecs = sb("vecs", [128, DEPTH * NV], F32)
        cstf = sb("cstf", [128, NCST], F32)
        cstb = sb("cstb", [128, 640], BF16)
        epsT = sb("epsT", [128, 2], F32)
        arena = sb("arena", [128, ARENA // 2], BF16)
        psT = [st.enter_context(nc.psum_tensor(f"psT{i}", [128, 2048], F32)) for i in range(2)]

        def ab(off, n):
            return arena[:, off // 2: off // 2 + n]

        def af(off, n):
            return arena[:, off // 2: off // 2 + 2 * n].bitcast(F32)

        ident32 = cstf[:, C_ID:C_ID + 128]
        sel = cstf[:, C_SEL:C_SEL + 512].rearrange("p (h m) -> p h m", h=4)
        invc = cstf[:, C_INVC:C_INVC + 64].rearrange("p (g t) -> p g t", g=4)
        identb = cstb[:, 0:128]
        mask2 = cstb[:, 128:384]
        odivb = cstb[:, 384:512]
        onesb = cstb[:, 512:640]

        P = Prog(nc)
        bank_ctr = [0]

        def nb():
            b = bank_ctr[0] % 8
            bank_ctr[0] += 1
            return b

        def bank(b):
            return psT[b // 4][:, (b % 4) * 512:(b % 4 + 1) * 512]

        def bk(b):
            return ("bk", b)

        def barrier():
            P.barrier(lambda e: e.memset(epsT[:, 1:2], 0.0))

        P.op("sp", lambda e: e.dma_start(out=vecs[:], in_=vecd), writes=["vecs"], dma="d_c0")
        P.op("sp", lambda e: e.dma_start(out=cstf[:], in_=cstd), writes=["cstf"], dma="d_c1")
        P.op("dve", lambda e: e.memset(epsT[:, 0:1], 1e-6), writes=["eps"])
        P.op("dve", lambda e: e.tensor_copy(cstb[:, 0:128], cstf[:, C_ID:C_ID + 128]), reads=["cstf"], writes=["cstb"])
        P.op("dve", lambda e: e.tensor_copy(cstb[:, 128:640], cstf[:, C_MASK:C_MASK + 512]), reads=["cstf"], writes=["cstb"])

        corder = consume_order()
        gseq = [(l, nm) for _ in range(nseq) for l in range(depth) for nm in corder]
        wst = {"next": 0, "free": list(range(NSLOT)), "slot": {}, "pos": 0, "saved": set()}

        def w_issue():
            while wst["free"] and wst["next"] < len(gseq):
                idx = wst["next"]
                l, nm = gseq[idx]
                slot = wst["free"].pop(0)
                o, n = toff[nm]
                if (l, nm) in wst["saved"]:
                    P.op("sp", (lambda e, slot=slot, l=l, o=o, n=n: e.dma_start(out=ring[slot][:, 0:n], in_=wsc[l, :, o:o + n])),
                         reads=[("wsc", l, nm)], writes=[("w", slot)], dma=f"d_w{slot}")
                else:
                    P.op("pool", (lambda e, slot=slot, l=l, o=o, n=n: e.dma_start(out=ring[slot][:, 0:n], in_=wt[l, :, o:o + n])),
                         writes=[("w", slot)], dma=f"d_w{slot}")
                    P.op("sp", (lambda e, slot=slot, l=l, o=o, n=n: e.dma_start(out=wsc[l, :, o:o + n], in_=ring[slot][:, 0:n])),
                         reads=[("w", slot)], writes=[("wsc", l, nm)], dma="d_ws")
                    wst["saved"].add((l, nm))
                wst["slot"][idx] = slot
                wst["next"] += 1

        def w_get(l, nm):
            idx = wst["pos"]
            assert gseq[idx] == (l, nm), (gseq[idx], l, nm)
            w_issue()
            assert idx in wst["slot"], ("weight ring too small at", nm)
            wst["pos"] += 1
            slot = wst["slot"][idx]
            return slot

        def w_free(slot):
            wst["free"].append(slot)
            w_issue()

        def wv(slot, k, c):
            return ring[slot][:, 0:k * c].rearrange("p (k c) -> p k c", k=k)

        def wk(slot):
            return ("w", slot)

        def proj(slot, view, chunk, tt, extra_reads=()):
            b = nb()
            for k in range(8):
                P.op("pe", (lambda e, b=b, k=k: e.matmul(bank(b), view[:, k, chunk * 128:(chunk + 1) * 128],
                                                         hT[:, k, tt * 512:(tt + 1) * 512], start=(k == 0), stop=(k == 7))),
                     reads=[wk(slot), ("hT", tt)], writes=[bk(b)])
            return b

        rstd = af(A_RSTD, 2048)

        sqh = [ab(A_Y, 2048).rearrange("p (k t) -> p k t", k=8), ab(A_Y + 4096, 2048).rearrange("p (k t) -> p k t", k=8)]
        RSTD_KEYS = [("rstd", j) for j in range(8)]

        def norm_stats(js=range(8), xkey=None):
            for j in js:
                sqb = sqh[j % 2]
                xs = xT[:, :, j * 256:(j + 1) * 256]
                xk = ["xT"] if xkey is None else [xkey]
                if j % 2 == 0:
                    P.op("act", (lambda e, sqb=sqb, xs=xs: e.activation(sqb, xs, AF.Square)), reads=xk, writes=[("sq", j % 2)])
                else:
                    P.op("dve", (lambda e, sqb=sqb, xs=xs: e.tensor_tensor(sqb, xs, xs, ALU.mult)), reads=xk, writes=[("sq", j % 2)])
                b = nb()
                for k in range(8):
                    P.op("pe", (lambda e, b=b, k=k, sqb=sqb: e.matmul(bank(b)[:, 0:256], odivb, sqb[:, k, :], start=(k == 0), stop=(k == 7))),
                         reads=[("sq", j % 2), "cstb"], writes=[bk(b)])
                rs_ = rstd[:, j * 256:(j + 1) * 256]
                P.op("act", (lambda e, b=b, rs_=rs_: e.activation(rs_, bank(b)[:, 0:256], AF.Ln, bias=epsT[:, 0:1], scale=1.0)),
                     reads=[bk(b), "eps"], writes=[("rstd", j)])
                P.op("act", (lambda e, rs_=rs_: e.activation(rs_, rs_, AF.Exp, scale=-0.5)), reads=[("rstd", j)], writes=[("rstd", j)])

        def pview(ap2, d):
            return ap2 if d == 1 else ap2.rearrange("p (j r) -> p r j", r=d)

        def cview(ap2, d):
            return ap2 if d == 1 else ap2.rearrange("p (r j) -> p r j", r=d)

        ALL_UNITS = [(tt, k) for tt in range(4) for k in range(8)]

        def normalize(gv, d, units=None):
            for tt, k in (ALL_UNITS if units is None else units):
                rk = RSTD_KEYS if d != 1 else [("rstd", 2 * tt), ("rstd", 2 * tt + 1)]
                P.op("dve", (lambda e, k=k, tt=tt: e.scalar_tensor_tensor(span_c(hT[:, k, tt * 512:(tt + 1) * 512], d), span_p(xT[:, k, :], d, tt),
                                                                         gv[:, k:k + 1], span_p(rstd, d, tt), ALU.mult, ALU.mult)),
                     reads=["xT", "vecs"] + rk, writes=[("hT", tt)], loose=[("hT", tt)])

        def span_p(ap2, d, m):
            if d == 1:
                return ap2[:, m * 512:(m + 1) * 512]
            if d == 4:
                return ap2.rearrange("p (j r) -> p r j", r=4)[:, m, :]
            return ap2.rearrange("p (j r) -> p r j", r=16)[:, 4 * m:4 * m + 4, :]

        def span_c(ap2, d):
            return ap2.rearrange("p (a b) -> p a b", a=4) if d == 16 else ap2

        def phase_attn(l):
            gv = vecs[:, l * NV + V_GMIX: l * NV + V_GMIX + 8]
            ropeC = ab(A_ROPE, 2048)
            ropeS = ab(A_ROPE + 4096, 2048)
            accD = af(A_ACCD, 2048)
            accN = af(A_ACCN, 8192).rearrange("p (h t) -> p h t", h=4)
            qk = [ab(A_X, 2048), ab(A_X + 4096, 2048)]
            Vh = ab(A_X + 8192, 2048).rearrange("p (b d) -> p b d", b=16)
            qkrot = [ab(A_X + 12288, 2048), ab(A_X + 16384, 2048)]
            attnT = ab(A_X, 8192).rearrange("p (h t) -> p h t", h=4)
            NPT = 12
            PT = [ab(A_Y + i * 512, 256) for i in range(NPT)]
            t1 = af(A_Y + 6144, 512)
            t2 = af(A_Y + 8192, 512)
            scale = 1.0 / math.sqrt(128.0)
            t3 = af(A_X + 8192, 512)
            SWAP16 = list(range(16, 32)) + list(range(0, 16))

            barrier()
            P.op("pool", lambda e: e.dma_start(out=ab(A_ROPE, 4096), in_=roped), writes=["rope"], dma="d_rope")
            norm_stats()
            barrier()
            pre_norm = [False]
            for gi, g in enumerate(GORDER):
                d = GROUPS[g][1]
                nbk = (S // d) // 128
                if not pre_norm[0]:
                    normalize(gv, d)
                pre_norm[0] = False
                rs = w_get(l, ("R", g))
                Rv = wv(rs, 8, 256)
                for which in range(2):
                    for tt in range(4):
                        b1 = proj(rs, Rv, which, tt)
                        P.op("dve", (lambda e, b1=b1, tt=tt, d=d: e.tensor_tensor(span_c(t1, d), span_c(bank(b1), d), span_p(ropeC, d, tt), ALU.mult)),
                             reads=[bk(b1), "rope"], writes=["t1"])
                        P.op("dve", (lambda e, b1=b1, tt=tt, d=d: e.tensor_tensor(span_c(t2, d), span_c(bank(b1), d), span_p(ropeS, d, tt), ALU.mult)),
                             reads=[bk(b1), "rope"], writes=["t2"])
                        P.op("dve", (lambda e: e.stream_shuffle(t3, t2, SWAP16)), reads=["t2"], writes=["Vh"])
                        P.op("dve", (lambda e, which=which, tt=tt: e.tensor_tensor(qkrot[which][:, tt * 512:(tt + 1) * 512], t1, t3, ALU.add)),
                             reads=["t1", "Vh"], writes=[("qkrot", which)], loose=[("qkrot", which)])
                w_free(rs)
                for hh in range(4):
                    hs = w_get(l, ("HQK", g, hh))
                    Hv = wv(hs, 8, 256)
                    for which in range(2):
                        for tt in range(4):
                            b = proj(hs, Hv, which, tt)
                            P.op("act", (lambda e, b=b, which=which, tt=tt: e.copy(qk[which][:, tt * 512:(tt + 1) * 512], bank(b))),
                                 reads=[bk(b)], writes=[("qk", which)], loose=[("qk", which)])
                        P.op("dve", (lambda e, which=which, hh=hh: e.tensor_copy(qk[which][32 * hh:32 * hh + 32, :],
                                                                                 qkrot[which][32 * hh:32 * hh + 32, :])),
                             reads=[("qkrot", which)], writes=[("qk", which)])
                    w_free(hs)
                    vs = w_get(l, ("HV", g, hh))
                    Vv = wv(vs, 8, 128)
                    for blk in range(16):
                        b = nb()
                        for k in range(8):
                            P.op("pe", (lambda e, b=b, k=k, blk=blk, Vv=Vv: e.matmul(bank(b)[:, 0:128], hT[:, k, blk * 128:(blk + 1) * 128],
                                                                              Vv[:, k, :], start=(k == 0), stop=(k == 7))),
                                 reads=[wk(vs), ("hT", blk // 4)], writes=[bk(b)])
                        P.op("act", (lambda e, b=b, blk=blk: e.copy(Vh[:, blk, :], bank(b)[:, 0:128])),
                             reads=[bk(b)], writes=["Vh"], loose=["Vh"])
                    w_free(vs)
                    nxt_d = GROUPS[GORDER[gi + 1]][1] if (hh == 3 and gi < 2) else None
                    def S_(m, nbk=nbk):
                        for B in range(4 * m, 4 * m + 4):
                            ncol = 256 if (B % nbk) < nbk - 1 else 128
                            sbk = nb()
                            P.op("pe", (lambda e, sbk=sbk, B=B, ncol=ncol: e.matmul(bank(sbk)[:, 0:ncol], qk[1][:, B * 128:(B + 1) * 128],
                                                                                    qk[0][:, B * 128:B * 128 + ncol], start=True, stop=True)),
                                 reads=[("qk", 0), ("qk", 1)], writes=[bk(sbk)])
                            P.op("act", (lambda e, sbk=sbk, B=B, ncol=ncol: e.activation(PT[B % NPT][:, 0:ncol], bank(sbk)[:, 0:ncol], AF.Exp, scale=scale)),
                                 reads=[bk(sbk)], writes=[("PT", B % NPT)])
                            P.op("dve", (lambda e, B=B, ncol=ncol: e.tensor_tensor(PT[B % NPT][:, 0:ncol], PT[B % NPT][:, 0:ncol], mask2[:, 0:ncol], ALU.mult)),
                                 reads=[("PT", B % NPT), "cstb"], writes=[("PT", B % NPT)])

                    def V_(m, nbk=nbk, gi=gi, d=d, hh=hh):
                        nbn, nbd = nb(), nb()
                        for Bq in range(4 * m, 4 * m + 4):
                            srcs = []
                            if Bq % nbk > 0:
                                srcs.append((Bq - 1, 128))
                            srcs.append((Bq, 0))
                            for tgt, isN in ((nbn, True), (nbd, False)):
                                for i, (Bk, co) in enumerate(srcs):
                                    lhs = Vh[:, Bk, :] if isN else onesb
                                    P.op("pe", (lambda e, tgt=tgt, lhs=lhs, Bk=Bk, co=co, Bq=Bq, i=i, n=len(srcs): e.matmul(
                                        bank(tgt)[:, (Bq % 4) * 128:(Bq % 4 + 1) * 128], lhs, PT[Bk % NPT][:, co:co + 128],
                                        start=(i == 0), stop=(i == n - 1))),
                                         reads=[("PT", Bk % NPT), "Vh", "cstb"], writes=[bk(tgt)])
                        dN = span_p(accN[:, hh, :], d, m)
                        dD = span_p(accD[32 * hh:32 * hh + 32, :], d, m)
                        sN = span_c(bank(nbn), d)
                        sD = span_c(bank(nbd)[32 * hh:32 * hh + 32, :], d)
                        if gi == 0:
                            P.op("act", (lambda e, dN=dN, sN=sN: e.copy(dN, sN)), reads=[bk(nbn)], writes=["accN"], loose=["accN"])
                            P.op("act", (lambda e, dD=dD, sD=sD: e.copy(dD, sD)), reads=[bk(nbd)], writes=["accD"], loose=["accD"])
                        else:
                            P.op("dve", (lambda e, dN=dN, sN=sN: e.tensor_tensor(dN, sN, dN, ALU.add)), reads=[bk(nbn), "accN"], writes=["accN"], loose=["accN"])
                            P.op("dve", (lambda e, dD=dD, sD=sD: e.tensor_tensor(dD, sD, dD, ALU.add)), reads=[bk(nbd), "accD"], writes=["accD"], loose=["accD"])

                    kq = list(ALL_UNITS)

                    def nrm(n):
                        if nxt_d is not None:
                            for _ in range(4 * n):
                                if kq:
                                    normalize(gv, nxt_d, units=[kq.pop(0)])

                    S_(0)
                    for m in range(4):
                        if m + 1 < 4:
                            S_(m + 1)
                        nrm(1)
                        V_(m)
                        nrm(1)
                    if nxt_d is not None:
                        pre_norm[0] = True
            barrier()
            for tt in range(4):
                P.op("act", (lambda e, tt=tt: e.activation(accD[:, tt * 512:(tt + 1) * 512], accD[:, tt * 512:(tt + 1) * 512], AF.Ln)),
                     reads=["accD"], writes=[("racc", tt)])
                P.op("act", (lambda e, tt=tt: e.activation(accD[:, tt * 512:(tt + 1) * 512], accD[:, tt * 512:(tt + 1) * 512], AF.Exp, scale=-1.0)),
                     reads=[("racc", tt)], writes=[("racc", tt)])
                for hh in range(4):
                    b = nb()
                    P.op("pe", (lambda e, b=b, hh=hh, tt=tt: e.matmul(bank(b), sel[:, hh, :], accD[:, tt * 512:(tt + 1) * 512], start=True, stop=True)),
                         reads=[("racc", tt), "cstf"], writes=[bk(b)])
                    P.op("dve", (lambda e, b=b, hh=hh, tt=tt: e.tensor_tensor(attnT[:, hh, tt * 512:(tt + 1) * 512], bank(b), accN[:, hh, tt * 512:(tt + 1) * 512], ALU.mult)),
                         reads=[bk(b), "accN"], writes=["attnT"], loose=["attnT"])
            return attnT

        def phase_merge(l, attnT):
            gv = vecs[:, l * NV + V_GMIX: l * NV + V_GMIX + 8]
            psc = vecs[:, l * NV + V_PSC: l * NV + V_PSC + 8]
            mixed = ab(8192, 8192).rearrange("p (k t) -> p k t", k=8)
            merged = ab(24576, 8192).rearrange("p (k t) -> p k t", k=8)
            UW = 528
            ub = [[af(40960 + (st_ * 3 + i) * 2112, UW) for i in range(3)] for st_ in range(2)]
            usetc = [0]
            pooled2 = [ab(73728, 2048).rearrange("p (k t) -> p k t", k=2), ab(A_Y + 6144, 2048).rearrange("p (k t) -> p k t", k=2)]
            sg = [af(A_Y, 512), af(A_Y + 2048, 512)]
            m1 = af(A_Y + 4096, 512)
            barrier()
            sgc = [0]
            for hf in range(2):
                T0 = hf * 1024
                pws = w_get(l, ("PW",))
                PWv = ring[pws][:, 0:2048].rearrange("p (g k c) -> p g k c", g=4, k=2)
                pending = []
                for pj in range(4):
                    pus = w_get(l, ("PU", pj))
                    PUv = wv(pus, 8, 256)
                    pooled = pooled2[pj % 2]
                    pkey = ("pooled", pj % 2)
                    for c4 in range(2):
                        c = pj * 2 + c4
                        grp, c2 = c // 2, c % 2
                        w = POOLW[grp]
                        prevU = None
                        for t2 in range(2):
                            tt = 2 * hf + t2
                            st_ = usetc[0] % 2
                            usetc[0] += 1
                            ubs = ub[st_]
                            uk = lambda i, st_=st_: ("ub", st_, i)
                            U = ubs[0]
                            b = proj(pus, PUv, c4, tt)
                            P.op("act", (lambda e, b=b, U=U: e.copy(U[:, 16:UW], bank(b))), reads=[bk(b)], writes=[uk(0)])
                            if tt == 0:
                                P.op("dve", (lambda e, U=U: e.memset(U[:, 0:16], 0.0)), writes=[uk(0)])
                            elif t2 == 1:
                                P.op("act", (lambda e, U=U, prevU=prevU: e.copy(U[:, 0:16], prevU[:, UW - 16:UW])),
                                     reads=[("ub", 1 - st_, 0)], writes=[uk(0)])
                            else:
                                b = nb()
                                T0 = tt * 512
                                for k in range(8):
                                    P.op("pe", (lambda e, b=b, k=k, c4=c4, PUv=PUv, T0=T0: e.matmul(bank(b)[:, 0:16], PUv[:, k, c4 * 128:(c4 + 1) * 128],
                                                                                                 hT[:, k, T0 - 16:T0], start=(k == 0), stop=(k == 7))),
                                         reads=[wk(pus), ("hT", (T0 - 16) // 512)], writes=[bk(b)])
                                P.op("act", (lambda e, b=b, U=U: e.copy(U[:, 0:16], bank(b)[:, 0:16])), reads=[bk(b)], writes=[uk(0)])
                            cur, ci = U, 0
                            step = 1
                            while step < w:
                                ni = 1 if ci != 1 else 2
                                nxt = ubs[ni]
                                P.op("dve", (lambda e, cur=cur, nxt=nxt, step=step: e.tensor_tensor(nxt[:, step:UW], cur[:, step:UW], cur[:, 0:UW - step], ALU.add)),
                                     reads=[uk(ci)], writes=[uk(ni)])
                                cur, ci = nxt, ni
                                step *= 2
                            P.op("dve", (lambda e, cur=cur, U=U, c2=c2, w=w, t2=t2, pooled=pooled: e.scalar_tensor_tensor(pooled[:, c2, t2 * 512:(t2 + 1) * 512], cur[:, 16:UW], 1.0 / w,
                                                                                                            U[:, 16:UW], ALU.mult, ALU.subtract)),
                                 reads=[uk(ci), uk(0)], writes=[pkey])
                            if tt == 0:
                                P.op("dve", (lambda e, cur=cur, grp=grp: e.tensor_tensor(m1[:, 0:16], cur[:, 16:32], invc[:, grp, :], ALU.mult)),
                                     reads=[uk(ci), "cstf"], writes=["m1"])
                                P.op("dve", (lambda e, U=U, c2=c2, pooled=pooled: e.tensor_tensor(pooled[:, c2, 0:16], m1[:, 0:16], U[:, 16:32], ALU.subtract)),
                                     reads=["m1", uk(0)], writes=[pkey])
                            prevU = U
                        if c2 == 1:
                            def pw_stage(grp=grp, pooled=pooled, pkey=pkey):
                                for oc in range(2):
                                    for t2 in range(2):
                                        b = nb()
                                        for k2 in range(2):
                                            P.op("pe", (lambda e, b=b, k2=k2, oc=oc, t2=t2, grp=grp, PWv=PWv, pooled=pooled: e.matmul(
                                                bank(b), PWv[:, grp, k2, oc * 128:(oc + 1) * 128], pooled[:, k2, t2 * 512:(t2 + 1) * 512],
                                                start=(k2 == 0), stop=(k2 == 1))),
                                                 reads=[wk(pws), pkey], writes=[bk(b)])
                                        cc = grp * 2 + oc
                                        P.op("act", (lambda e, b=b, cc=cc, t2=t2: e.activation(mixed[:, cc, t2 * 512:(t2 + 1) * 512], bank(b), AF.Identity,
                                                                                              scale=psc[:, cc:cc + 1])),
                                             reads=[bk(b), "vecs"], writes=["mixed"], loose=["mixed"])
                            for f_ in pending:
                                f_()
                            pending = [pw_stage]
                    w_free(pus)
                for f_ in pending:
                    f_()
                w_free(pws)
                for j in range(4):
                    ys = w_get(l, ("YB", j))
                    gs = w_get(l, ("GB", j))
                    Yv_, Gv_ = wv(ys, 8, 256), wv(gs, 8, 256)
                    for c4 in range(2):
                        c = 2 * j + c4
                        for t2 in range(2):
                            b1 = nb()
                            for k in range(8):
                                P.op("pe", (lambda e, b1=b1, k=k, c4=c4, t2=t2, Yv_=Yv_: e.matmul(bank(b1), Yv_[:, k, c4 * 128:(c4 + 1) * 128],
                                                                                        mixed[:, k, t2 * 512:(t2 + 1) * 512], start=(k == 0), stop=(k == 7))),
                                     reads=[wk(ys), "mixed"], writes=[bk(b1)])
                            b2 = proj(gs, Gv_, c4, 2 * hf + t2)
                            sgi = sgc[0] % 2
                            sgc[0] += 1
                            P.op("act", (lambda e, b2=b2, sgi=sgi: e.activation(sg[sgi], bank(b2), AF.Sigmoid)), reads=[bk(b2)], writes=[("sg", sgi)])
                            P.op("dve", (lambda e, b1=b1, sgi=sgi, c=c, t2=t2: e.tensor_tensor(merged[:, c, t2 * 512:(t2 + 1) * 512], bank(b1), sg[sgi], ALU.mult)),
                                 reads=[bk(b1), ("sg", sgi)], writes=["merged"], loose=["merged"])
                    w_free(ys)
                    w_free(gs)
                for jj in range(2):
                    yas = w_get(l, ("YA", jj))
                    YAv = wv(yas, 4, 512)
                    for j2 in range(2):
                        j = 2 * jj + j2
                        gs = w_get(l, ("GA", j))
                        Gv_ = wv(gs, 8, 256)
                        for c4 in range(2):
                            c = 2 * j + c4
                            cl = c - 4 * jj
                            for t2 in range(2):
                                tt = 2 * hf + t2
                                b1 = nb()
                                for k in range(4):
                                    P.op("pe", (lambda e, b1=b1, k=k, cl=cl, tt=tt, YAv=YAv: e.matmul(bank(b1), YAv[:, k, cl * 128:(cl + 1) * 128],
                                                                                                    attnT[:, k, tt * 512:(tt + 1) * 512], start=(k == 0), stop=(k == 3))),
                                         reads=[wk(yas), "attnT"], writes=[bk(b1)])
                                b2 = proj(gs, Gv_, c4, tt)
                                sgi = sgc[0] % 2
                                sgc[0] += 1
                                P.op("act", (lambda e, b2=b2, sgi=sgi: e.activation(sg[sgi], bank(b2), AF.Sigmoid)), reads=[bk(b2)], writes=[("sg", sgi)])
                                P.op("dve", (lambda e, b1=b1, sgi=sgi: e.tensor_tensor(m1, bank(b1), sg[sgi], ALU.mult)),
                                     reads=[bk(b1), ("sg", sgi)], writes=["m1"])
                                P.op("dve", (lambda e, c=c, t2=t2: e.tensor_tensor(merged[:, c, t2 * 512:(t2 + 1) * 512], m1, merged[:, c, t2 * 512:(t2 + 1) * 512], ALU.add)),
                                     reads=["m1", "merged"], writes=["merged"], loose=["merged"])
                        w_free(gs)
                    w_free(yas)
                for j in range(4):
                    ws_ = w_get(l, ("WO", j))
                    Wv_ = wv(ws_, 8, 256)
                    for c4 in range(2):
                        c = 2 * j + c4
                        for t2 in range(2):
                            tt = 2 * hf + t2
                            b = nb()
                            for k in range(8):
                                P.op("pe", (lambda e, b=b, k=k, c4=c4, t2=t2, Wv_=Wv_: e.matmul(bank(b), Wv_[:, k, c4 * 128:(c4 + 1) * 128],
                                                                                      merged[:, k, t2 * 512:(t2 + 1) * 512], start=(k == 0), stop=(k == 7))),
                                     reads=[wk(ws_), "merged"], writes=[bk(b)])
                            P.op("dve", (lambda e, b=b, c=c, tt=tt: e.tensor_tensor(xT[:, c, tt * 512:(tt + 1) * 512], bank(b), xT[:, c, tt * 512:(tt + 1) * 512], ALU.add)),
                                 reads=[bk(b), "xT"], writes=["xT"], loose=["xT"])
                    w_free(ws_)

        def phase_ffn(l, s):
            gv = vecs[:, l * NV + V_GFFN: l * NV + V_GFFN + 8]
            cw = vecs[:, l * NV + V_CW: l * NV + V_CW + 132].rearrange("p (c t) -> p c t", t=3)
            cb = vecs[:, l * NV + V_CB: l * NV + V_CB + 44]
            actT = ab(8192, 16384).rearrange("p (k t) -> p k t", k=8)
            Yb = [[af(40960, 2048), af(49152, 2048)], [af(57344, 2048), af(65536, 2048)]]
            barrier()
            norm_stats()
            normalize(gv, 1)
            gctr = [0]
            for pi, pairs in enumerate(FFN_PARTS):
                for pr in pairs:
                    ups = w_get(l, ("UP", pr))
                    UPv = wv(ups, 8, 256)
                    Ys = Yb[pr % 2]
                    for role in range(2):
                        chunk = role
                        cidx = pr + 22 * role
                        G = gctr[0] % 2
                        gctr[0] += 1
                        for tt in range(4):
                            for k in range(8):
                                P.op("pe", (lambda e, G=G, k=k, tt=tt, chunk=chunk, UPv=UPv: e.matmul(psT[G][:, tt * 512:(tt + 1) * 512],
                                                                                                    UPv[:, k, chunk * 128:(chunk + 1) * 128],
                                                                                                    hT[:, k, tt * 512:(tt + 1) * 512], start=(k == 0), stop=(k == 7))),
                                     reads=[wk(ups), ("hT", tt)], writes=[bk(4 * G + tt)])
                        Y = Ys[role]
                        gb = [bk(4 * G + i) for i in range(4)]
                        yk = ("Y", pr % 2, role)
                        P.op("act", (lambda e, Y=Y, G=G, cidx=cidx: e.activation(Y, psT[G][:, :], AF.Identity, bias=cb[:, cidx:cidx + 1],
                                                                                  scale=cw[:, cidx, 2:3])),
                             reads=gb + ["vecs"], writes=[yk])
                        P.op("dve", (lambda e, Y=Y, G=G, cidx=cidx: e.scalar_tensor_tensor(Y[:, 1:S], psT[G][:, 0:S - 1], cw[:, cidx, 1:2], Y[:, 1:S],
                                                                                            ALU.mult, ALU.add)),
                             reads=gb + ["vecs", yk], writes=[yk])
                        P.op("dve", (lambda e, Y=Y, G=G, cidx=cidx: e.scalar_tensor_tensor(Y[:, 2:S], psT[G][:, 0:S - 2], cw[:, cidx, 0:1], Y[:, 2:S],
                                                                                            ALU.mult, ALU.add)),
                             reads=gb + ["vecs", yk], writes=[yk])
                    P.op("act", (lambda e, Ys=Ys: e.activation(Ys[0], Ys[0], AF.Silu)), reads=[("Y", pr % 2, 0)], writes=[("Y", pr % 2, 0)])
                    P.op("dve", (lambda e, Ys=Ys, pr=pr, p0=pairs[0]: e.tensor_tensor(actT[:, pr - p0, :], Ys[0], Ys[1], ALU.mult)),
                         reads=[("Y", pr % 2, 0), ("Y", pr % 2, 1)], writes=["actT"], loose=["actT"])
                    w_free(ups)
                dns = [w_get(l, ("DN", j)) for j in range(pairs[0] // 2, (pairs[-1] + 1) // 2)]
                dv = [wv(dn_, 2, 1024) for dn_ in dns]
                dkeys = [wk(dn_) for dn_ in dns]
                nk = len(pairs)
                last = (pi == len(FFN_PARTS) - 1)
                if last:
                    P.op("pool", lambda e: e.dma_start(out=ab(57344, 4096).rearrange("p (t f) -> p t f", t=16), in_=pin[l, s].rearrange("t p f -> p t f")),
                         writes=["ptok", ("Y", 1, 0), ("Y", 1, 1)], dma="d_p")
                gpl = vecs[:, l * NV + V_GPLE: l * NV + V_GPLE + 8]

                def pre_ple(tt):
                    norm_stats(js=(2 * tt, 2 * tt + 1), xkey=("xTt", tt))
                    normalize(gpl, 1, units=[(tt, k) for k in range(8)])

                order = [(c, tt) for tt in range(4) for c in range(8)] if last else [(c, tt) for c in range(8) for tt in range(4)]
                for c, tt in order:
                    if last and c == 0 and tt >= 2:
                        pre_ple(tt - 2)
                    b = nb()
                    for kk in range(nk):
                        P.op("pe", (lambda e, b=b, kk=kk, c=c, tt=tt, nk=nk, dv=dv: e.matmul(bank(b), dv[kk // 2][:, kk % 2, c * 128:(c + 1) * 128],
                                                                                      actT[:, kk, tt * 512:(tt + 1) * 512], start=(kk == 0), stop=(kk == nk - 1))),
                             reads=dkeys + ["actT"], writes=[bk(b)])
                    P.op("dve", (lambda e, b=b, c=c, tt=tt: e.tensor_tensor(xT[:, c, tt * 512:(tt + 1) * 512], bank(b), xT[:, c, tt * 512:(tt + 1) * 512], ALU.add)),
                         reads=[bk(b), "xT"], writes=["xT", ("xTt", tt)], loose=["xT", ("xTt", tt)])
                if last:
                    pre_ple(2)
                    pre_ple(3)
                for dn_ in dns:
                    w_free(dn_)

        def phase_ple(l, s):
            gv = vecs[:, l * NV + V_GPLE: l * NV + V_GPLE + 8]
            ptok = ab(57344, 4096).rearrange("p (t f) -> p t f", t=16)
            pT = ab(16384, 4096).rearrange("p (k t) -> p k t", k=2)
            sg = [af(24576, 512), af(26624, 512)]
            m1 = [af(28672, 512), af(30720, 512)]
            barrier()
            for k2 in range(2):
                for g4 in range(4):
                    b = nb()
                    bb = bank(b).bitcast(BF16)
                    for i in range(4):
                        tile = g4 * 4 + i
                        P.op("pe", (lambda e, bb=bb, i=i, tile=tile, k2=k2: e.transpose(bb[:, i * 128:(i + 1) * 128], ptok[:, tile, k2 * 128:(k2 + 1) * 128], identb)),
                             reads=["ptok", "cstb"], writes=[bk(b)])
                    P.op("act", (lambda e, bb=bb, k2=k2, g4=g4: e.copy(pT[:, k2, g4 * 512:(g4 + 1) * 512], bb[:, 0:512])), reads=[bk(b)], writes=["pT"], loose=["pT"])
            ps_ = w_get(l, ("PLE",))
            PLv = wv(ps_, 2, 1024)
            ctr = 0
            for j in range(4):
                gs = w_get(l, ("PG", j))
                Gv_ = wv(gs, 8, 256)
                for c4 in range(2):
                    c = 2 * j + c4
                    for tt in range(4):
                        b1 = nb()
                        for k2 in range(2):
                            P.op("pe", (lambda e, b1=b1, k2=k2, c=c, tt=tt: e.matmul(bank(b1), PLv[:, k2, c * 128:(c + 1) * 128],
                                                                                    pT[:, k2, tt * 512:(tt + 1) * 512], start=(k2 == 0), stop=(k2 == 1))),
                                 reads=[wk(ps_), "pT"], writes=[bk(b1)])
                        b2 = proj(gs, Gv_, c4, tt)
                        i = ctr % 2
                        ctr += 1
                        P.op("act", (lambda e, b2=b2, i=i: e.activation(sg[i], bank(b2), AF.Sigmoid)), reads=[bk(b2)], writes=[("sg", i)])
                        P.op("dve", (lambda e, b1=b1, i=i: e.tensor_tensor(m1[i], bank(b1), sg[i], ALU.mult)), reads=[bk(b1), ("sg", i)], writes=[("m1", i)])
                        P.op("dve", (lambda e, i=i, c=c, tt=tt: e.tensor_tensor(xT[:, c, tt * 512:(tt + 1) * 512], m1[i], xT[:, c, tt * 512:(tt + 1) * 512], ALU.add)),
                             reads=[("m1", i), "xT"], writes=["xT"], loose=["xT"])
                w_free(gs)
            w_free(ps_)

        def load_x(s):
            stage = [af(8192 + 4096 * i, 1024) for i in range(4)]
            barrier()
            for tile in range(16):
                sgt = stage[tile % 4]
                P.op("sp", (lambda e, sgt=sgt, tile=tile: e.dma_start(out=sgt, in_=xin[s, tile])), writes=[("stg", tile % 4)], dma=f"d_x{tile % 4}")
                for hb in range(2):
                    b = nb()
                    for i in range(4):
                        k = hb * 4 + i
                        P.op("pe", (lambda e, b=b, i=i, k=k, sgt=sgt: e.transpose(bank(b)[:, i * 128:(i + 1) * 128], sgt[:, k * 128:(k + 1) * 128], ident32)),
                             reads=[("stg", tile % 4), "cstf"], writes=[bk(b)])
                    eng = "act" if hb == 0 else "dve"
                    dst = xT[:, hb * 4:hb * 4 + 4, tile * 128:(tile + 1) * 128]
                    src = bank(b).rearrange("p (a t) -> p a t", a=4)
                    if eng == "act":
                        P.op("act", (lambda e, dst=dst, src=src: e.copy(dst, src)), reads=[bk(b)], writes=["xT"], loose=["xT"])
                    else:
                        P.op("dve", (lambda e, dst=dst, src=src: e.tensor_copy(dst, src)), reads=[bk(b)], writes=["xT"], loose=["xT"])

        def final(s):
            gfin = af(8192, 1024)
            ost = [af(12288, 1024), af(16384, 1024)]
            junk = af(20480, 1024)
            ss = af(24576, 8)
            barrier()
            P.op("sp", lambda e: e.dma_start(out=gfin, in_=gfind), writes=["gfin"], dma="d_c0")
            for tile in range(16):
                G = tile % 2
                half = (tile // 2) % 2
                pz = psT[G][:, half * 1024:(half + 1) * 1024]
                keys = [bk(4 * G + 2 * half), bk(4 * G + 2 * half + 1)]
                for k in range(8):
                    P.op("pe", (lambda e, pz=pz, k=k, tile=tile: e.transpose(pz[:, k * 128:(k + 1) * 128], xT[:, k, tile * 128:(tile + 1) * 128], ident32)),
                         reads=["xT", "cstf"], writes=keys)
                si = tile % 4
                P.op("dve", (lambda e, si=si: e.memset(ss[:, si:si + 1], 0.0)), writes=[("ss", si)])
                P.op("act", (lambda e, pz=pz, si=si: e.activation(junk, pz, AF.Square, accum_out=ss[:, si:si + 1])), reads=keys + [("ss", si)],
                     writes=["junk", ("ss", si)])
                P.op("act", (lambda e, si=si: e.activation(ss[:, si:si + 1], ss[:, si:si + 1], AF.Sqrt, bias=epsT[:, 0:1], scale=1.0 / 1024.0)),
                     reads=[("ss", si), "eps"], writes=[("ss", si)])
                P.op("dve", (lambda e, si=si: e.reciprocal(ss[:, si:si + 1], ss[:, si:si + 1])), reads=[("ss", si)], writes=[("ss", si)])
                o = ost[tile % 2]
                P.op("dve", (lambda e, o=o, pz=pz, si=si: e.scalar_tensor_tensor(o, pz, ss[:, si:si + 1], gfin, ALU.mult, ALU.mult)),
                     reads=keys + [("ss", si), "gfin"], writes=[("ost", tile % 2)])
                P.op("sp", (lambda e, o=o, tile=tile: e.dma_start(out=outd[s, tile], in_=o)), reads=[("ost", tile % 2)], dma=f"d_o{tile % 2}")

        for s in range(nseq):
            load_x(s)
            for l in range(depth):
                attnT = phase_attn(l)
                phase_merge(l, attnT)
                phase_ffn(l, s)
                phase_ple(l, s)
            final(s)
        keys = list(P.res.keys())
        P.op("sp", None, reads=keys, writes=keys)
        P.emit()
    return nc


_NC_CACHE = {}


def kernel(x, p, g_mix, w_in, w_ya, w_yb, pool_w, pool_scale, w_o, g_ffn, w_up, conv_w, conv_b,
           w_down, g_ple, w_ple, w_ple_gate, g_final):
    f = lambda a: np.ascontiguousarray(np.asarray(a, dtype=np.float32))
    x, p = f(x), f(p)
    B = x.shape[0]
    nseq = B // NCORES
    wts = build_weight_stream(f(w_in), f(w_ya), f(w_yb), f(pool_w), f(w_o), f(w_up), f(w_down), f(w_ple), f(w_ple_gate))
    cst, rope = build_consts()
    vec = build_vecs(f(g_mix), f(g_ffn), f(g_ple), f(pool_scale), f(conv_w), f(conv_b))
    gfin = np.ascontiguousarray(np.broadcast_to(f(g_final)[None, :], (128, 1024)))
    if nseq not in _NC_CACHE:
        _NC_CACHE[nseq] = build(nseq, DEPTH)
    nc = _NC_CACHE[nseq]
    in_maps = []
    for c in range(NCORES):
        xs = x[c * nseq:(c + 1) * nseq].reshape(nseq, 16, 128, 1024)
        ps = p[:, c * nseq:(c + 1) * nseq].reshape(DEPTH, nseq, 16, 128, 256)
        in_maps.append({"xin": np.ascontiguousarray(xs), "pin": np.ascontiguousarray(ps), "wt": wts, "vec": vec,
                        "cst": cst, "rope": rope, "gfin": gfin})
    res = run_bass_kernel_spmd(nc, in_maps, core_ids=list(range(NCORES)))
    out = np.concatenate([r["out"].reshape(nseq, S, D) for r in res.results], axis=0)
    return out.astype(np.float32)
```

```python
import contextlib
import math
import numpy as np
import concourse.bass as bass
import concourse.mybir as mybir
from concourse.bass_utils import run_bass_kernel_spmd

F32 = mybir.dt.float32
BF16 = mybir.dt.bfloat16
AF = mybir.ActivationFunctionType
ALU = mybir.AluOpType
AX = mybir.AxisListType

NCORES = 8
S = 2048
D = 1024
DEPTH = 2
NSLOT = 4
GROUPS = ((128, 1), (512, 4), (2048, 16))
POOLW = (2, 4, 8, 16)
GORDER = (1, 2, 0)
NV = 8 + 8 + 8 + 8 + 44 * 3 + 44
V_GMIX, V_GFFN, V_GPLE, V_PSC, V_CW, V_CB = 0, 8, 16, 24, 32, 32 + 132
C_ID, C_SEL, C_INVC, C_MASK, C_ODIV, C_ONE = 0, 128, 640, 704, 960, 1088
NCST = 1216


class _Op:
    __slots__ = ("eng", "fn", "stream", "sidx", "waits", "clock_after", "flagged", "is_dma")


class Prog:
    ENGS = ("pe", "act", "dve", "pool", "sp")

    def __init__(self, nc):
        self.nc = nc
        self.ops = {e: [] for e in self.ENGS}
        self.eng_clock = {e: {} for e in self.ENGS}
        self.stream_ops = {e: [] for e in self.ENGS}
        self.res = {}
        self.epoch = None

    def op(self, eng, fn, reads=(), writes=(), dma=None, loose=()):
        o = _Op()
        o.eng = eng
        o.fn = fn
        o.is_dma = dma is not None
        o.stream = dma if dma is not None else eng
        sl = self.stream_ops.setdefault(o.stream, [])
        o.sidx = len(sl) + 1
        o.flagged = o.is_dma
        deps = {}

        def add(tok, kind):
            if tok is None:
                return
            s, i = tok
            if s == eng and not o.is_dma:
                if eng == "pe" or kind.endswith("_L"):
                    return
            if deps.get(s, 0) < i:
                deps[s] = i

        for k in reads:
            r = self.res.get(k)
            if r is not None:
                add(r["w"], "RAW")
        for k in writes:
            r = self.res.get(k)
            if r is not None:
                sfx = "_L" if k in loose else ""
                add(r["w"], "WAW" + sfx)
                for s, i in r["r"].items():
                    add((s, i), "WAR" + sfx)
        if o.is_dma and o.sidx > 1:
            add((o.stream, o.sidx - 1), "RAW")
        if self.epoch is not None:
            add(self.epoch, "RAW")
        clk = self.eng_clock[eng]
        waits = []
        for s, i in deps.items():
            if clk.get(s, 0) >= i:
                continue
            waits.append((s, i))
            dop = self.stream_ops[s][i - 1]
            dop.flagged = True
            for s2, i2 in dop.clock_after.items():
                if clk.get(s2, 0) < i2:
                    clk[s2] = i2
        o.waits = waits
        ca = dict(clk)
        ca[o.stream] = o.sidx
        o.clock_after = ca
        sl.append(o)
        self.ops[eng].append(o)
        tok = (o.stream, o.sidx)
        for k in reads:
            r = self.res.setdefault(k, {"w": None, "r": {}})
            if r["r"].get(o.stream, 0) < o.sidx:
                r["r"][o.stream] = o.sidx
        for k in writes:
            self.res[k] = {"w": tok, "r": {}}
        return o

    def barrier(self, fn):
        keys = list(self.res.keys())
        o = self.op("dve", fn, reads=keys, writes=keys)
        self.res = {}
        self.epoch = (o.stream, o.sidx)

    def emit(self):
        nc = self.nc
        streams = [s for s in self.stream_ops if self.stream_ops[s]]
        val = {}
        for s in streams:
            c = 0
            vs = []
            for o in self.stream_ops[s]:
                if o.is_dma:
                    c += 16
                elif o.flagged:
                    c += 1
                vs.append(c)
            val[s] = vs
        with contextlib.ExitStack() as st:
            sems = {s: st.enter_context(nc.semaphore("s_" + s)) for s in streams}
            block = st.enter_context(nc.Block())
            engobj = {"pe": block.tensor, "act": block.scalar, "dve": block.vector,
                      "pool": block.gpsimd, "sp": block.sync}

            def make(e):
                def body(eng):
                    for o in self.ops[e]:
                        for (s, i) in o.waits:
                            eng.wait_ge(sems[s], val[s][i - 1])
                        if o.fn is None:
                            continue
                        ins = o.fn(eng)
                        if o.is_dma:
                            ins.then_inc(sems[o.stream], 16)
                        elif o.flagged:
                            ins.then_inc(sems[o.stream], 1)
                return body

            for e in self.ENGS:
                if self.ops[e]:
                    engobj[e](make(e))


def _head_perm(hh):
    rest = list(range(32, 128))
    out = []
    for qd in range(4):
        if qd == hh:
            out += list(range(32))
        else:
            out += rest[:32]
            rest = rest[32:]
    return out


def layer_tiles():
    t = []
    for g in range(3):
        t += [(("R", g), 2048)]
        for hh in range(4):
            t += [(("HQK", g, hh), 2048), (("HV", g, hh), 1024)]
    t.append((("PW",), 2048))
    t += [(("PU", j), 2048) for j in range(4)]
    for j in range(4):
        t += [(("YB", j), 2048), (("GB", j), 2048)]
    t += [(("YA", j), 2048) for j in range(2)]
    t += [(("GA", j), 2048) for j in range(4)]
    t += [(("WO", j), 2048) for j in range(4)]
    t += [(("UP", j), 2048) for j in range(22)]
    t += [(("DN", j), 2048) for j in range(11)]
    t.append((("PLE",), 2048))
    t += [(("PG", j), 2048) for j in range(4)]
    return t


def tile_offsets():
    off = {}
    o = 0
    for name, n in layer_tiles():
        off[name] = (o, n)
        o += n
    return off, o


FFN_PARTS = (range(0, 8), range(8, 16), range(16, 22))
POOL_ORDER = (3, 2, 1, 0)


def consume_order():
    seq = []
    for g in GORDER:
        seq += [("R", g)]
        for hh in range(4):
            seq += [("HQK", g, hh), ("HV", g, hh)]
    for hf in range(2):
        seq += [("PW",)] + [("PU", j) for j in POOL_ORDER]
        for j in range(4):
            seq += [("YB", j), ("GB", j)]
        seq += [("YA", 0), ("GA", 0), ("GA", 1), ("YA", 1), ("GA", 2), ("GA", 3)]
        seq += [("WO", j) for j in range(4)]
    for pairs in FFN_PARTS:
        seq += [("UP", pr) for pr in pairs]
        seq += [("DN", j) for j in range(pairs[0] // 2, (pairs[-1] + 1) // 2)]
    seq += [("PLE",)] + [("PG", j) for j in range(4)]
    return seq


def _k1024(W, cols):
    sub = W[:, cols]
    n = sub.shape[1]
    return sub.reshape(8, 128, n).transpose(1, 0, 2).reshape(128, 8 * n)


def build_weight_stream(w_in, w_ya, w_yb, pool_w, w_o, w_up, w_down, w_ple, w_ple_gate):
    off, tot = tile_offsets()
    out = np.zeros((DEPTH, 128, tot), np.float32)
    ar = np.arange
    for l in range(DEPTH):
        def put(name, arr):
            o, n = off[name]
            assert arr.shape == (128, n), (name, arr.shape, n)
            out[l, :, o:o + n] = arr
        for g in range(3):
            r = []
            for base0 in (0, 1536):
                for hh in range(4):
                    b = base0 + g * 512 + hh * 128
                    r += [b + i for i in range(32)]
            put(("R", g), _k1024(w_in[l], r))
            for hh in range(4):
                pm = _head_perm(hh)
                bq = g * 512 + hh * 128
                put(("HQK", g, hh), _k1024(w_in[l], [bq + m for m in pm] + [1536 + bq + m for m in pm]))
                put(("HV", g, hh), _k1024(w_in[l], [3072 + bq + i for i in range(128)]))
        for j in range(4):
            put(("PU", j), _k1024(w_in[l], list(4608 + j * 256 + ar(256))))
            put(("GA", j), _k1024(w_in[l], list(5632 + j * 256 + ar(256))))
            put(("GB", j), _k1024(w_in[l], list(6656 + j * 256 + ar(256))))
            put(("YB", j), _k1024(w_yb[l], list(j * 256 + ar(256))))
            put(("WO", j), _k1024(w_o[l], list(j * 256 + ar(256))))
            put(("PG", j), _k1024(w_ple_gate[l], list(j * 256 + ar(256))))
        put(("PW",), pool_w[l].reshape(4, 2, 128, 256).transpose(2, 0, 1, 3).reshape(128, 2048))
        for j in range(2):
            put(("YA", j), w_ya[l][:, j * 512:(j + 1) * 512].reshape(4, 128, 512).transpose(1, 0, 2).reshape(128, 2048))
        put(("PLE",), w_ple[l].reshape(2, 128, 1024).transpose(1, 0, 2).reshape(128, 2048))
        for pr in range(22):
            put(("UP", pr), _k1024(w_up[l], list(pr * 128 + ar(128)) + list(2816 + pr * 128 + ar(128))))
        for j in range(11):
            blk = w_down[l][j * 256:(j + 1) * 256]
            put(("DN", j), blk.reshape(2, 128, 1024).transpose(1, 0, 2).reshape(128, 2048))
    return out


def build_consts():
    cst = np.zeros((128, NCST), np.float32)
    cst[:, C_ID:C_ID + 128] = np.eye(128, dtype=np.float32)
    for hh in range(4):
        cst[32 * hh, C_SEL + hh * 128:C_SEL + (hh + 1) * 128] = 1.0
    for g, w in enumerate(POOLW):
        t = np.arange(16)
        cst[:, C_INVC + g * 16:C_INVC + (g + 1) * 16] = (1.0 / np.minimum(t + 1, w)).astype(np.float32)[None, :]
    p = np.arange(128)[:, None]
    j = np.arange(128)[None, :]
    cst[:, C_MASK:C_MASK + 128] = (p <= j)
    cst[:, C_MASK + 128:C_MASK + 256] = (p >= j)
    cst[:, C_ODIV:C_ODIV + 128] = 1.0 / 1024.0
    cst[:, C_ONE:C_ONE + 128] = 1.0
    pos = np.arange(S, dtype=np.float32)
    inv_freq = np.exp(np.arange(0, 32, 2, dtype=np.float32) * np.float32(-math.log(500000.0) / 32)).astype(np.float32)
    ang = (pos[:, None] * inv_freq[None, :]).astype(np.float32)
    cos, sin = np.cos(ang).T.astype(np.float32), np.sin(ang).T.astype(np.float32)
    c32 = np.concatenate([cos, cos], 0)
    s32 = np.concatenate([sin, -sin], 0)
    rope = np.zeros((128, 2 * S), np.float32)
    rope[:, :S] = np.tile(c32, (4, 1))
    rope[:, S:] = np.tile(s32, (4, 1))
    return cst, rope


def build_vecs(g_mix, g_ffn, g_ple, pool_scale, conv_w, conv_b):
    v = np.zeros((128, DEPTH * NV), np.float32)
    fm = lambda a: a.reshape(-1, 128).T
    for l in range(DEPTH):
        b = l * NV
        v[:, b + V_GMIX:b + V_GMIX + 8] = fm(g_mix[l])
        v[:, b + V_GFFN:b + V_GFFN + 8] = fm(g_ffn[l])
        v[:, b + V_GPLE:b + V_GPLE + 8] = fm(g_ple[l])
        v[:, b + V_PSC:b + V_PSC + 8] = fm(pool_scale[l])
        cw = conv_w[l].reshape(3, 44, 128).transpose(2, 1, 0).reshape(128, 132)
        v[:, b + V_CW:b + V_CW + 132] = cw
        v[:, b + V_CB:b + V_CB + 44] = fm(conv_b[l])
    return v


A_RSTD, A_ROPE, A_ACCD, A_ACCN, A_X, A_Y = 0, 8192, 16384, 24576, 57344, 77824
ARENA = 88064


def build(nseq=4, depth=DEPTH):
    nc = bass.Bass("TRN2", target_bir_lowering=False)
    toff, TOT = tile_offsets()
    xin = nc.dram_tensor("xin", [nseq, 16, 128, 1024], F32, kind="ExternalInput").ap()
    pin = nc.dram_tensor("pin", [DEPTH, nseq, 16, 128, 256], F32, kind="ExternalInput").ap()
    wt = nc.dram_tensor("wt", [DEPTH, 128, TOT], F32, kind="ExternalInput").ap()
    vecd = nc.dram_tensor("vec", [128, DEPTH * NV], F32, kind="ExternalInput").ap()
    cstd = nc.dram_tensor("cst", [128, NCST], F32, kind="ExternalInput").ap()
    roped = nc.dram_tensor("rope", [128, 2 * S], F32, kind="ExternalInput").ap()
    gfind = nc.dram_tensor("gfin", [128, 1024], F32, kind="ExternalInput").ap()
    outd = nc.dram_tensor("out", [nseq, 16, 128, 1024], F32, kind="ExternalOutput").ap()
    wsc = nc.dram_tensor("wsc", [DEPTH, 128, TOT], BF16).ap()

    with contextlib.ExitStack() as st:
        sb = lambda name, shape, dt: st.enter_context(nc.sbuf_tensor(name, shape, dt))
        xT = sb("xT", [128, 8, S], F32)
        hT = sb("hT", [128, 8, S], BF16)
        ring = [sb(f"ring{i}", [128, 2048], BF16) for i in range(NSLOT)]
        vecs = sb("vecs", [128, DEPTH * NV], F32)
        cstf = sb("cstf", [128, NCST], F32)
        cstb = sb("cstb", [128, 640], BF16)
        epsT = sb("epsT", [128, 2], F32)
        arena = sb("arena", [128, ARENA // 2], BF16)
        psT = [st.enter_context(nc.psum_tensor(f"psT{i}", [128, 2048], F32)) for i in range(2)]

        def ab(off, n):
            return arena[:, off // 2: off // 2 + n]

        def af(off, n):
            return arena[:, off // 2: off // 2 + 2 * n].bitcast(F32)

        ident32 = cstf[:, C_ID:C_ID + 128]
        sel = cstf[:, C_SEL:C_SEL + 512].rearrange("p (h m) -> p h m", h=4)
        invc = cstf[:, C_INVC:C_INVC + 64].rearrange("p (g t) -> p g t", g=4)
        identb = cstb[:, 0:128]
        mask2 = cstb[:, 128:384]
        odivb = cstb[:, 384:512]
        onesb = cstb[:, 512:640]

        P = Prog(nc)
        bank_ctr = [0]

        def nb():
            b = bank_ctr[0] % 8
            bank_ctr[0] += 1
            return b

        def bank(b):
            return psT[b // 4][:, (b % 4) * 512:(b % 4 + 1) * 512]

        def bk(b):
            return ("bk", b)

        def barrier():
            P.barrier(lambda e: e.memset(epsT[:, 1:2], 0.0))

        P.op("sp", lambda e: e.dma_start(out=vecs[:], in_=vecd), writes=["vecs"], dma="d_c0")
        P.op("sp", lambda e: e.dma_start(out=cstf[:], in_=cstd), writes=["cstf"], dma="d_c1")
        P.op("dve", lambda e: e.memset(epsT[:, 0:1], 1e-6), writes=["eps"])
        P.op("dve", lambda e: e.tensor_copy(cstb[:, 0:128], cstf[:, C_ID:C_ID + 128]), reads=["cstf"], writes=["cstb"])
        P.op("dve", lambda e: e.tensor_copy(cstb[:, 128:640], cstf[:, C_MASK:C_MASK + 512]), reads=["cstf"], writes=["cstb"])

        corder = consume_order()
        gseq = [(l, nm) for _ in range(nseq) for l in range(depth) for nm in corder]
        wst = {"next": 0, "free": list(range(NSLOT)), "slot": {}, "pos": 0, "saved": set()}

        def w_issue():
            while wst["free"] and wst["next"] < len(gseq):
                idx = wst["next"]
                l, nm = gseq[idx]
                slot = wst["free"].pop(0)
                o, n = toff[nm]
                if (l, nm) in wst["saved"]:
                    P.op("sp", (lambda e, slot=slot, l=l, o=o, n=n: e.dma_start(out=ring[slot][:, 0:n], in_=wsc[l, :, o:o + n])),
                         reads=[("wsc", l, nm)], writes=[("w", slot)], dma=f"d_w{slot}")
                else:
                    P.op("pool", (lambda e, slot=slot, l=l, o=o, n=n: e.dma_start(out=ring[slot][:, 0:n], in_=wt[l, :, o:o + n])),
                         writes=[("w", slot)], dma=f"d_w{slot}")
                    P.op("sp", (lambda e, slot=slot, l=l, o=o, n=n: e.dma_start(out=wsc[l, :, o:o + n], in_=ring[slot][:, 0:n])),
                         reads=[("w", slot)], writes=[("wsc", l, nm)], dma="d_ws")
                    wst["saved"].add((l, nm))
                wst["slot"][idx] = slot
                wst["next"] += 1

        def w_get(l, nm):
            idx = wst["pos"]
            assert gseq[idx] == (l, nm), (gseq[idx], l, nm)
            w_issue()
            assert idx in wst["slot"], ("weight ring too small at", nm)
            wst["pos"] += 1
            slot = wst["slot"][idx]
            return slot

        def w_free(slot):
            wst["free"].append(slot)
            w_issue()

        def wv(slot, k, c):
            return ring[slot][:, 0:k * c].rearrange("p (k c) -> p k c", k=k)

        def wk(slot):
            return ("w", slot)

        def proj(slot, view, chunk, tt, extra_reads=()):
            b = nb()
            for k in range(8):
                P.op("pe", (lambda e, b=b, k=k: e.matmul(bank(b), view[:, k, chunk * 128:(chunk + 1) * 128],
                                                         hT[:, k, tt * 512:(tt + 1) * 512], start=(k == 0), stop=(k == 7))),
                     reads=[wk(slot), ("hT", tt)], writes=[bk(b)])
            return b

        rstd = af(A_RSTD, 2048)

        sqh = [ab(A_Y, 2048).rearrange("p (k t) -> p k t", k=8), ab(A_Y + 4096, 2048).rearrange("p (k t) -> p k t", k=8)]
        RSTD_KEYS = [("rstd", j) for j in range(8)]

        def norm_stats(js=range(8), xkey=None):
            for j in js:
                sqb = sqh[j % 2]
                xs = xT[:, :, j * 256:(j + 1) * 256]
                xk = ["xT"] if xkey is None else [xkey]
                if j % 2 == 0:
                    P.op("act", (lambda e, sqb=sqb, xs=xs: e.activation(sqb, xs, AF.Square)), reads=xk, writes=[("sq", j % 2)])
                else:
                    P.op("dve", (lambda e, sqb=sqb, xs=xs: e.tensor_tensor(sqb, xs, xs, ALU.mult)), reads=xk, writes=[("sq", j % 2)])
                b = nb()
                for k in range(8):
                    P.op("pe", (lambda e, b=b, k=k, sqb=sqb: e.matmul(bank(b)[:, 0:256], odivb, sqb[:, k, :], start=(k == 0), stop=(k == 7))),
                         reads=[("sq", j % 2), "cstb"], writes=[bk(b)])
                rs_ = rstd[:, j * 256:(j + 1) * 256]
                P.op("act", (lambda e, b=b, rs_=rs_: e.activation(rs_, bank(b)[:, 0:256], AF.Ln, bias=epsT[:, 0:1], scale=1.0)),
                     reads=[bk(b), "eps"], writes=[("rstd", j)])
                P.op("act", (lambda e, rs_=rs_: e.activation(rs_, rs_, AF.Exp, scale=-0.5)), reads=[("rstd", j)], writes=[("rstd", j)])

        def pview(ap2, d):
            return ap2 if d == 1 else ap2.rearrange("p (j r) -> p r j", r=d)

        def cview(ap2, d):
            return ap2 if d == 1 else ap2.rearrange("p (r j) -> p r j", r=d)

        ALL_UNITS = [(tt, k) for tt in range(4) for k in range(8)]

        def normalize(gv, d, units=None):
            for tt, k in (ALL_UNITS if units is None else units):
                rk = RSTD_KEYS if d != 1 else [("rstd", 2 * tt), ("rstd", 2 * tt + 1)]
                P.op("dve", (lambda e, k=k, tt=tt: e.scalar_tensor_tensor(span_c(hT[:, k, tt * 512:(tt + 1) * 512], d), span_p(xT[:, k, :], d, tt),
                                                                         gv[:, k:k + 1], span_p(rstd, d, tt), ALU.mult, ALU.mult)),
                     reads=["xT", "vecs"] + rk, writes=[("hT", tt)], loose=[("hT", tt)])

        def span_p(ap2, d, m):
            if d == 1:
                return ap2[:, m * 512:(m + 1) * 512]
            if d == 4:
                return ap2.rearrange("p (j r) -> p r j", r=4)[:, m, :]
            return ap2.rearrange("p (j r) -> p r j", r=16)[:, 4 * m:4 * m + 4, :]

        def span_c(ap2, d):
            return ap2.rearrange("p (a b) -> p a b", a=4) if d == 16 else ap2

        def phase_attn(l):
            gv = vecs[:, l * NV + V_GMIX: l * NV + V_GMIX + 8]
            ropeC = ab(A_ROPE, 2048)
            ropeS = ab(A_ROPE + 4096, 2048)
            accD = af(A_ACCD, 2048)
            accN = af(A_ACCN, 8192).rearrange("p (h t) -> p h t", h=4)
            qk = [ab(A_X, 2048), ab(A_X + 4096, 2048)]
            Vh = ab(A_X + 8192, 2048).rearrange("p (b d) -> p b d", b=16)
            qkrot = [ab(A_X + 12288, 2048), ab(A_X + 16384, 2048)]
            attnT = ab(A_X, 8192).rearrange("p (h t) -> p h t", h=4)
            NPT = 12
            PT = [ab(A_Y + i * 512, 256) for i in range(NPT)]
            t1 = af(A_Y + 6144, 512)
            t2 = af(A_Y + 8192, 512)
            scale = 1.0 / math.sqrt(128.0)
            t3 = af(A_X + 8192, 512)
            SWAP16 = list(range(16, 32)) + list(range(0, 16))

            barrier()
            P.op("pool", lambda e: e.dma_start(out=ab(A_ROPE, 4096), in_=roped), writes=["rope"], dma="d_rope")
            norm_stats()
            barrier()
            pre_norm = [False]
            for gi, g in enumerate(GORDER):
                d = GROUPS[g][1]
                nbk = (S // d) // 128
                if not pre_norm[0]:
                    normalize(gv, d)
                pre_norm[0] = False
                rs = w_get(l, ("R", g))
                Rv = wv(rs, 8, 256)
                for which in range(2):
                    for tt in range(4):
                        b1 = proj(rs, Rv, which, tt)
                        P.op("dve", (lambda e, b1=b1, tt=tt, d=d: e.tensor_tensor(span_c(t1, d), span_c(bank(b1), d), span_p(ropeC, d, tt), ALU.mult)),
                             reads=[bk(b1), "rope"], writes=["t1"])
                        P.op("dve", (lambda e, b1=b1, tt=tt, d=d: e.tensor_tensor(span_c(t2, d), span_c(bank(b1), d), span_p(ropeS, d, tt), ALU.mult)),
                             reads=[bk(b1), "rope"], writes=["t2"])
                        P.op("dve", (lambda e: e.stream_shuffle(t3, t2, SWAP16)), reads=["t2"], writes=["Vh"])
                        P.op("dve", (lambda e, which=which, tt=tt: e.tensor_tensor(qkrot[which][:, tt * 512:(tt + 1) * 512], t1, t3, ALU.add)),
                             reads=["t1", "Vh"], writes=[("qkrot", which)], loose=[("qkrot", which)])
                w_free(rs)
                for hh in range(4):
                    hs = w_get(l, ("HQK", g, hh))
                    Hv = wv(hs, 8, 256)
                    for which in range(2):
                        for tt in range(4):
                            b = proj(hs, Hv, which, tt)
                            P.op("act", (lambda e, b=b, which=which, tt=tt: e.copy(qk[which][:, tt * 512:(tt + 1) * 512], bank(b))),
                                 reads=[bk(b)], writes=[("qk", which)], loose=[("qk", which)])
                        P.op("dve", (lambda e, which=which, hh=hh: e.tensor_copy(qk[which][32 * hh:32 * hh + 32, :],
                                                                                 qkrot[which][32 * hh:32 * hh + 32, :])),
                             reads=[("qkrot", which)], writes=[("qk", which)])
                    w_free(hs)
                    vs = w_get(l, ("HV", g, hh))
                    Vv = wv(vs, 8, 128)
                    for blk in range(16):
                        b = nb()
                        for k in range(8):
                            P.op("pe", (lambda e, b=b, k=k, blk=blk, Vv=Vv: e.matmul(bank(b)[:, 0:128], hT[:, k, blk * 128:(blk + 1) * 128],
                                                                              Vv[:, k, :], start=(k == 0), stop=(k == 7))),
                                 reads=[wk(vs), ("hT", blk // 4)], writes=[bk(b)])
                        P.op("act", (lambda e, b=b, blk=blk: e.copy(Vh[:, blk, :], bank(b)[:, 0:128])),
                             reads=[bk(b)], writes=["Vh"], loose=["Vh"])
                    w_free(vs)
                    nxt_d = GROUPS[GORDER[gi + 1]][1] if (hh == 3 and gi < 2) else None
                    def S_(m, nbk=nbk):
                        for B in range(4 * m, 4 * m + 4):
                            ncol = 256 if (B % nbk) < nbk - 1 else 128
                            sbk = nb()
                            P.op("pe", (lambda e, sbk=sbk, B=B, ncol=ncol: e.matmul(bank(sbk)[:, 0:ncol], qk[1][:, B * 128:(B + 1) * 128],
                                                                                    qk[0][:, B * 128:B * 128 + ncol], start=True, stop=True)),
                                 reads=[("qk", 0), ("qk", 1)], writes=[bk(sbk)])
                            P.op("act", (lambda e, sbk=sbk, B=B, ncol=ncol: e.activation(PT[B % NPT][:, 0:ncol], bank(sbk)[:, 0:ncol], AF.Exp, scale=scale)),
                                 reads=[bk(sbk)], writes=[("PT", B % NPT)])
                            P.op("dve", (lambda e, B=B, ncol=ncol: e.tensor_tensor(PT[B % NPT][:, 0:ncol], PT[B % NPT][:, 0:ncol], mask2[:, 0:ncol], ALU.mult)),
                                 reads=[("PT", B % NPT), "cstb"], writes=[("PT", B % NPT)])

                    def V_(m, nbk=nbk, gi=gi, d=d, hh=hh):
                        nbn, nbd = nb(), nb()
                        for Bq in range(4 * m, 4 * m + 4):
                            srcs = []
                            if Bq % nbk > 0:
                                srcs.append((Bq - 1, 128))
                            srcs.append((Bq, 0))
                            for tgt, isN in ((nbn, True), (nbd, False)):
                                for i, (Bk, co) in enumerate(srcs):
                                    lhs = Vh[:, Bk, :] if isN else onesb
                                    P.op("pe", (lambda e, tgt=tgt, lhs=lhs, Bk=Bk, co=co, Bq=Bq, i=i, n=len(srcs): e.matmul(
                                        bank(tgt)[:, (Bq % 4) * 128:(Bq % 4 + 1) * 128], lhs, PT[Bk % NPT][:, co:co + 128],
                                        start=(i == 0), stop=(i == n - 1))),
                                         reads=[("PT", Bk % NPT), "Vh", "cstb"], writes=[bk(tgt)])
                        dN = span_p(accN[:, hh, :], d, m)
                        dD = span_p(accD[32 * hh:32 * hh + 32, :], d, m)
                        sN = span_c(bank(nbn), d)
                        sD = span_c(bank(nbd)[32 * hh:32 * hh + 32, :], d)
                        if gi == 0:
                            P.op("act", (lambda e, dN=dN, sN=sN: e.copy(dN, sN)), reads=[bk(nbn)], writes=["accN"], loose=["accN"])
                            P.op("act", (lambda e, dD=dD, sD=sD: e.copy(dD, sD)), reads=[bk(nbd)], writes=["accD"], loose=["accD"])
                        else:
                            P.op("dve", (lambda e, dN=dN, sN=sN: e.tensor_tensor(dN, sN, dN, ALU.add)), reads=[bk(nbn), "accN"], writes=["accN"], loose=["accN"])
                            P.op("dve", (lambda e, dD=dD, sD=sD: e.tensor_tensor(dD, sD, dD, ALU.add)), reads=[bk(nbd), "accD"], writes=["accD"], loose=["accD"])

                    kq = list(ALL_UNITS)

                    def nrm(n):
                        if nxt_d is not None:
                            for _ in range(4 * n):
                                if kq:
                                    normalize(gv, nxt_d, units=[kq.pop(0)])

                    S_(0)
                    for m in range(4):
                        if m + 1 < 4:
                            S_(m + 1)
                        nrm(1)
                        V_(m)
                        nrm(1)
                    if nxt_d is not None:
                        pre_norm[0] = True
            barrier()
            for tt in range(4):
                P.op("act", (lambda e, tt=tt: e.activation(accD[:, tt * 512:(tt + 1) * 512], accD[:, tt * 512:(tt + 1) * 512], AF.Ln)),
                     reads=["accD"], writes=[("racc", tt)])
                P.op("act", (lambda e, tt=tt: e.activation(accD[:, tt * 512:(tt + 1) * 512], accD[:, tt * 512:(tt + 1) * 512], AF.Exp, scale=-1.0)),
                     reads=[("racc", tt)], writes=[("racc", tt)])
                for hh in range(4):
                    b = nb()
                    P.op("pe", (lambda e, b=b, hh=hh, tt=tt: e.matmul(bank(b), sel[:, hh, :], accD[:, tt * 512:(tt + 1) * 512], start=True, stop=True)),
                         reads=[("racc", tt), "cstf"], writes=[bk(b)])
                    P.op("dve", (lambda e, b=b, hh=hh, tt=tt: e.tensor_tensor(attnT[:, hh, tt * 512:(tt + 1) * 512], bank(b), accN[:, hh, tt * 512:(tt + 1) * 512], ALU.mult)),
                         reads=[bk(b), "accN"], writes=["attnT"], loose=["attnT"])
            return attnT

        def phase_merge(l, attnT):
            gv = vecs[:, l * NV + V_GMIX: l * NV + V_GMIX + 8]
            psc = vecs[:, l * NV + V_PSC: l * NV + V_PSC + 8]
            mixed = ab(8192, 8192).rearrange("p (k t) -> p k t", k=8)
            merged = ab(24576, 8192).rearrange("p (k t) -> p k t", k=8)
            UW = 528
            ub = [[af(40960 + (st_ * 3 + i) * 2112, UW) for i in range(3)] for st_ in range(2)]
            usetc = [0]
            pooled2 = [ab(73728, 2048).rearrange("p (k t) -> p k t", k=2), ab(A_Y + 6144, 2048).rearrange("p (k t) -> p k t", k=2)]
            sg = [af(A_Y, 512), af(A_Y + 2048, 512)]
            m1 = af(A_Y + 4096, 512)
            barrier()
            sgc = [0]
            for hf in range(2):
                T0 = hf * 1024
                pws = w_get(l, ("PW",))
                PWv = ring[pws][:, 0:2048].rearrange("p (g k c) -> p g k c", g=4, k=2)
                pending = []
                for pi_, pj in enumerate(POOL_ORDER):
                    pus = w_get(l, ("PU", pj))
                    PUv = wv(pus, 8, 256)
                    pooled = pooled2[pi_ % 2]
                    pkey = ("pooled", pi_ % 2)
                    for c4 in range(2):
                        c = pj * 2 + c4
                        grp, c2 = c // 2, c % 2
                        w = POOLW[grp]
                        prevU = None
                        for t2 in range(2):
                            tt = 2 * hf + t2
                            st_ = usetc[0] % 2
                            usetc[0] += 1
                            ubs = ub[st_]
                            uk = lambda i, st_=st_: ("ub", st_, i)
                            U = ubs[0]
                            b = proj(pus, PUv, c4, tt)
                            P.op("act", (lambda e, b=b, U=U: e.copy(U[:, 16:UW], bank(b))), reads=[bk(b)], writes=[uk(0)])
                            if tt == 0:
                                P.op("dve", (lambda e, U=U: e.memset(U[:, 0:16], 0.0)), writes=[uk(0)])
                            elif t2 == 1:
                                P.op("act", (lambda e, U=U, prevU=prevU: e.copy(U[:, 0:16], prevU[:, UW - 16:UW])),
                                     reads=[("ub", 1 - st_, 0)], writes=[uk(0)])
                            else:
                                b = nb()
                                T0 = tt * 512
                                for k in range(8):
                                    P.op("pe", (lambda e, b=b, k=k, c4=c4, PUv=PUv, T0=T0: e.matmul(bank(b)[:, 0:16], PUv[:, k, c4 * 128:(c4 + 1) * 128],
                                                                                                 hT[:, k, T0 - 16:T0], start=(k == 0), stop=(k == 7))),
                                         reads=[wk(pus), ("hT", (T0 - 16) // 512)], writes=[bk(b)])
                                P.op("act", (lambda e, b=b, U=U: e.copy(U[:, 0:16], bank(b)[:, 0:16])), reads=[bk(b)], writes=[uk(0)])
                            cur, ci = U, 0
                            step = 1
                            while step < w:
                                ni = 1 if ci != 1 else 2
                                nxt = ubs[ni]
                                P.op("dve", (lambda e, cur=cur, nxt=nxt, step=step: e.tensor_tensor(nxt[:, step:UW], cur[:, step:UW], cur[:, 0:UW - step], ALU.add)),
                                     reads=[uk(ci)], writes=[uk(ni)])
                                cur, ci = nxt, ni
                                step *= 2
                            P.op("dve", (lambda e, cur=cur, U=U, c2=c2, w=w, t2=t2, pooled=pooled: e.scalar_tensor_tensor(pooled[:, c2, t2 * 512:(t2 + 1) * 512], cur[:, 16:UW], 1.0 / w,
                                                                                                            U[:, 16:UW], ALU.mult, ALU.subtract)),
                                 reads=[uk(ci), uk(0)], writes=[pkey])
                            if tt == 0:
                                P.op("dve", (lambda e, cur=cur, grp=grp: e.tensor_tensor(m1[:, 0:16], cur[:, 16:32], invc[:, grp, :], ALU.mult)),
                                     reads=[uk(ci), "cstf"], writes=["m1"])
                                P.op("dve", (lambda e, U=U, c2=c2, pooled=pooled: e.tensor_tensor(pooled[:, c2, 0:16], m1[:, 0:16], U[:, 16:32], ALU.subtract)),
                                     reads=["m1", uk(0)], writes=[pkey])
                            prevU = U
                        if c2 == 1:
                            def pw_stage(grp=grp, pooled=pooled, pkey=pkey):
                                for oc in range(2):
                                    for t2 in range(2):
                                        b = nb()
                                        for k2 in range(2):
                                            P.op("pe", (lambda e, b=b, k2=k2, oc=oc, t2=t2, grp=grp, PWv=PWv, pooled=pooled: e.matmul(
                                                bank(b), PWv[:, grp, k2, oc * 128:(oc + 1) * 128], pooled[:, k2, t2 * 512:(t2 + 1) * 512],
                                                start=(k2 == 0), stop=(k2 == 1))),
                                                 reads=[wk(pws), pkey], writes=[bk(b)])
                                        cc = grp * 2 + oc
                                        P.op("act", (lambda e, b=b, cc=cc, t2=t2: e.activation(mixed[:, cc, t2 * 512:(t2 + 1) * 512], bank(b), AF.Identity,
                                                                                              scale=psc[:, cc:cc + 1])),
                                             reads=[bk(b), "vecs"], writes=["mixed"], loose=["mixed"])
                            for f_ in pending:
                                f_()
                            pending = [pw_stage]
                    w_free(pus)
                for f_ in pending:
                    f_()
                w_free(pws)
                for j in range(4):
                    ys = w_get(l, ("YB", j))
                    gs = w_get(l, ("GB", j))
                    Yv_, Gv_ = wv(ys, 8, 256), wv(gs, 8, 256)
                    for c4 in range(2):
                        c = 2 * j + c4
                        for t2 in range(2):
                            b1 = nb()
                            for k in range(8):
                                P.op("pe", (lambda e, b1=b1, k=k, c4=c4, t2=t2, Yv_=Yv_: e.matmul(bank(b1), Yv_[:, k, c4 * 128:(c4 + 1) * 128],
                                                                                        mixed[:, k, t2 * 512:(t2 + 1) * 512], start=(k == 0), stop=(k == 7))),
                                     reads=[wk(ys), "mixed"], writes=[bk(b1)])
                            b2 = proj(gs, Gv_, c4, 2 * hf + t2)
                            sgi = sgc[0] % 2
                            sgc[0] += 1
                            P.op("act", (lambda e, b2=b2, sgi=sgi: e.activation(sg[sgi], bank(b2), AF.Sigmoid)), reads=[bk(b2)], writes=[("sg", sgi)])
                            P.op("dve", (lambda e, b1=b1, sgi=sgi, c=c, t2=t2: e.tensor_tensor(merged[:, c, t2 * 512:(t2 + 1) * 512], bank(b1), sg[sgi], ALU.mult)),
                                 reads=[bk(b1), ("sg", sgi)], writes=["merged"], loose=["merged"])
                    w_free(ys)
                    w_free(gs)
                for jj in range(2):
                    yas = w_get(l, ("YA", jj))
                    YAv = wv(yas, 4, 512)
                    for j2 in range(2):
                        j = 2 * jj + j2
                        gs = w_get(l, ("GA", j))
                        Gv_ = wv(gs, 8, 256)
                        for c4 in range(2):
                            c = 2 * j + c4
                            cl = c - 4 * jj
                            for t2 in range(2):
                                tt = 2 * hf + t2
                                b1 = nb()
                                for k in range(4):
                                    P.op("pe", (lambda e, b1=b1, k=k, cl=cl, tt=tt, YAv=YAv: e.matmul(bank(b1), YAv[:, k, cl * 128:(cl + 1) * 128],
                                                                                                    attnT[:, k, tt * 512:(tt + 1) * 512], start=(k == 0), stop=(k == 3))),
                                         reads=[wk(yas), "attnT"], writes=[bk(b1)])
                                b2 = proj(gs, Gv_, c4, tt)
                                sgi = sgc[0] % 2
                                sgc[0] += 1
                                P.op("act", (lambda e, b2=b2, sgi=sgi: e.activation(sg[sgi], bank(b2), AF.Sigmoid)), reads=[bk(b2)], writes=[("sg", sgi)])
                                P.op("dve", (lambda e, b1=b1, sgi=sgi: e.tensor_tensor(m1, bank(b1), sg[sgi], ALU.mult)),
                                     reads=[bk(b1), ("sg", sgi)], writes=["m1"])
                                P.op("dve", (lambda e, c=c, t2=t2: e.tensor_tensor(merged[:, c, t2 * 512:(t2 + 1) * 512], m1, merged[:, c, t2 * 512:(t2 + 1) * 512], ALU.add)),
                                     reads=["m1", "merged"], writes=["merged"], loose=["merged"])
                        w_free(gs)
                    w_free(yas)
                for j in range(4):
                    ws_ = w_get(l, ("WO", j))
                    Wv_ = wv(ws_, 8, 256)
                    for c4 in range(2):
                        c = 2 * j + c4
                        for t2 in range(2):
                            tt = 2 * hf + t2
                            b = nb()
                            for k in range(8):
                                P.op("pe", (lambda e, b=b, k=k, c4=c4, t2=t2, Wv_=Wv_: e.matmul(bank(b), Wv_[:, k, c4 * 128:(c4 + 1) * 128],
                                                                                      merged[:, k, t2 * 512:(t2 + 1) * 512], start=(k == 0), stop=(k == 7))),
                                     reads=[wk(ws_), "merged"], writes=[bk(b)])
                            P.op("dve", (lambda e, b=b, c=c, tt=tt: e.tensor_tensor(xT[:, c, tt * 512:(tt + 1) * 512], bank(b), xT[:, c, tt * 512:(tt + 1) * 512], ALU.add)),
                                 reads=[bk(b), "xT"], writes=["xT"], loose=["xT"])
                    w_free(ws_)

        def phase_ffn(l, s):
            gv = vecs[:, l * NV + V_GFFN: l * NV + V_GFFN + 8]
            cw = vecs[:, l * NV + V_CW: l * NV + V_CW + 132].rearrange("p (c t) -> p c t", t=3)
            cb = vecs[:, l * NV + V_CB: l * NV + V_CB + 44]
            actT = ab(8192, 16384).rearrange("p (k t) -> p k t", k=8)
            Yb = [[af(40960, 2048), af(49152, 2048)], [af(57344, 2048), af(65536, 2048)]]
            barrier()
            norm_stats()
            normalize(gv, 1)
            gctr = [0]
            for pi, pairs in enumerate(FFN_PARTS):
                for pr in pairs:
                    ups = w_get(l, ("UP", pr))
                    UPv = wv(ups, 8, 256)
                    Ys = Yb[pr % 2]
                    for role in range(2):
                        chunk = role
                        cidx = pr + 22 * role
                        G = gctr[0] % 2
                        gctr[0] += 1
                        for tt in range(4):
                            for k in range(8):
                                P.op("pe", (lambda e, G=G, k=k, tt=tt, chunk=chunk, UPv=UPv: e.matmul(psT[G][:, tt * 512:(tt + 1) * 512],
                                                                                                    UPv[:, k, chunk * 128:(chunk + 1) * 128],
                                                                                                    hT[:, k, tt * 512:(tt + 1) * 512], start=(k == 0), stop=(k == 7))),
                                     reads=[wk(ups), ("hT", tt)], writes=[bk(4 * G + tt)])
                        Y = Ys[role]
                        gb = [bk(4 * G + i) for i in range(4)]
                        yk = ("Y", pr % 2, role)
                        P.op("act", (lambda e, Y=Y, G=G, cidx=cidx: e.activation(Y, psT[G][:, :], AF.Identity, bias=cb[:, cidx:cidx + 1],
                                                                                  scale=cw[:, cidx, 2:3])),
                             reads=gb + ["vecs"], writes=[yk])
                        P.op("dve", (lambda e, Y=Y, G=G, cidx=cidx: e.scalar_tensor_tensor(Y[:, 1:S], psT[G][:, 0:S - 1], cw[:, cidx, 1:2], Y[:, 1:S],
                                                                                            ALU.mult, ALU.add)),
                             reads=gb + ["vecs", yk], writes=[yk])
                        P.op("dve", (lambda e, Y=Y, G=G, cidx=cidx: e.scalar_tensor_tensor(Y[:, 2:S], psT[G][:, 0:S - 2], cw[:, cidx, 0:1], Y[:, 2:S],
                                                                                            ALU.mult, ALU.add)),
                             reads=gb + ["vecs", yk], writes=[yk])
                    P.op("act", (lambda e, Ys=Ys: e.activation(Ys[0], Ys[0], AF.Silu)), reads=[("Y", pr % 2, 0)], writes=[("Y", pr % 2, 0)])
                    P.op("dve", (lambda e, Ys=Ys, pr=pr, p0=pairs[0]: e.tensor_tensor(actT[:, pr - p0, :], Ys[0], Ys[1], ALU.mult)),
                         reads=[("Y", pr % 2, 0), ("Y", pr % 2, 1)], writes=["actT"], loose=["actT"])
                    w_free(ups)
                dns = [w_get(l, ("DN", j)) for j in range(pairs[0] // 2, (pairs[-1] + 1) // 2)]
                dv = [wv(dn_, 2, 1024) for dn_ in dns]
                dkeys = [wk(dn_) for dn_ in dns]
                nk = len(pairs)
                last = (pi == len(FFN_PARTS) - 1)
                if last:
                    P.op("pool", lambda e: e.dma_start(out=ab(57344, 4096).rearrange("p (t f) -> p t f", t=16), in_=pin[l, s].rearrange("t p f -> p t f")),
                         writes=["ptok", ("Y", 1, 0), ("Y", 1, 1)], dma="d_p")
                gpl = vecs[:, l * NV + V_GPLE: l * NV + V_GPLE + 8]

                def pre_ple(tt):
                    norm_stats(js=(2 * tt, 2 * tt + 1), xkey=("xTt", tt))
                    normalize(gpl, 1, units=[(tt, k) for k in range(8)])

                order = [(c, tt) for tt in range(4) for c in range(8)] if last else [(c, tt) for c in range(8) for tt in range(4)]
                for c, tt in order:
                    if last and c == 0 and tt >= 2:
                        pre_ple(tt - 2)
                    b = nb()
                    for kk in range(nk):
                        P.op("pe", (lambda e, b=b, kk=kk, c=c, tt=tt, nk=nk, dv=dv: e.matmul(bank(b), dv[kk // 2][:, kk % 2, c * 128:(c + 1) * 128],
                                                                                      actT[:, kk, tt * 512:(tt + 1) * 512], start=(kk == 0), stop=(kk == nk - 1))),
                             reads=dkeys + ["actT"], writes=[bk(b)])
                    P.op("dve", (lambda e, b=b, c=c, tt=tt: e.tensor_tensor(xT[:, c, tt * 512:(tt + 1) * 512], bank(b), xT[:, c, tt * 512:(tt + 1) * 512], ALU.add)),
                         reads=[bk(b), "xT"], writes=["xT", ("xTt", tt)], loose=["xT", ("xTt", tt)])
                if last:
                    pre_ple(2)
                    pre_ple(3)
                for dn_ in dns:
                    w_free(dn_)

        def phase_ple(l, s):
            gv = vecs[:, l * NV + V_GPLE: l * NV + V_GPLE + 8]
            ptok = ab(57344, 4096).rearrange("p (t f) -> p t f", t=16)
            pT = ab(16384, 4096).rearrange("p (k t) -> p k t", k=2)
            sg = [af(24576, 512), af(26624, 512)]
            m1 = [af(28672, 512), af(30720, 512)]
            barrier()
            for k2 in range(2):
                for g4 in range(4):
                    b = nb()
                    bb = bank(b).bitcast(BF16)
                    for i in range(4):
                        tile = g4 * 4 + i
                        P.op("pe", (lambda e, bb=bb, i=i, tile=tile, k2=k2: e.transpose(bb[:, i * 128:(i + 1) * 128], ptok[:, tile, k2 * 128:(k2 + 1) * 128], identb)),
                             reads=["ptok", "cstb"], writes=[bk(b)])
                    P.op("act", (lambda e, bb=bb, k2=k2, g4=g4: e.copy(pT[:, k2, g4 * 512:(g4 + 1) * 512], bb[:, 0:512])), reads=[bk(b)], writes=["pT"], loose=["pT"])
            ps_ = w_get(l, ("PLE",))
            PLv = wv(ps_, 2, 1024)
            ctr = 0
            for j in range(4):
                gs = w_get(l, ("PG", j))
                Gv_ = wv(gs, 8, 256)
                for c4 in range(2):
                    c = 2 * j + c4
                    for tt in range(4):
                        b1 = nb()
                        for k2 in range(2):
                            P.op("pe", (lambda e, b1=b1, k2=k2, c=c, tt=tt: e.matmul(bank(b1), PLv[:, k2, c * 128:(c + 1) * 128],
                                                                                    pT[:, k2, tt * 512:(tt + 1) * 512], start=(k2 == 0), stop=(k2 == 1))),
                                 reads=[wk(ps_), "pT"], writes=[bk(b1)])
                        b2 = proj(gs, Gv_, c4, tt)
                        i = ctr % 2
                        ctr += 1
                        P.op("act", (lambda e, b2=b2, i=i: e.activation(sg[i], bank(b2), AF.Sigmoid)), reads=[bk(b2)], writes=[("sg", i)])
                        P.op("dve", (lambda e, b1=b1, i=i: e.tensor_tensor(m1[i], bank(b1), sg[i], ALU.mult)), reads=[bk(b1), ("sg", i)], writes=[("m1", i)])
                        P.op("dve", (lambda e, i=i, c=c, tt=tt: e.tensor_tensor(xT[:, c, tt * 512:(tt + 1) * 512], m1[i], xT[:, c, tt * 512:(tt + 1) * 512], ALU.add)),
                             reads=[("m1", i), "xT"], writes=["xT"], loose=["xT"])
                w_free(gs)
            w_free(ps_)

        def load_x(s):
            stage = [af(8192 + 4096 * i, 1024) for i in range(4)]
            barrier()
            for tile in range(16):
                sgt = stage[tile % 4]
                P.op("sp", (lambda e, sgt=sgt, tile=tile: e.dma_start(out=sgt, in_=xin[s, tile])), writes=[("stg", tile % 4)], dma=f"d_x{tile % 4}")
                for hb in range(2):
                    b = nb()
                    for i in range(4):
                        k = hb * 4 + i
                        P.op("pe", (lambda e, b=b, i=i, k=k, sgt=sgt: e.transpose(bank(b)[:, i * 128:(i + 1) * 128], sgt[:, k * 128:(k + 1) * 128], ident32)),
                             reads=[("stg", tile % 4), "cstf"], writes=[bk(b)])
                    eng = "act" if hb == 0 else "dve"
                    dst = xT[:, hb * 4:hb * 4 + 4, tile * 128:(tile + 1) * 128]
                    src = bank(b).rearrange("p (a t) -> p a t", a=4)
                    if eng == "act":
                        P.op("act", (lambda e, dst=dst, src=src: e.copy(dst, src)), reads=[bk(b)], writes=["xT"], loose=["xT"])
                    else:
                        P.op("dve", (lambda e, dst=dst, src=src: e.tensor_copy(dst, src)), reads=[bk(b)], writes=["xT"], loose=["xT"])

        def final(s):
            gfin = af(8192, 1024)
            ost = [af(12288, 1024), af(16384, 1024)]
            junk = af(20480, 1024)
            ss = af(24576, 8)
            barrier()
            P.op("sp", lambda e: e.dma_start(out=gfin, in_=gfind), writes=["gfin"], dma="d_c0")
            for tile in range(16):
                G = tile % 2
                half = (tile // 2) % 2
                pz = psT[G][:, half * 1024:(half + 1) * 1024]
                keys = [bk(4 * G + 2 * half), bk(4 * G + 2 * half + 1)]
                for k in range(8):
                    P.op("pe", (lambda e, pz=pz, k=k, tile=tile: e.transpose(pz[:, k * 128:(k + 1) * 128], xT[:, k, tile * 128:(tile + 1) * 128], ident32)),
                         reads=["xT", "cstf"], writes=keys)
                si = tile % 4
                P.op("dve", (lambda e, si=si: e.memset(ss[:, si:si + 1], 0.0)), writes=[("ss", si)])
                P.op("act", (lambda e, pz=pz, si=si: e.activation(junk, pz, AF.Square, accum_out=ss[:, si:si + 1])), reads=keys + [("ss", si)],
                     writes=["junk", ("ss", si)])
                P.op("act", (lambda e, si=si: e.activation(ss[:, si:si + 1], ss[:, si:si + 1], AF.Sqrt, bias=epsT[:, 0:1], scale=1.0 / 1024.0)),
                     reads=[("ss", si), "eps"], writes=[("ss", si)])
                P.op("dve", (lambda e, si=si: e.reciprocal(ss[:, si:si + 1], ss[:, si:si + 1])), reads=[("ss", si)], writes=[("ss", si)])
                o = ost[tile % 2]
                P.op("dve", (lambda e, o=o, pz=pz, si=si: e.scalar_tensor_tensor(o, pz, ss[:, si:si + 1], gfin, ALU.mult, ALU.mult)),
                     reads=keys + [("ss", si), "gfin"], writes=[("ost", tile % 2)])
                P.op("sp", (lambda e, o=o, tile=tile: e.dma_start(out=outd[s, tile], in_=o)), reads=[("ost", tile % 2)], dma=f"d_o{tile % 2}")

        for s in range(nseq):
            load_x(s)
            for l in range(depth):
                attnT = phase_attn(l)
                phase_merge(l, attnT)
                phase_ffn(l, s)
                phase_ple(l, s)
            final(s)
        keys = list(P.res.keys())
        P.op("sp", None, reads=keys, writes=keys)
        P.emit()
    return nc


_NC_CACHE = {}


def kernel(x, p, g_mix, w_in, w_ya, w_yb, pool_w, pool_scale, w_o, g_ffn, w_up, conv_w, conv_b,
           w_down, g_ple, w_ple, w_ple_gate, g_final):
    f = lambda a: np.ascontiguousarray(np.asarray(a, dtype=np.float32))
    x, p = f(x), f(p)
    B = x.shape[0]
    nseq = B // NCORES
    wts = build_weight_stream(f(w_in), f(w_ya), f(w_yb), f(pool_w), f(w_o), f(w_up), f(w_down), f(w_ple), f(w_ple_gate))
    cst, rope = build_consts()
    vec = build_vecs(f(g_mix), f(g_ffn), f(g_ple), f(pool_scale), f(conv_w), f(conv_b))
    gfin = np.ascontiguousarray(np.broadcast_to(f(g_final)[None, :], (128, 1024)))
    if nseq not in _NC_CACHE:
        _NC_CACHE[nseq] = build(nseq, DEPTH)
    nc = _NC_CACHE[nseq]
    in_maps = []
    for c in range(NCORES):
        xs = x[c * nseq:(c + 1) * nseq].reshape(nseq, 16, 128, 1024)
        ps = p[:, c * nseq:(c + 1) * nseq].reshape(DEPTH, nseq, 16, 128, 256)
        in_maps.append({"xin": np.ascontiguousarray(xs), "pin": np.ascontiguousarray(ps), "wt": wts, "vec": vec,
                        "cst": cst, "rope": rope, "gfin": gfin})
    res = run_bass_kernel_spmd(nc, in_maps, core_ids=list(range(NCORES)))
    out = np.concatenate([r["out"].reshape(nseq, S, D) for r in res.results], axis=0)
    return out.astype(np.float32)
```

```python
import contextlib
import math
import numpy as np
import concourse.bass as bass
import concourse.mybir as mybir
from concourse.bass_utils import run_bass_kernel_spmd

F32 = mybir.dt.float32
BF16 = mybir.dt.bfloat16
AF = mybir.ActivationFunctionType
ALU = mybir.AluOpType
AX = mybir.AxisListType

NCORES = 8
S = 2048
D = 1024
DEPTH = 2
NSLOT = 4
GROUPS = ((128, 1), (512, 4), (2048, 16))
POOLW = (2, 4, 8, 16)
GORDER = (1, 2, 0)
NV = 8 + 8 + 8 + 8 + 44 * 3 + 44
V_GMIX, V_GFFN, V_GPLE, V_PSC, V_CW, V_CB = 0, 8, 16, 24, 32, 32 + 132
C_ID, C_SEL, C_INVC, C_MASK, C_ODIV, C_ONE = 0, 128, 640, 704, 960, 1088
NCST = 1216


class _Op:
    __slots__ = ("eng", "fn", "stream", "sidx", "waits", "clock_after", "flagged", "is_dma")


class Prog:
    ENGS = ("pe", "act", "dve", "pool", "sp")

    def __init__(self, nc):
        self.nc = nc
        self.ops = {e: [] for e in self.ENGS}
        self.eng_clock = {e: {} for e in self.ENGS}
        self.stream_ops = {e: [] for e in self.ENGS}
        self.res = {}
        self.epoch = None

    def op(self, eng, fn, reads=(), writes=(), dma=None, loose=()):
        o = _Op()
        o.eng = eng
        o.fn = fn
        o.is_dma = dma is not None
        o.stream = dma if dma is not None else eng
        sl = self.stream_ops.setdefault(o.stream, [])
        o.sidx = len(sl) + 1
        o.flagged = o.is_dma
        deps = {}

        def add(tok, kind):
            if tok is None:
                return
            s, i = tok
            if s == eng and not o.is_dma:
                if eng == "pe" or kind.endswith("_L"):
                    return
            if deps.get(s, 0) < i:
                deps[s] = i

        for k in reads:
            r = self.res.get(k)
            if r is not None:
                add(r["w"], "RAW")
        for k in writes:
            r = self.res.get(k)
            if r is not None:
                sfx = "_L" if k in loose else ""
                add(r["w"], "WAW" + sfx)
                for s, i in r["r"].items():
                    add((s, i), "WAR" + sfx)
        if o.is_dma and o.sidx > 1:
            add((o.stream, o.sidx - 1), "RAW")
        if self.epoch is not None:
            add(self.epoch, "RAW")
        clk = self.eng_clock[eng]
        waits = []
        for s, i in deps.items():
            if clk.get(s, 0) >= i:
                continue
            waits.append((s, i))
            dop = self.stream_ops[s][i - 1]
            dop.flagged = True
            for s2, i2 in dop.clock_after.items():
                if clk.get(s2, 0) < i2:
                    clk[s2] = i2
        o.waits = waits
        ca = dict(clk)
        ca[o.stream] = o.sidx
        o.clock_after = ca
        sl.append(o)
        self.ops[eng].append(o)
        tok = (o.stream, o.sidx)
        for k in reads:
            r = self.res.setdefault(k, {"w": None, "r": {}})
            if r["r"].get(o.stream, 0) < o.sidx:
                r["r"][o.stream] = o.sidx
        for k in writes:
            self.res[k] = {"w": tok, "r": {}}
        return o

    def barrier(self, fn):
        keys = list(self.res.keys())
        o = self.op("dve", fn, reads=keys, writes=keys)
        self.res = {}
        self.epoch = (o.stream, o.sidx)

    def emit(self):
        nc = self.nc
        streams = [s for s in self.stream_ops if self.stream_ops[s]]
        val = {}
        for s in streams:
            c = 0
            vs = []
            for o in self.stream_ops[s]:
                if o.is_dma:
                    c += 16
                elif o.flagged:
                    c += 1
                vs.append(c)
            val[s] = vs
        with contextlib.ExitStack() as st:
            sems = {s: st.enter_context(nc.semaphore("s_" + s)) for s in streams}
            block = st.enter_context(nc.Block())
            engobj = {"pe": block.tensor, "act": block.scalar, "dve": block.vector,
                      "pool": block.gpsimd, "sp": block.sync}

            def make(e):
                def body(eng):
                    for o in self.ops[e]:
                        for (s, i) in o.waits:
                            eng.wait_ge(sems[s], val[s][i - 1])
                        if o.fn is None:
                            continue
                        ins = o.fn(eng)
                        if o.is_dma:
                            ins.then_inc(sems[o.stream], 16)
                        elif o.flagged:
                            ins.then_inc(sems[o.stream], 1)
                return body

            for e in self.ENGS:
                if self.ops[e]:
                    engobj[e](make(e))


def _head_perm(hh):
    rest = list(range(32, 128))
    out = []
    for qd in range(4):
        if qd == hh:
            out += list(range(32))
        else:
            out += rest[:32]
            rest = rest[32:]
    return out


def layer_tiles():
    t = []
    for g in range(3):
        t += [(("R", g), 2048)]
        for hh in range(4):
            t += [(("HQK", g, hh), 2048), (("HV", g, hh), 1024)]
    t.append((("PW",), 2048))
    t += [(("PU", j), 2048) for j in range(4)]
    for j in range(4):
        t += [(("YB", j), 2048), (("GB", j), 2048)]
    t += [(("YA", j), 2048) for j in range(2)]
    t += [(("GA", j), 2048) for j in range(4)]
    t += [(("WO", j), 2048) for j in range(4)]
    t += [(("UP", j), 2048) for j in range(22)]
    t += [(("DN", j), 2048) for j in range(11)]
    t.append((("PLE",), 2048))
    t += [(("PG", j), 2048) for j in range(4)]
    return t


def tile_offsets():
    off = {}
    o = 0
    for name, n in layer_tiles():
        off[name] = (o, n)
        o += n
    return off, o


FFN_PARTS = (range(0, 8), range(8, 16), range(16, 22))
POOL_ORDER = (3, 2, 1, 0)


def consume_order():
    seq = []
    for g in GORDER:
        seq += [("R", g)]
        for hh in range(4):
            seq += [("HQK", g, hh), ("HV", g, hh)]
    for hf in range(2):
        seq += [("PW",)] + [("PU", j) for j in POOL_ORDER]
        for j in range(4):
            seq += [("YB", j), ("GB", j)]
        seq += [("YA", 0), ("GA", 0), ("GA", 1), ("YA", 1), ("GA", 2), ("GA", 3)]
        seq += [("WO", j) for j in range(4)]
    for pairs in FFN_PARTS:
        seq += [("UP", pr) for pr in pairs]
        seq += [("DN", j) for j in range(pairs[0] // 2, (pairs[-1] + 1) // 2)]
    seq += [("PLE",)] + [("PG", j) for j in range(4)]
    return seq


def _k1024(W, cols):
    sub = W[:, cols]
    n = sub.shape[1]
    return sub.reshape(8, 128, n).transpose(1, 0, 2).reshape(128, 8 * n)


def build_weight_stream(w_in, w_ya, w_yb, pool_w, w_o, w_up, w_down, w_ple, w_ple_gate):
    off, tot = tile_offsets()
    out = np.zeros((DEPTH, 128, tot), np.float32)
    ar = np.arange
    for l in range(DEPTH):
        def put(name, arr):
            o, n = off[name]
            assert arr.shape == (128, n), (name, arr.shape, n)
            out[l, :, o:o + n] = arr
        for g in range(3):
            r = []
            for base0 in (0, 1536):
                for hh in range(4):
                    b = base0 + g * 512 + hh * 128
                    r += [b + i for i in range(32)]
            put(("R", g), _k1024(w_in[l], r))
            for hh in range(4):
                pm = _head_perm(hh)
                bq = g * 512 + hh * 128
                put(("HQK", g, hh), _k1024(w_in[l], [bq + m for m in pm] + [1536 + bq + m for m in pm]))
                put(("HV", g, hh), _k1024(w_in[l], [3072 + bq + i for i in range(128)]))
        for j in range(4):
            put(("PU", j), _k1024(w_in[l], list(4608 + j * 256 + ar(256))))
            put(("GA", j), _k1024(w_in[l], list(5632 + j * 256 + ar(256))))
            put(("GB", j), _k1024(w_in[l], list(6656 + j * 256 + ar(256))))
            put(("YB", j), _k1024(w_yb[l], list(j * 256 + ar(256))))
            put(("WO", j), _k1024(w_o[l], list(j * 256 + ar(256))))
            put(("PG", j), _k1024(w_ple_gate[l], list(j * 256 + ar(256))))
        put(("PW",), pool_w[l].reshape(4, 2, 128, 256).transpose(2, 0, 1, 3).reshape(128, 2048))
        for j in range(2):
            put(("YA", j), w_ya[l][:, j * 512:(j + 1) * 512].reshape(4, 128, 512).transpose(1, 0, 2).reshape(128, 2048))
        put(("PLE",), w_ple[l].reshape(2, 128, 1024).transpose(1, 0, 2).reshape(128, 2048))
        for pr in range(22):
            put(("UP", pr), _k1024(w_up[l], list(pr * 128 + ar(128)) + list(2816 + pr * 128 + ar(128))))
        for j in range(11):
            blk = w_down[l][j * 256:(j + 1) * 256]
            put(("DN", j), blk.reshape(2, 128, 1024).transpose(1, 0, 2).reshape(128, 2048))
    return out


def build_consts():
    cst = np.zeros((128, NCST), np.float32)
    cst[:, C_ID:C_ID + 128] = np.eye(128, dtype=np.float32)
    for hh in range(4):
        cst[32 * hh, C_SEL + hh * 128:C_SEL + (hh + 1) * 128] = 1.0
    for g, w in enumerate(POOLW):
        t = np.arange(16)
        cst[:, C_INVC + g * 16:C_INVC + (g + 1) * 16] = (1.0 / np.minimum(t + 1, w)).astype(np.float32)[None, :]
    p = np.arange(128)[:, None]
    j = np.arange(128)[None, :]
    cst[:, C_MASK:C_MASK + 128] = (p <= j)
    cst[:, C_MASK + 128:C_MASK + 256] = (p >= j)
    cst[:, C_ODIV:C_ODIV + 128] = 1.0 / 1024.0
    cst[:, C_ONE:C_ONE + 128] = 1.0
    pos = np.arange(S, dtype=np.float32)
    inv_freq = np.exp(np.arange(0, 32, 2, dtype=np.float32) * np.float32(-math.log(500000.0) / 32)).astype(np.float32)
    ang = (pos[:, None] * inv_freq[None, :]).astype(np.float32)
    cos, sin = np.cos(ang).T.astype(np.float32), np.sin(ang).T.astype(np.float32)
    c32 = np.concatenate([cos, cos], 0)
    s32 = np.concatenate([sin, -sin], 0)
    rope = np.zeros((128, 2 * S), np.float32)
    rope[:, :S] = np.tile(c32, (4, 1))
    rope[:, S:] = np.tile(s32, (4, 1))
    return cst, rope


def build_vecs(g_mix, g_ffn, g_ple, pool_scale, conv_w, conv_b):
    v = np.zeros((128, DEPTH * NV), np.float32)
    fm = lambda a: a.reshape(-1, 128).T
    for l in range(DEPTH):
        b = l * NV
        v[:, b + V_GMIX:b + V_GMIX + 8] = fm(g_mix[l])
        v[:, b + V_GFFN:b + V_GFFN + 8] = fm(g_ffn[l])
        v[:, b + V_GPLE:b + V_GPLE + 8] = fm(g_ple[l])
        v[:, b + V_PSC:b + V_PSC + 8] = fm(pool_scale[l])
        cw = conv_w[l].reshape(3, 44, 128).transpose(2, 1, 0).reshape(128, 132)
        v[:, b + V_CW:b + V_CW + 132] = cw
        v[:, b + V_CB:b + V_CB + 44] = fm(conv_b[l])
    return v


A_RSTD, A_ROPE, A_ACCD, A_ACCN, A_X, A_Y = 0, 8192, 16384, 24576, 57344, 77824
ARENA = 88064


def build(nseq=4, depth=DEPTH):
    nc = bass.Bass("TRN2", target_bir_lowering=False)
    toff, TOT = tile_offsets()
    xin = nc.dram_tensor("xin", [nseq, 16, 128, 1024], F32, kind="ExternalInput").ap()
    pin = nc.dram_tensor("pin", [DEPTH, nseq, 16, 128, 256], F32, kind="ExternalInput").ap()
    wt = nc.dram_tensor("wt", [DEPTH, 128, TOT], F32, kind="ExternalInput").ap()
    vecd = nc.dram_tensor("vec", [128, DEPTH * NV], F32, kind="ExternalInput").ap()
    cstd = nc.dram_tensor("cst", [128, NCST], F32, kind="ExternalInput").ap()
    roped = nc.dram_tensor("rope", [128, 2 * S], F32, kind="ExternalInput").ap()
    gfind = nc.dram_tensor("gfin", [128, 1024], F32, kind="ExternalInput").ap()
    outd = nc.dram_tensor("out", [nseq, 16, 128, 1024], F32, kind="ExternalOutput").ap()
    wsc = nc.dram_tensor("wsc", [DEPTH, 128, TOT], BF16).ap()

    with contextlib.ExitStack() as st:
        sb = lambda name, shape, dt: st.enter_context(nc.sbuf_tensor(name, shape, dt))
        xT = sb("xT", [128, 8, S], F32)
        hT = sb("hT", [128, 8, S], BF16)
        ring = [sb(f"ring{i}", [128, 2048], BF16) for i in range(NSLOT)]
        vecs = sb("vecs", [128, DEPTH * NV], F32)
        cstf = sb("cstf", [128, NCST], F32)
        cstb = sb("cstb", [128, 640], BF16)
        epsT = sb("epsT", [128, 2], F32)
        arena = sb("arena", [128, ARENA // 2], BF16)
        psT = [st.enter_context(nc.psum_tensor(f"psT{i}", [128, 2048], F32)) for i in range(2)]

        def ab(off, n):
            return arena[:, off // 2: off // 2 + n]

        def af(off, n):
            return arena[:, off // 2: off // 2 + 2 * n].bitcast(F32)

        ident32 = cstf[:, C_ID:C_ID + 128]
        sel = cstf[:, C_SEL:C_SEL + 512].rearrange("p (h m) -> p h m", h=4)
        invc = cstf[:, C_INVC:C_INVC + 64].rearrange("p (g t) -> p g t", g=4)
        identb = cstb[:, 0:128]
        mask2 = cstb[:, 128:384]
        odivb = cstb[:, 384:512]
        onesb = cstb[:, 512:640]

        P = Prog(nc)
        bank_ctr = [0]

        def nb():
            b = bank_ctr[0] % 8
            bank_ctr[0] += 1
            return b

        def bank(b):
            return psT[b // 4][:, (b % 4) * 512:(b % 4 + 1) * 512]

        def bk(b):
            return ("bk", b)

        def barrier():
            P.barrier(lambda e: e.memset(epsT[:, 1:2], 0.0))

        P.op("sp", lambda e: e.dma_start(out=vecs[:], in_=vecd), writes=["vecs"], dma="d_c0")
        P.op("sp", lambda e: e.dma_start(out=cstf[:], in_=cstd), writes=["cstf"], dma="d_c1")
        P.op("dve", lambda e: e.memset(epsT[:, 0:1], 1e-6), writes=["eps"])
        P.op("dve", lambda e: e.tensor_copy(cstb[:, 0:128], cstf[:, C_ID:C_ID + 128]), reads=["cstf"], writes=["cstb"])
        P.op("dve", lambda e: e.tensor_copy(cstb[:, 128:640], cstf[:, C_MASK:C_MASK + 512]), reads=["cstf"], writes=["cstb"])

        corder = consume_order()
        gseq = [(l, nm) for _ in range(nseq) for l in range(depth) for nm in corder]
        wst = {"next": 0, "free": list(range(NSLOT)), "slot": {}, "pos": 0, "saved": set()}

        def w_issue():
            while wst["free"] and wst["next"] < len(gseq):
                idx = wst["next"]
                l, nm = gseq[idx]
                slot = wst["free"].pop(0)
                o, n = toff[nm]
                if (l, nm) in wst["saved"]:
                    P.op("sp", (lambda e, slot=slot, l=l, o=o, n=n: e.dma_start(out=ring[slot][:, 0:n], in_=wsc[l, :, o:o + n])),
                         reads=[("wsc", l, nm)], writes=[("w", slot)], dma=f"d_w{slot}")
                else:
                    P.op("pool", (lambda e, slot=slot, l=l, o=o, n=n: e.dma_start(out=ring[slot][:, 0:n], in_=wt[l, :, o:o + n])),
                         writes=[("w", slot)], dma=f"d_w{slot}")
                    P.op("sp", (lambda e, slot=slot, l=l, o=o, n=n: e.dma_start(out=wsc[l, :, o:o + n], in_=ring[slot][:, 0:n])),
                         reads=[("w", slot)], writes=[("wsc", l, nm)], dma=f"d_ws{idx % 2}")
                    wst["saved"].add((l, nm))
                wst["slot"][idx] = slot
                wst["next"] += 1

        def w_get(l, nm):
            idx = wst["pos"]
            assert gseq[idx] == (l, nm), (gseq[idx], l, nm)
            w_issue()
            assert idx in wst["slot"], ("weight ring too small at", nm)
            wst["pos"] += 1
            slot = wst["slot"][idx]
            return slot

        def w_free(slot):
            wst["free"].append(slot)
            w_issue()

        def wv(slot, k, c):
            return ring[slot][:, 0:k * c].rearrange("p (k c) -> p k c", k=k)

        def wk(slot):
            return ("w", slot)

        def proj(slot, view, chunk, tt, extra_reads=()):
            b = nb()
            for k in range(8):
                P.op("pe", (lambda e, b=b, k=k: e.matmul(bank(b), view[:, k, chunk * 128:(chunk + 1) * 128],
                                                         hT[:, k, tt * 512:(tt + 1) * 512], start=(k == 0), stop=(k == 7))),
                     reads=[wk(slot), ("hT", tt)], writes=[bk(b)])
            return b

        rstd = af(A_RSTD, 2048)

        sqh = [ab(A_Y, 2048).rearrange("p (k t) -> p k t", k=8), ab(A_Y + 4096, 2048).rearrange("p (k t) -> p k t", k=8)]
        RSTD_KEYS = [("rstd", j) for j in range(8)]

        def norm_stats(js=range(8), xkey=None):
            for j in js:
                sqb = sqh[j % 2]
                xs = xT[:, :, j * 256:(j + 1) * 256]
                xk = ["xT"] if xkey is None else [xkey]
                if j % 2 == 0:
                    P.op("act", (lambda e, sqb=sqb, xs=xs: e.activation(sqb, xs, AF.Square)), reads=xk, writes=[("sq", j % 2)])
                else:
                    P.op("dve", (lambda e, sqb=sqb, xs=xs: e.tensor_tensor(sqb, xs, xs, ALU.mult)), reads=xk, writes=[("sq", j % 2)])
                b = nb()
                for k in range(8):
                    P.op("pe", (lambda e, b=b, k=k, sqb=sqb: e.matmul(bank(b)[:, 0:256], odivb, sqb[:, k, :], start=(k == 0), stop=(k == 7))),
                         reads=[("sq", j % 2), "cstb"], writes=[bk(b)])
                rs_ = rstd[:, j * 256:(j + 1) * 256]
                P.op("act", (lambda e, b=b, rs_=rs_: e.activation(rs_, bank(b)[:, 0:256], AF.Ln, bias=epsT[:, 0:1], scale=1.0)),
                     reads=[bk(b), "eps"], writes=[("rstd", j)])
                P.op("act", (lambda e, rs_=rs_: e.activation(rs_, rs_, AF.Exp, scale=-0.5)), reads=[("rstd", j)], writes=[("rstd", j)])

        def pview(ap2, d):
            return ap2 if d == 1 else ap2.rearrange("p (j r) -> p r j", r=d)

        def cview(ap2, d):
            return ap2 if d == 1 else ap2.rearrange("p (r j) -> p r j", r=d)

        ALL_UNITS = [(tt, k) for tt in range(4) for k in range(8)]

        def normalize(gv, d, units=None):
            for tt, k in (ALL_UNITS if units is None else units):
                rk = RSTD_KEYS if d != 1 else [("rstd", 2 * tt), ("rstd", 2 * tt + 1)]
                P.op("dve", (lambda e, k=k, tt=tt: e.scalar_tensor_tensor(span_c(hT[:, k, tt * 512:(tt + 1) * 512], d), span_p(xT[:, k, :], d, tt),
                                                                         gv[:, k:k + 1], span_p(rstd, d, tt), ALU.mult, ALU.mult)),
                     reads=["xT", "vecs"] + rk, writes=[("hT", tt)], loose=[("hT", tt)])

        def span_p(ap2, d, m):
            if d == 1:
                return ap2[:, m * 512:(m + 1) * 512]
            if d == 4:
                return ap2.rearrange("p (j r) -> p r j", r=4)[:, m, :]
            return ap2.rearrange("p (j r) -> p r j", r=16)[:, 4 * m:4 * m + 4, :]

        def span_c(ap2, d):
            return ap2.rearrange("p (a b) -> p a b", a=4) if d == 16 else ap2

        def phase_attn(l):
            gv = vecs[:, l * NV + V_GMIX: l * NV + V_GMIX + 8]
            ropeC = ab(A_ROPE, 2048)
            ropeS = ab(A_ROPE + 4096, 2048)
            accD = af(A_ACCD, 2048)
            accN = af(A_ACCN, 8192).rearrange("p (h t) -> p h t", h=4)
            qk = [ab(A_X, 2048), ab(A_X + 4096, 2048)]
            Vh = ab(A_X + 8192, 2048).rearrange("p (b d) -> p b d", b=16)
            qkrot = [ab(A_X + 12288, 2048), ab(A_X + 16384, 2048)]
            attnT = ab(A_X, 8192).rearrange("p (h t) -> p h t", h=4)
            NPT = 12
            PT = [ab(A_Y + i * 512, 256) for i in range(NPT)]
            t1 = af(A_Y + 6144, 512)
            t2 = af(A_Y + 8192, 512)
            scale = 1.0 / math.sqrt(128.0)
            t3 = af(A_X + 8192, 512)
            SWAP16 = list(range(16, 32)) + list(range(0, 16))

            barrier()
            P.op("pool", lambda e: e.dma_start(out=ab(A_ROPE, 4096), in_=roped), writes=["rope"], dma="d_rope")
            norm_stats()
            barrier()
            pre_norm = [False]
            for gi, g in enumerate(GORDER):
                d = GROUPS[g][1]
                nbk = (S // d) // 128
                if not pre_norm[0]:
                    normalize(gv, d)
                pre_norm[0] = False
                rs = w_get(l, ("R", g))
                Rv = wv(rs, 8, 256)
                for which in range(2):
                    for tt in range(4):
                        b1 = proj(rs, Rv, which, tt)
                        P.op("dve", (lambda e, b1=b1, tt=tt, d=d: e.tensor_tensor(span_c(t1, d), span_c(bank(b1), d), span_p(ropeC, d, tt), ALU.mult)),
                             reads=[bk(b1), "rope"], writes=["t1"])
                        P.op("dve", (lambda e, b1=b1, tt=tt, d=d: e.tensor_tensor(span_c(t2, d), span_c(bank(b1), d), span_p(ropeS, d, tt), ALU.mult)),
                             reads=[bk(b1), "rope"], writes=["t2"])
                        P.op("dve", (lambda e: e.stream_shuffle(t3, t2, SWAP16)), reads=["t2"], writes=["Vh"])
                        P.op("dve", (lambda e, which=which, tt=tt: e.tensor_tensor(qkrot[which][:, tt * 512:(tt + 1) * 512], t1, t3, ALU.add)),
                             reads=["t1", "Vh"], writes=[("qkrot", which)], loose=[("qkrot", which)])
                w_free(rs)
                for hh in range(4):
                    hs = w_get(l, ("HQK", g, hh))
                    Hv = wv(hs, 8, 256)
                    for which in range(2):
                        for tt in range(4):
                            b = proj(hs, Hv, which, tt)
                            P.op("act", (lambda e, b=b, which=which, tt=tt: e.copy(qk[which][:, tt * 512:(tt + 1) * 512], bank(b))),
                                 reads=[bk(b)], writes=[("qk", which)], loose=[("qk", which)])
                        P.op("dve", (lambda e, which=which, hh=hh: e.tensor_copy(qk[which][32 * hh:32 * hh + 32, :],
                                                                                 qkrot[which][32 * hh:32 * hh + 32, :])),
                             reads=[("qkrot", which)], writes=[("qk", which)])
                    w_free(hs)
                    vs = w_get(l, ("HV", g, hh))
                    Vv = wv(vs, 8, 128)
                    for blk in range(16):
                        b = nb()
                        for k in range(8):
                            P.op("pe", (lambda e, b=b, k=k, blk=blk, Vv=Vv: e.matmul(bank(b)[:, 0:128], hT[:, k, blk * 128:(blk + 1) * 128],
                                                                              Vv[:, k, :], start=(k == 0), stop=(k == 7))),
                                 reads=[wk(vs), ("hT", blk // 4)], writes=[bk(b)])
                        P.op("act", (lambda e, b=b, blk=blk: e.copy(Vh[:, blk, :], bank(b)[:, 0:128])),
                             reads=[bk(b)], writes=["Vh"], loose=["Vh"])
                    w_free(vs)
                    nxt_d = GROUPS[GORDER[gi + 1]][1] if (hh == 3 and gi < 2) else None
                    def S_(m, nbk=nbk):
                        for B in range(4 * m, 4 * m + 4):
                            ncol = 256 if (B % nbk) < nbk - 1 else 128
                            sbk = nb()
                            P.op("pe", (lambda e, sbk=sbk, B=B, ncol=ncol: e.matmul(bank(sbk)[:, 0:ncol], qk[1][:, B * 128:(B + 1) * 128],
                                                                                    qk[0][:, B * 128:B * 128 + ncol], start=True, stop=True)),
                                 reads=[("qk", 0), ("qk", 1)], writes=[bk(sbk)])
                            P.op("act", (lambda e, sbk=sbk, B=B, ncol=ncol: e.activation(PT[B % NPT][:, 0:ncol], bank(sbk)[:, 0:ncol], AF.Exp, scale=scale)),
                                 reads=[bk(sbk)], writes=[("PT", B % NPT)])
                            P.op("dve", (lambda e, B=B, ncol=ncol: e.tensor_tensor(PT[B % NPT][:, 0:ncol], PT[B % NPT][:, 0:ncol], mask2[:, 0:ncol], ALU.mult)),
                                 reads=[("PT", B % NPT), "cstb"], writes=[("PT", B % NPT)])

                    def V_(m, nbk=nbk, gi=gi, d=d, hh=hh):
                        nbn, nbd = nb(), nb()
                        for Bq in range(4 * m, 4 * m + 4):
                            srcs = []
                            if Bq % nbk > 0:
                                srcs.append((Bq - 1, 128))
                            srcs.append((Bq, 0))
                            for tgt, isN in ((nbn, True), (nbd, False)):
                                for i, (Bk, co) in enumerate(srcs):
                                    lhs = Vh[:, Bk, :] if isN else onesb
                                    P.op("pe", (lambda e, tgt=tgt, lhs=lhs, Bk=Bk, co=co, Bq=Bq, i=i, n=len(srcs): e.matmul(
                                        bank(tgt)[:, (Bq % 4) * 128:(Bq % 4 + 1) * 128], lhs, PT[Bk % NPT][:, co:co + 128],
                                        start=(i == 0), stop=(i == n - 1))),
                                         reads=[("PT", Bk % NPT), "Vh", "cstb"], writes=[bk(tgt)])
                        dN = span_p(accN[:, hh, :], d, m)
                        dD = span_p(accD[32 * hh:32 * hh + 32, :], d, m)
                        sN = span_c(bank(nbn), d)
                        sD = span_c(bank(nbd)[32 * hh:32 * hh + 32, :], d)
                        if gi == 0:
                            P.op("act", (lambda e, dN=dN, sN=sN: e.copy(dN, sN)), reads=[bk(nbn)], writes=["accN"], loose=["accN"])
                            P.op("act", (lambda e, dD=dD, sD=sD: e.copy(dD, sD)), reads=[bk(nbd)], writes=["accD"], loose=["accD"])
                        else:
                            P.op("dve", (lambda e, dN=dN, sN=sN: e.tensor_tensor(dN, sN, dN, ALU.add)), reads=[bk(nbn), "accN"], writes=["accN"], loose=["accN"])
                            P.op("dve", (lambda e, dD=dD, sD=sD: e.tensor_tensor(dD, sD, dD, ALU.add)), reads=[bk(nbd), "accD"], writes=["accD"], loose=["accD"])

                    kq = list(ALL_UNITS)

                    def nrm(n):
                        if nxt_d is not None:
                            for _ in range(4 * n):
                                if kq:
                                    normalize(gv, nxt_d, units=[kq.pop(0)])

                    S_(0)
                    for m in range(4):
                        if m + 1 < 4:
                            S_(m + 1)
                        nrm(1)
                        V_(m)
                        nrm(1)
                    if nxt_d is not None:
                        pre_norm[0] = True
            barrier()
            for tt in range(4):
                P.op("act", (lambda e, tt=tt: e.activation(accD[:, tt * 512:(tt + 1) * 512], accD[:, tt * 512:(tt + 1) * 512], AF.Ln)),
                     reads=["accD"], writes=[("racc", tt)])
                P.op("act", (lambda e, tt=tt: e.activation(accD[:, tt * 512:(tt + 1) * 512], accD[:, tt * 512:(tt + 1) * 512], AF.Exp, scale=-1.0)),
                     reads=[("racc", tt)], writes=[("racc", tt)])
                for hh in range(4):
                    b = nb()
                    P.op("pe", (lambda e, b=b, hh=hh, tt=tt: e.matmul(bank(b), sel[:, hh, :], accD[:, tt * 512:(tt + 1) * 512], start=True, stop=True)),
                         reads=[("racc", tt), "cstf"], writes=[bk(b)])
                    P.op("dve", (lambda e, b=b, hh=hh, tt=tt: e.tensor_tensor(attnT[:, hh, tt * 512:(tt + 1) * 512], bank(b), accN[:, hh, tt * 512:(tt + 1) * 512], ALU.mult)),
                         reads=[bk(b), "accN"], writes=["attnT"], loose=["attnT"])
            return attnT

        def phase_merge(l, attnT):
            gv = vecs[:, l * NV + V_GMIX: l * NV + V_GMIX + 8]
            psc = vecs[:, l * NV + V_PSC: l * NV + V_PSC + 8]
            mixed = ab(8192, 8192).rearrange("p (k t) -> p k t", k=8)
            merged = ab(24576, 8192).rearrange("p (k t) -> p k t", k=8)
            UW = 528
            ub = [[af(40960 + (st_ * 3 + i) * 2112, UW) for i in range(3)] for st_ in range(2)]
            usetc = [0]
            pooled2 = [ab(73728, 2048).rearrange("p (k t) -> p k t", k=2), ab(A_Y + 6144, 2048).rearrange("p (k t) -> p k t", k=2)]
            sg = [af(A_Y, 512), af(A_Y + 2048, 512)]
            m1 = af(A_Y + 4096, 512)
            barrier()
            sgc = [0]
            for hf in range(2):
                T0 = hf * 1024
                pws = w_get(l, ("PW",))
                PWv = ring[pws][:, 0:2048].rearrange("p (g k c) -> p g k c", g=4, k=2)
                pending = []
                for pi_, pj in enumerate(POOL_ORDER):
                    pus = w_get(l, ("PU", pj))
                    PUv = wv(pus, 8, 256)
                    pooled = pooled2[pi_ % 2]
                    pkey = ("pooled", pi_ % 2)
                    for c4 in range(2):
                        c = pj * 2 + c4
                        grp, c2 = c // 2, c % 2
                        w = POOLW[grp]
                        prevU = None
                        for t2 in range(2):
                            tt = 2 * hf + t2
                            st_ = usetc[0] % 2
                            usetc[0] += 1
                            ubs = ub[st_]
                            uk = lambda i, st_=st_: ("ub", st_, i)
                            U = ubs[0]
                            b = proj(pus, PUv, c4, tt)
                            P.op("act", (lambda e, b=b, U=U: e.copy(U[:, 16:UW], bank(b))), reads=[bk(b)], writes=[uk(0)])
                            if tt == 0:
                                P.op("dve", (lambda e, U=U: e.memset(U[:, 0:16], 0.0)), writes=[uk(0)])
                            elif t2 == 1:
                                P.op("act", (lambda e, U=U, prevU=prevU: e.copy(U[:, 0:16], prevU[:, UW - 16:UW])),
                                     reads=[("ub", 1 - st_, 0)], writes=[uk(0)])
                            else:
                                b = nb()
                                T0 = tt * 512
                                for k in range(8):
                                    P.op("pe", (lambda e, b=b, k=k, c4=c4, PUv=PUv, T0=T0: e.matmul(bank(b)[:, 0:16], PUv[:, k, c4 * 128:(c4 + 1) * 128],
                                                                                                 hT[:, k, T0 - 16:T0], start=(k == 0), stop=(k == 7))),
                                         reads=[wk(pus), ("hT", (T0 - 16) // 512)], writes=[bk(b)])
                                P.op("act", (lambda e, b=b, U=U: e.copy(U[:, 0:16], bank(b)[:, 0:16])), reads=[bk(b)], writes=[uk(0)])
                            cur, ci = U, 0
                            step = 1
                            while step < w:
                                ni = 1 if ci != 1 else 2
                                nxt = ubs[ni]
                                P.op("dve", (lambda e, cur=cur, nxt=nxt, step=step: e.tensor_tensor(nxt[:, step:UW], cur[:, step:UW], cur[:, 0:UW - step], ALU.add)),
                                     reads=[uk(ci)], writes=[uk(ni)])
                                cur, ci = nxt, ni
                                step *= 2
                            P.op("dve", (lambda e, cur=cur, U=U, c2=c2, w=w, t2=t2, pooled=pooled: e.scalar_tensor_tensor(pooled[:, c2, t2 * 512:(t2 + 1) * 512], cur[:, 16:UW], 1.0 / w,
                                                                                                            U[:, 16:UW], ALU.mult, ALU.subtract)),
                                 reads=[uk(ci), uk(0)], writes=[pkey])
                            if tt == 0:
                                P.op("dve", (lambda e, cur=cur, grp=grp: e.tensor_tensor(m1[:, 0:16], cur[:, 16:32], invc[:, grp, :], ALU.mult)),
                                     reads=[uk(ci), "cstf"], writes=["m1"])
                                P.op("dve", (lambda e, U=U, c2=c2, pooled=pooled: e.tensor_tensor(pooled[:, c2, 0:16], m1[:, 0:16], U[:, 16:32], ALU.subtract)),
                                     reads=["m1", uk(0)], writes=[pkey])
                            prevU = U
                        if c2 == 1:
                            def pw_stage(grp=grp, pooled=pooled, pkey=pkey):
                                for oc in range(2):
                                    for t2 in range(2):
                                        b = nb()
                                        for k2 in range(2):
                                            P.op("pe", (lambda e, b=b, k2=k2, oc=oc, t2=t2, grp=grp, PWv=PWv, pooled=pooled: e.matmul(
                                                bank(b), PWv[:, grp, k2, oc * 128:(oc + 1) * 128], pooled[:, k2, t2 * 512:(t2 + 1) * 512],
                                                start=(k2 == 0), stop=(k2 == 1))),
                                                 reads=[wk(pws), pkey], writes=[bk(b)])
                                        cc = grp * 2 + oc
                                        P.op("act", (lambda e, b=b, cc=cc, t2=t2: e.activation(mixed[:, cc, t2 * 512:(t2 + 1) * 512], bank(b), AF.Identity,
                                                                                              scale=psc[:, cc:cc + 1])),
                                             reads=[bk(b), "vecs"], writes=["mixed"], loose=["mixed"])
                            for f_ in pending:
                                f_()
                            pending = [pw_stage]
                    w_free(pus)
                for f_ in pending:
                    f_()
                w_free(pws)
                for j in range(4):
                    ys = w_get(l, ("YB", j))
                    gs = w_get(l, ("GB", j))
                    Yv_, Gv_ = wv(ys, 8, 256), wv(gs, 8, 256)
                    for c4 in range(2):
                        c = 2 * j + c4
                        for t2 in range(2):
                            b1 = nb()
                            for k in range(8):
                                P.op("pe", (lambda e, b1=b1, k=k, c4=c4, t2=t2, Yv_=Yv_: e.matmul(bank(b1), Yv_[:, k, c4 * 128:(c4 + 1) * 128],
                                                                                        mixed[:, k, t2 * 512:(t2 + 1) * 512], start=(k == 0), stop=(k == 7))),
                                     reads=[wk(ys), "mixed"], writes=[bk(b1)])
                            b2 = proj(gs, Gv_, c4, 2 * hf + t2)
                            sgi = sgc[0] % 2
                            sgc[0] += 1
                            P.op("act", (lambda e, b2=b2, sgi=sgi: e.activation(sg[sgi], bank(b2), AF.Sigmoid)), reads=[bk(b2)], writes=[("sg", sgi)])
                            P.op("dve", (lambda e, b1=b1, sgi=sgi, c=c, t2=t2: e.tensor_tensor(merged[:, c, t2 * 512:(t2 + 1) * 512], bank(b1), sg[sgi], ALU.mult)),
                                 reads=[bk(b1), ("sg", sgi)], writes=["merged"], loose=["merged"])
                    w_free(ys)
                    w_free(gs)
                for jj in range(2):
                    yas = w_get(l, ("YA", jj))
                    YAv = wv(yas, 4, 512)
                    for j2 in range(2):
                        j = 2 * jj + j2
                        gs = w_get(l, ("GA", j))
                        Gv_ = wv(gs, 8, 256)
                        for c4 in range(2):
                            c = 2 * j + c4
                            cl = c - 4 * jj
                            for t2 in range(2):
                                tt = 2 * hf + t2
                                b1 = nb()
                                for k in range(4):
                                    P.op("pe", (lambda e, b1=b1, k=k, cl=cl, tt=tt, YAv=YAv: e.matmul(bank(b1), YAv[:, k, cl * 128:(cl + 1) * 128],
                                                                                                    attnT[:, k, tt * 512:(tt + 1) * 512], start=(k == 0), stop=(k == 3))),
                                         reads=[wk(yas), "attnT"], writes=[bk(b1)])
                                b2 = proj(gs, Gv_, c4, tt)
                                sgi = sgc[0] % 2
                                sgc[0] += 1
                                P.op("act", (lambda e, b2=b2, sgi=sgi: e.activation(sg[sgi], bank(b2), AF.Sigmoid)), reads=[bk(b2)], writes=[("sg", sgi)])
                                P.op("dve", (lambda e, b1=b1, sgi=sgi: e.tensor_tensor(m1, bank(b1), sg[sgi], ALU.mult)),
                                     reads=[bk(b1), ("sg", sgi)], writes=["m1"])
                                P.op("dve", (lambda e, c=c, t2=t2: e.tensor_tensor(merged[:, c, t2 * 512:(t2 + 1) * 512], m1, merged[:, c, t2 * 512:(t2 + 1) * 512], ALU.add)),
                                     reads=["m1", "merged"], writes=["merged"], loose=["merged"])
                        w_free(gs)
                    w_free(yas)
                for j in range(4):
                    ws_ = w_get(l, ("WO", j))
                    Wv_ = wv(ws_, 8, 256)
                    for c4 in range(2):
                        c = 2 * j + c4
                        for t2 in range(2):
                            tt = 2 * hf + t2
                            b = nb()
                            for k in range(8):
                                P.op("pe", (lambda e, b=b, k=k, c4=c4, t2=t2, Wv_=Wv_: e.matmul(bank(b), Wv_[:, k, c4 * 128:(c4 + 1) * 128],
                                                                                      merged[:, k, t2 * 512:(t2 + 1) * 512], start=(k == 0), stop=(k == 7))),
                                     reads=[wk(ws_), "merged"], writes=[bk(b)])
                            P.op("dve", (lambda e, b=b, c=c, tt=tt: e.tensor_tensor(xT[:, c, tt * 512:(tt + 1) * 512], bank(b), xT[:, c, tt * 512:(tt + 1) * 512], ALU.add)),
                                 reads=[bk(b), "xT"], writes=["xT"], loose=["xT"])
                    w_free(ws_)

        def phase_ffn(l, s):
            gv = vecs[:, l * NV + V_GFFN: l * NV + V_GFFN + 8]
            cw = vecs[:, l * NV + V_CW: l * NV + V_CW + 132].rearrange("p (c t) -> p c t", t=3)
            cb = vecs[:, l * NV + V_CB: l * NV + V_CB + 44]
            actT = ab(8192, 16384).rearrange("p (k t) -> p k t", k=8)
            Yb = [[af(40960, 2048), af(49152, 2048)], [af(57344, 2048), af(65536, 2048)]]
            barrier()
            norm_stats()
            normalize(gv, 1)
            gctr = [0]
            for pi, pairs in enumerate(FFN_PARTS):
                for pr in pairs:
                    ups = w_get(l, ("UP", pr))
                    UPv = wv(ups, 8, 256)
                    Ys = Yb[pr % 2]
                    for role in range(2):
                        chunk = role
                        cidx = pr + 22 * role
                        G = gctr[0] % 2
                        gctr[0] += 1
                        for tt in range(4):
                            for k in range(8):
                                P.op("pe", (lambda e, G=G, k=k, tt=tt, chunk=chunk, UPv=UPv: e.matmul(psT[G][:, tt * 512:(tt + 1) * 512],
                                                                                                    UPv[:, k, chunk * 128:(chunk + 1) * 128],
                                                                                                    hT[:, k, tt * 512:(tt + 1) * 512], start=(k == 0), stop=(k == 7))),
                                     reads=[wk(ups), ("hT", tt)], writes=[bk(4 * G + tt)])
                        Y = Ys[role]
                        gb = [bk(4 * G + i) for i in range(4)]
                        yk = ("Y", pr % 2, role)
                        P.op("act", (lambda e, Y=Y, G=G, cidx=cidx: e.activation(Y, psT[G][:, :], AF.Identity, bias=cb[:, cidx:cidx + 1],
                                                                                  scale=cw[:, cidx, 2:3])),
                             reads=gb + ["vecs"], writes=[yk])
                        P.op("dve", (lambda e, Y=Y, G=G, cidx=cidx: e.scalar_tensor_tensor(Y[:, 1:S], psT[G][:, 0:S - 1], cw[:, cidx, 1:2], Y[:, 1:S],
                                                                                            ALU.mult, ALU.add)),
                             reads=gb + ["vecs", yk], writes=[yk])
                        P.op("dve", (lambda e, Y=Y, G=G, cidx=cidx: e.scalar_tensor_tensor(Y[:, 2:S], psT[G][:, 0:S - 2], cw[:, cidx, 0:1], Y[:, 2:S],
                                                                                            ALU.mult, ALU.add)),
                             reads=gb + ["vecs", yk], writes=[yk])
                    P.op("act", (lambda e, Ys=Ys: e.activation(Ys[0], Ys[0], AF.Silu)), reads=[("Y", pr % 2, 0)], writes=[("Y", pr % 2, 0)])
                    P.op("dve", (lambda e, Ys=Ys, pr=pr, p0=pairs[0]: e.tensor_tensor(actT[:, pr - p0, :], Ys[0], Ys[1], ALU.mult)),
                         reads=[("Y", pr % 2, 0), ("Y", pr % 2, 1)], writes=["actT"], loose=["actT"])
                    w_free(ups)
                dns = [w_get(l, ("DN", j)) for j in range(pairs[0] // 2, (pairs[-1] + 1) // 2)]
                dv = [wv(dn_, 2, 1024) for dn_ in dns]
                dkeys = [wk(dn_) for dn_ in dns]
                nk = len(pairs)
                last = (pi == len(FFN_PARTS) - 1)
                if last:
                    P.op("pool", lambda e: e.dma_start(out=ab(57344, 4096).rearrange("p (t f) -> p t f", t=16), in_=pin[l, s].rearrange("t p f -> p t f")),
                         writes=["ptok", ("Y", 1, 0), ("Y", 1, 1)], dma="d_p")
                gpl = vecs[:, l * NV + V_GPLE: l * NV + V_GPLE + 8]

                def pre_ple(tt):
                    norm_stats(js=(2 * tt, 2 * tt + 1), xkey=("xTt", tt))
                    normalize(gpl, 1, units=[(tt, k) for k in range(8)])

                order = [(c, tt) for tt in range(4) for c in range(8)] if last else [(c, tt) for c in range(8) for tt in range(4)]
                for c, tt in order:
                    if last and c == 0 and tt >= 2:
                        pre_ple(tt - 2)
                    b = nb()
                    for kk in range(nk):
                        P.op("pe", (lambda e, b=b, kk=kk, c=c, tt=tt, nk=nk, dv=dv: e.matmul(bank(b), dv[kk // 2][:, kk % 2, c * 128:(c + 1) * 128],
                                                                                      actT[:, kk, tt * 512:(tt + 1) * 512], start=(kk == 0), stop=(kk == nk - 1))),
                             reads=dkeys + ["actT"], writes=[bk(b)])
                    P.op("dve", (lambda e, b=b, c=c, tt=tt: e.tensor_tensor(xT[:, c, tt * 512:(tt + 1) * 512], bank(b), xT[:, c, tt * 512:(tt + 1) * 512], ALU.add)),
                         reads=[bk(b), "xT"], writes=["xT", ("xTt", tt)], loose=["xT", ("xTt", tt)])
                if last:
                    pre_ple(2)
                    pre_ple(3)
                for dn_ in dns:
                    w_free(dn_)

        def phase_ple(l, s):
            gv = vecs[:, l * NV + V_GPLE: l * NV + V_GPLE + 8]
            ptok = ab(57344, 4096).rearrange("p (t f) -> p t f", t=16)
            pT = ab(16384, 4096).rearrange("p (k t) -> p k t", k=2)
            sg = [af(24576, 512), af(26624, 512)]
            m1 = [af(28672, 512), af(30720, 512)]
            barrier()
            for k2 in range(2):
                for g4 in range(4):
                    b = nb()
                    bb = bank(b).bitcast(BF16)
                    for i in range(4):
                        tile = g4 * 4 + i
                        P.op("pe", (lambda e, bb=bb, i=i, tile=tile, k2=k2: e.transpose(bb[:, i * 128:(i + 1) * 128], ptok[:, tile, k2 * 128:(k2 + 1) * 128], identb)),
                             reads=["ptok", "cstb"], writes=[bk(b)])
                    P.op("act", (lambda e, bb=bb, k2=k2, g4=g4: e.copy(pT[:, k2, g4 * 512:(g4 + 1) * 512], bb[:, 0:512])), reads=[bk(b)], writes=["pT"], loose=["pT"])
            ps_ = w_get(l, ("PLE",))
            PLv = wv(ps_, 2, 1024)
            ctr = 0
            for j in range(4):
                gs = w_get(l, ("PG", j))
                Gv_ = wv(gs, 8, 256)
                for c4 in range(2):
                    c = 2 * j + c4
                    for tt in range(4):
                        b1 = nb()
                        for k2 in range(2):
                            P.op("pe", (lambda e, b1=b1, k2=k2, c=c, tt=tt: e.matmul(bank(b1), PLv[:, k2, c * 128:(c + 1) * 128],
                                                                                    pT[:, k2, tt * 512:(tt + 1) * 512], start=(k2 == 0), stop=(k2 == 1))),
                                 reads=[wk(ps_), "pT"], writes=[bk(b1)])
                        b2 = proj(gs, Gv_, c4, tt)
                        i = ctr % 2
                        ctr += 1
                        P.op("act", (lambda e, b2=b2, i=i: e.activation(sg[i], bank(b2), AF.Sigmoid)), reads=[bk(b2)], writes=[("sg", i)])
                        P.op("dve", (lambda e, b1=b1, i=i: e.tensor_tensor(m1[i], bank(b1), sg[i], ALU.mult)), reads=[bk(b1), ("sg", i)], writes=[("m1", i)])
                        P.op("dve", (lambda e, i=i, c=c, tt=tt: e.tensor_tensor(xT[:, c, tt * 512:(tt + 1) * 512], m1[i], xT[:, c, tt * 512:(tt + 1) * 512], ALU.add)),
                             reads=[("m1", i), "xT"], writes=["xT"], loose=["xT"])
                w_free(gs)
            w_free(ps_)

        def load_x(s):
            stage = [af(8192 + 4096 * i, 1024) for i in range(4)]
            barrier()
            for tile in range(16):
                sgt = stage[tile % 4]
                P.op("sp", (lambda e, sgt=sgt, tile=tile: e.dma_start(out=sgt, in_=xin[s, tile])), writes=[("stg", tile % 4)], dma=f"d_x{tile % 4}")
                for hb in range(2):
                    b = nb()
                    for i in range(4):
                        k = hb * 4 + i
                        P.op("pe", (lambda e, b=b, i=i, k=k, sgt=sgt: e.transpose(bank(b)[:, i * 128:(i + 1) * 128], sgt[:, k * 128:(k + 1) * 128], ident32)),
                             reads=[("stg", tile % 4), "cstf"], writes=[bk(b)])
                    eng = "act" if hb == 0 else "dve"
                    dst = xT[:, hb * 4:hb * 4 + 4, tile * 128:(tile + 1) * 128]
                    src = bank(b).rearrange("p (a t) -> p a t", a=4)
                    if eng == "act":
                        P.op("act", (lambda e, dst=dst, src=src: e.copy(dst, src)), reads=[bk(b)], writes=["xT"], loose=["xT"])
                    else:
                        P.op("dve", (lambda e, dst=dst, src=src: e.tensor_copy(dst, src)), reads=[bk(b)], writes=["xT"], loose=["xT"])

        def final(s):
            gfin = af(8192, 1024)
            ost = [af(12288, 1024), af(16384, 1024)]
            junk = af(20480, 1024)
            ss = af(24576, 8)
            barrier()
            P.op("sp", lambda e: e.dma_start(out=gfin, in_=gfind), writes=["gfin"], dma="d_c0")
            for tile in range(16):
                G = tile % 2
                half = (tile // 2) % 2
                pz = psT[G][:, half * 1024:(half + 1) * 1024]
                keys = [bk(4 * G + 2 * half), bk(4 * G + 2 * half + 1)]
                for k in range(8):
                    P.op("pe", (lambda e, pz=pz, k=k, tile=tile: e.transpose(pz[:, k * 128:(k + 1) * 128], xT[:, k, tile * 128:(tile + 1) * 128], ident32)),
                         reads=["xT", "cstf"], writes=keys)
                si = tile % 4
                P.op("dve", (lambda e, si=si: e.memset(ss[:, si:si + 1], 0.0)), writes=[("ss", si)])
                P.op("act", (lambda e, pz=pz, si=si: e.activation(junk, pz, AF.Square, accum_out=ss[:, si:si + 1])), reads=keys + [("ss", si)],
                     writes=["junk", ("ss", si)])
                P.op("act", (lambda e, si=si: e.activation(ss[:, si:si + 1], ss[:, si:si + 1], AF.Sqrt, bias=epsT[:, 0:1], scale=1.0 / 1024.0)),
                     reads=[("ss", si), "eps"], writes=[("ss", si)])
                P.op("dve", (lambda e, si=si: e.reciprocal(ss[:, si:si + 1], ss[:, si:si + 1])), reads=[("ss", si)], writes=[("ss", si)])
                o = ost[tile % 2]
                P.op("dve", (lambda e, o=o, pz=pz, si=si: e.scalar_tensor_tensor(o, pz, ss[:, si:si + 1], gfin, ALU.mult, ALU.mult)),
                     reads=keys + [("ss", si), "gfin"], writes=[("ost", tile % 2)])
                P.op("sp", (lambda e, o=o, tile=tile: e.dma_start(out=outd[s, tile], in_=o)), reads=[("ost", tile % 2)], dma=f"d_o{tile % 2}")

        for s in range(nseq):
            load_x(s)
            for l in range(depth):
                attnT = phase_attn(l)
                phase_merge(l, attnT)
                phase_ffn(l, s)
                phase_ple(l, s)
            final(s)
        keys = list(P.res.keys())
        P.op("sp", None, reads=keys, writes=keys)
        P.emit()
    return nc


_NC_CACHE = {}


def kernel(x, p, g_mix, w_in, w_ya, w_yb, pool_w, pool_scale, w_o, g_ffn, w_up, conv_w, conv_b,
           w_down, g_ple, w_ple, w_ple_gate, g_final):
    f = lambda a: np.ascontiguousarray(np.asarray(a, dtype=np.float32))
    x, p = f(x), f(p)
    B = x.shape[0]
    nseq = B // NCORES
    wts = build_weight_stream(f(w_in), f(w_ya), f(w_yb), f(pool_w), f(w_o), f(w_up), f(w_down), f(w_ple), f(w_ple_gate))
    cst, rope = build_consts()
    vec = build_vecs(f(g_mix), f(g_ffn), f(g_ple), f(pool_scale), f(conv_w), f(conv_b))
    gfin = np.ascontiguousarray(np.broadcast_to(f(g_final)[None, :], (128, 1024)))
    if nseq not in _NC_CACHE:
        _NC_CACHE[nseq] = build(nseq, DEPTH)
    nc = _NC_CACHE[nseq]
    in_maps = []
    for c in range(NCORES):
        xs = x[c * nseq:(c + 1) * nseq].reshape(nseq, 16, 128, 1024)
        ps = p[:, c * nseq:(c + 1) * nseq].reshape(DEPTH, nseq, 16, 128, 256)
        in_maps.append({"xin": np.ascontiguousarray(xs), "pin": np.ascontiguousarray(ps), "wt": wts, "vec": vec,
                        "cst": cst, "rope": rope, "gfin": gfin})
    res = run_bass_kernel_spmd(nc, in_maps, core_ids=list(range(NCORES)))
    out = np.concatenate([r["out"].reshape(nseq, S, D) for r in res.results], axis=0)
    return out.astype(np.float32)
```
